# Optimizing a Trainium2 kernel written in Bass

```python
import math
import jax, jax.numpy as jnp
from jax import lax
import numpy as np

D_MODEL = 2048
BATCH = 4
SEQ = 2048
DEPTH = 2

HEAD_DIM = 64
N_MIXERS = 4
GROUP_WIDTH = D_MODEL // N_MIXERS
N_FOX_HEADS = GROUP_WIDTH // HEAD_DIM
N_SB_HEADS = GROUP_WIDTH // HEAD_DIM
N_DIFF_HEADS = GROUP_WIDTH // (2 * HEAD_DIM)
N_DIL_HEADS = GROUP_WIDTH // HEAD_DIM
MIX_WIDTH = N_MIXERS * GROUP_WIDTH
D_FF = 4 * D_MODEL
QUERY_BLOCK = 128
DILATED_BRANCHES = ((128, 1), (512, 4), (2048, 16))
ALIBI_MAX_EXP = 8.0
LN_EPS = 1e-5
RMS_EPS = 1e-5
FORGET_BIAS_INIT = 2.0
DEEPNORM_ALPHA = (2 * DEPTH) ** 0.25
DEEPNORM_BETA = (8 * DEPTH) ** -0.25
IN_SPLITS = (GROUP_WIDTH,) * 3 + (N_FOX_HEADS,) + (GROUP_WIDTH,) * 9
IN_WIDTH = 12 * GROUP_WIDTH + N_FOX_HEADS
V_SEGMENTS = (2, 6, 9, 12)

kernel_name = "hybrid_fox_stickbreak_diff_dilated_deepnorm"


def _layer_norm(x, g, b):
    xf = x.astype(jnp.float32)
    mu = jnp.mean(xf, axis=-1, keepdims=True)
    var = jnp.mean(jnp.square(xf - mu), axis=-1, keepdims=True)
    return ((xf - mu) * lax.rsqrt(var + LN_EPS) * g + b).astype(x.dtype)


def _alibi_slopes():
    n = N_DIFF_HEADS + N_DIL_HEADS
    return jnp.exp2(-ALIBI_MAX_EXP * jnp.arange(1, n + 1, dtype=jnp.float32) / n)


def _heads(a, n):
    B, S, _ = a.shape
    return a.reshape(B, S, n, -1).transpose(0, 2, 1, 3)


def _merge_heads(o):
    B, H, S, d = o.shape
    return o.transpose(0, 2, 1, 3).reshape(B, S, H * d)


def _to_query_blocks(a):
    B, H, S = a.shape[:3]
    a = a.reshape(B, H, S // QUERY_BLOCK, QUERY_BLOCK, *a.shape[3:])
    return jnp.moveaxis(a, 2, 0)


def _from_query_blocks(o):
    nb, B, H, qb = o.shape[:4]
    return jnp.moveaxis(o, 0, 2).reshape(B, H, nb * qb, *o.shape[4:])


def _query_positions(S):
    return jnp.arange(S).reshape(S // QUERY_BLOCK, QUERY_BLOCK)


def _forgetting_attention(q, k, v, log_f_cum):
    S = q.shape[2]
    scale = q.shape[-1] ** -0.5
    kpos = jnp.arange(S)

    def block(args):
        qb, cb, tq = args
        s = jnp.einsum('bhqd,bhkd->bhqk', qb, k).astype(jnp.float32) * scale
        s = s + cb[..., None] - log_f_cum[:, :, None, :]
        s = jnp.where(kpos[None, :] <= tq[:, None], s, -jnp.inf)
        p = jax.nn.softmax(s, axis=-1)
        return jnp.einsum('bhqk,bhkd->bhqd', p.astype(v.dtype), v)

    out = lax.map(block, (_to_query_blocks(q), _to_query_blocks(log_f_cum), _query_positions(S)))
    return _from_query_blocks(out)


def _stick_breaking_attention(q, k, v):
    S = q.shape[2]
    scale = q.shape[-1] ** -0.5
    kpos = jnp.arange(S)

    def block(args):
        qb, tq = args
        z = jnp.einsum('bhqd,bhkd->bhqk', qb, k).astype(jnp.float32) * scale
        strict = kpos[None, :] < tq[:, None]
        log_beta = jax.nn.log_sigmoid(z)
        log_one_minus = jnp.where(strict, jax.nn.log_sigmoid(-z), 0.0)
        later = lax.cumsum(log_one_minus, axis=3, reverse=True) - log_one_minus
        a = jnp.where(strict, jnp.exp(log_beta + later), 0.0)
        return jnp.einsum('bhqk,bhkd->bhqd', a.astype(v.dtype), v)

    out = lax.map(block, (_to_query_blocks(q), _query_positions(S)))
    return _from_query_blocks(out)


def _differential_attention(q1, q2, k1, k2, v, lam, slopes):
    S = q1.shape[2]
    scale = q1.shape[-1] ** -0.5
    kpos = jnp.arange(S)

    def block(args):
        q1b, q2b, tq = args
        dist = tq[:, None] - kpos[None, :]
        causal = dist >= 0
        bias = -slopes[:, None, None] * dist.astype(jnp.float32)

        def probs(qb, kk):
            s = jnp.einsum('bhqd,bhkd->bhqk', qb, kk).astype(jnp.float32) * scale + bias
            return jax.nn.softmax(jnp.where(causal, s, -jnp.inf), axis=-1)

        a = probs(q1b, k1) - lam * probs(q2b, k2)
        return jnp.einsum('bhqk,bhkd->bhqd', a.astype(v.dtype), v)

    out = lax.map(block, (_to_query_blocks(q1), _to_query_blocks(q2), _query_positions(S)))
    return _from_query_blocks(out)


def _dilated_branch(q, k, v, slopes, window, dilation):
    B, S, H, d = q.shape
    n = window // dilation
    unit = n * dilation
    Lp = -(-S // unit) * unit
    nb = Lp // unit

    def to_blocks(a):
        a = jnp.pad(a, ((0, 0), (0, Lp - S), (0, 0), (0, 0)))
        a = a.reshape(B, Lp // dilation, dilation, H, d).transpose(0, 2, 3, 1, 4)
        return a.reshape(B, dilation, H, nb, n, d)

    def with_prev(a):
        prev = jnp.concatenate([jnp.zeros_like(a[:, :, :, :1]), a[:, :, :, :-1]], axis=3)
        return jnp.concatenate([prev, a], axis=4)

    qb = to_blocks(q)
    kk = with_prev(to_blocks(k))
    vv = with_prev(to_blocks(v))
    s = jnp.einsum('brhnqd,brhnkd->brhnqk', qb, kk).astype(jnp.float32) * d ** -0.5
    i = jnp.arange(n)[:, None]
    j = jnp.arange(2 * n)[None, :]
    steps = n + i - j
    in_band = (steps >= 0) & (steps <= n)
    has_key = (jnp.arange(nb)[:, None, None] > 0) | (j[None] >= n)
    valid = in_band[None] & has_key
    s = s - slopes[:, None, None, None] * (steps * dilation).astype(jnp.float32)
    s = jnp.where(valid, s, -jnp.inf)
    m = jnp.max(s, axis=-1, keepdims=True)
    e = jnp.exp(s - m)
    denom = jnp.sum(e, axis=-1, keepdims=True)
    o = jnp.einsum('brhnqk,brhnkd->brhnqd', (e / denom).astype(v.dtype), vv)
    lse = (m + jnp.log(denom))[..., 0]

    def from_blocks(a):
        rest = a.shape[5:]
        a = a.reshape(B, dilation, H, Lp // dilation, *rest)
        perm = (0, 3, 1, 2) + tuple(range(4, a.ndim))
        return a.transpose(perm).reshape(B, Lp, H, *rest)[:, :S]

    return from_blocks(o), from_blocks(lse)


def _dilated_attention(q, k, v, slopes):
    outs, lses = [], []
    for window, dilation in DILATED_BRANCHES:
        o, lse = _dilated_branch(q, k, v, slopes, window, dilation)
        outs.append(o)
        lses.append(lse)
    w = jax.nn.softmax(jnp.stack(lses), axis=0)
    o = jnp.sum(w[..., None] * jnp.stack(outs).astype(jnp.float32), axis=0)
    return o.astype(v.dtype)


def _hybrid_layer(x, layer_idx, w_in, b_f, lq1, lk1, lq2, lk2, subln_g, w_out,
                  ln1_g, ln1_b, w1, w2, ln2_g, ln2_b):
    B, S, _ = x.shape
    h = jnp.einsum('bsd,de->bse', x, w_in)
    parts = jnp.split(h, np.cumsum(IN_SPLITS)[:-1].tolist(), axis=-1)
    fq, fk, fv, fz, sq, sk, sv, dq, dk, dv, gq, gk, gv = parts
    slopes = _alibi_slopes()

    log_f = jax.nn.log_sigmoid(fz.astype(jnp.float32) + b_f.astype(jnp.float32))
    log_f_cum = jnp.cumsum(log_f, axis=1).transpose(0, 2, 1)
    o_fox = _forgetting_attention(_heads(fq, N_FOX_HEADS), _heads(fk, N_FOX_HEADS),
                                  _heads(fv, N_FOX_HEADS), log_f_cum)

    o_sb = _stick_breaking_attention(_heads(sq, N_SB_HEADS), _heads(sk, N_SB_HEADS),
                                     _heads(sv, N_SB_HEADS))

    dq = dq.reshape(B, S, N_DIFF_HEADS, 2, HEAD_DIM).transpose(0, 2, 3, 1, 4)
    dk = dk.reshape(B, S, N_DIFF_HEADS, 2, HEAD_DIM).transpose(0, 2, 3, 1, 4)
    lam_init = 0.8 - 0.6 * math.exp(-0.3 * layer_idx)
    lam = (jnp.exp(jnp.sum(lq1.astype(jnp.float32) * lk1.astype(jnp.float32)))
           - jnp.exp(jnp.sum(lq2.astype(jnp.float32) * lk2.astype(jnp.float32))) + lam_init)
    od = _differential_attention(dq[:, :, 0], dq[:, :, 1], dk[:, :, 0], dk[:, :, 1],
                                 _heads(dv, N_DIFF_HEADS), lam, slopes[:N_DIFF_HEADS])
    odf = od.astype(jnp.float32)
    odf = odf * lax.rsqrt(jnp.mean(jnp.square(odf), axis=-1, keepdims=True) + RMS_EPS)
    o_diff = (odf * subln_g * (1.0 - lam_init)).astype(x.dtype)

    o_dil = _dilated_attention(gq.reshape(B, S, N_DIL_HEADS, HEAD_DIM),
                               gk.reshape(B, S, N_DIL_HEADS, HEAD_DIM),
                               gv.reshape(B, S, N_DIL_HEADS, HEAD_DIM),
                               slopes[N_DIFF_HEADS:])

    mixed = jnp.concatenate([_merge_heads(o_fox), _merge_heads(o_sb), _merge_heads(o_diff),
                             o_dil.reshape(B, S, -1)], axis=-1)
    y = jnp.einsum('bse,ed->bsd', mixed, w_out)
    x = _layer_norm(DEEPNORM_ALPHA * x + y, ln1_g, ln1_b)

    a = jax.nn.relu(jnp.einsum('bsd,df->bsf', x, w1))
    y = jnp.einsum('bsf,fd->bsd', a * a, w2)
    return _layer_norm(DEEPNORM_ALPHA * x + y, ln2_g, ln2_b)


def setup_inputs(seed: int = 0) -> dict:
    key = jax.random.key(seed)
    ks = jax.random.split(key, 16)
    nrm = jax.random.normal
    col_scale = np.ones((IN_WIDTH,), np.float32)
    off = np.concatenate([[0], np.cumsum(IN_SPLITS)])
    for seg in V_SEGMENTS:
        col_scale[off[seg]:off[seg + 1]] = DEEPNORM_BETA
    x = nrm(ks[0], (BATCH, SEQ, D_MODEL), jnp.float32)
    w_in = nrm(ks[1], (DEPTH, D_MODEL, IN_WIDTH), jnp.float32) * (D_MODEL ** -0.5) * jnp.asarray(col_scale)
    fox_forget_bias = FORGET_BIAS_INIT + 0.1 * nrm(ks[2], (DEPTH, N_FOX_HEADS), jnp.float32)
    diff_lambda_q1 = 0.1 * nrm(ks[3], (DEPTH, HEAD_DIM), jnp.float32)
    diff_lambda_k1 = 0.1 * nrm(ks[4], (DEPTH, HEAD_DIM), jnp.float32)
    diff_lambda_q2 = 0.1 * nrm(ks[5], (DEPTH, HEAD_DIM), jnp.float32)
    diff_lambda_k2 = 0.1 * nrm(ks[6], (DEPTH, HEAD_DIM), jnp.float32)
    diff_subln_gain = 1.0 + 0.02 * nrm(ks[7], (DEPTH, 2 * HEAD_DIM), jnp.float32)
    w_out = nrm(ks[8], (DEPTH, MIX_WIDTH, D_MODEL), jnp.float32) * (MIX_WIDTH ** -0.5) * DEEPNORM_BETA
    ln1_gain = 1.0 + 0.02 * nrm(ks[9], (DEPTH, D_MODEL), jnp.float32)
    ln1_bias = 0.02 * nrm(ks[10], (DEPTH, D_MODEL), jnp.float32)
    w_mlp_in = nrm(ks[11], (DEPTH, D_MODEL, D_FF), jnp.float32) * (D_MODEL ** -0.5) * DEEPNORM_BETA
    w_mlp_out = nrm(ks[12], (DEPTH, D_FF, D_MODEL), jnp.float32) * (D_FF ** -0.5) * DEEPNORM_BETA
    ln2_gain = 1.0 + 0.02 * nrm(ks[13], (DEPTH, D_MODEL), jnp.float32)
    ln2_bias = 0.02 * nrm(ks[14], (DEPTH, D_MODEL), jnp.float32)
    return {"x": x, "w_in": w_in, "fox_forget_bias": fox_forget_bias,
            "diff_lambda_q1": diff_lambda_q1, "diff_lambda_k1": diff_lambda_k1,
            "diff_lambda_q2": diff_lambda_q2, "diff_lambda_k2": diff_lambda_k2,
            "diff_subln_gain": diff_subln_gain, "w_out": w_out,
            "ln1_gain": ln1_gain, "ln1_bias": ln1_bias,
            "w_mlp_in": w_mlp_in, "w_mlp_out": w_mlp_out,
            "ln2_gain": ln2_gain, "ln2_bias": ln2_bias}


def reference(x, w_in, fox_forget_bias, diff_lambda_q1, diff_lambda_k1, diff_lambda_q2,
              diff_lambda_k2, diff_subln_gain, w_out, ln1_gain, ln1_bias, w_mlp_in,
              w_mlp_out, ln2_gain, ln2_bias):
    for l in range(DEPTH):
        x = _hybrid_layer(x, l, w_in[l], fox_forget_bias[l], diff_lambda_q1[l], diff_lambda_k1[l],
                          diff_lambda_q2[l], diff_lambda_k2[l], diff_subln_gain[l], w_out[l],
                          ln1_gain[l], ln1_bias[l], w_mlp_in[l], w_mlp_out[l],
                          ln2_gain[l], ln2_bias[l])
    return x
```

```python
import numpy as np
import concourse.bass as bass
import concourse.mybir as mybir
from concourse.bass_utils import run_bass_kernel_spmd

F32 = mybir.dt.float32
BF16 = mybir.dt.bfloat16
AF = mybir.ActivationFunctionType
ALU = mybir.AluOpType
AX = mybir.AxisListType

D = 2048
SEQ = 2048
KC = 16
DEPTH = 2
NEG = -30000.0
ALPHA = (2 * DEPTH) ** 0.25
NWT = 192
NSLOT = 3
OVERLAP = True
GROUPS = ([0, 2, 1, 3], [1, 3, 2, 0])
N_MASK = 17
CC_COLS = 192 + 128 + 16 + 16 + 2
PP_COLS = 64 + 1 + 256 + 1


def I(name, *args, **kw):
    return (name, args, kw)


class Tok:
    __slots__ = ("sem", "val", "eng")

    def __init__(self, sem, val, eng):
        self.sem, self.val, self.eng = sem, val, eng


class Sched:
    COMPUTE = ("pe", "act", "dve", "pool")

    def __init__(self, nc, sems, dsems):
        self.nc = nc
        self.sem = sems
        self.cnt = sems["_cnt"]
        self.dsem = dsems
        self.dcnt = dsems["_cnt"]
        self.pending = {e: [] for e in self.COMPUTE}
        self.streams = {e: [] for e in ("pe", "act", "dve", "pool", "sp")}
        self.res = {}
        self.dma_toks = []

    def _r(self, key):
        r = self.res.get(key)
        if r is None:
            r = self.res[key] = [None, []]
        return r

    def op(self, eng, fn, reads=(), writes=(), sig=True, dma=None):
        assert sig or eng == "pe"
        deps = []
        for k in reads:
            r = self._r(k)
            if r[0] is not None:
                deps.append(r[0])
        for k in writes:
            r = self._r(k)
            if r[0] is not None:
                deps.append(r[0])
            deps.extend(r[1])
        if dma is None and eng == "pe":
            deps = [d for d in deps if d.eng != "pe"]
        if dma is not None:
            assert dma in self.dsem, dma
            self.dcnt[dma] += 16
            tok = Tok(self.dsem[dma], self.dcnt[dma], "dma")
            inc = (self.dsem[dma], 16)
            self.dma_toks.append(tok)
        else:
            tok = Tok(self.sem[eng], None, eng)
            self.pending[eng].append(tok)
            inc = None
            if sig:
                self.cnt[eng] += 1
                for t in self.pending[eng]:
                    t.val = self.cnt[eng]
                self.pending[eng] = []
                inc = (self.sem[eng], 1)
        self.streams[eng].append((deps, fn, inc))
        for k in reads:
            self._r(k)[1].append(tok)
        for k in writes:
            r = self._r(k)
            r[0] = tok
            r[1] = []
        return tok

    def run(self):
        for e in self.COMPUTE:
            assert not self.pending[e], f"unsignalled tail on {e}"
        self.streams["sp"].append((list(self.dma_toks), None, None))
        with self.nc.Block() as block:
            binder = {"pe": block.tensor, "act": block.scalar, "dve": block.vector,
                      "pool": block.gpsimd, "sp": block.sync}
            for e, stream in self.streams.items():
                if not stream:
                    continue

                def body(engine, stream=stream):
                    waited = {}
                    for deps, fn, inc in stream:
                        need = {}
                        for d in deps:
                            assert d.val is not None
                            key = id(d.sem)
                            if waited.get(key, 0) >= d.val:
                                continue
                            if key not in need or need[key][1] < d.val:
                                need[key] = (d.sem, d.val)
                        for key, (s, v) in need.items():
                            engine.wait_ge(s, v)
                            waited[key] = v
                        if fn is None:
                            continue
                        inst = getattr(engine, fn[0])(*fn[1], **fn[2])
                        if inc is not None:
                            inst.then_inc(inc[0], inc[1])

                binder[e](body)


def _slopes():
    n = 12
    return np.exp2(-8.0 * np.arange(1, n + 1, dtype=np.float64) / n)


def _tokperm(half):
    return np.concatenate([np.arange(512) + 512 * g for g in GROUPS[half]])


def _mask_tables():
    sl = np.arange(128)[:, None]
    tl = np.arange(512)[None, :]
    m = np.zeros((N_MASK, 128, 512), np.float32)
    for i in range(4):
        m[i] = np.where(128 * i + sl <= tl, 0.0, NEG)
        m[4 + i] = np.where(128 * i + sl < tl, 0.0, NEG)

    def dil(rel):
        d = 128 * rel + tl - sl
        mult = ((d >= 0) & (d <= 128)).astype(np.int64) + ((d >= 0) & (d % 4 == 0) & (d <= 512)) \
            + ((d >= 0) & (d % 16 == 0) & (d <= 2048))
        return np.where(mult > 0, np.log(np.maximum(mult, 1)), NEG).astype(np.float32)

    for i in range(4):
        m[8 + i] = dil(-i)
    for r in range(1, 5):
        m[12 + (4 - r)] = dil(r)
    m[16] = dil(8)
    return m


def _const_mats():
    c = np.zeros((4, 128, 128), np.float32)
    c[0] = np.eye(128)
    c[1] = -np.eye(128)
    kk = np.arange(128)[:, None]
    mm = np.arange(128)[None, :]
    c[2] = (kk >= mm).astype(np.float32)
    c[3] = 1.0
    return c


def _core_consts(half):
    tp = _tokperm(half)
    s_nat = tp.reshape(16, 128).T.astype(np.float64)
    pad = np.zeros(16)
    if half == 0:
        pad[12:16] = NEG
    sl = _slopes()
    cc = np.zeros((128, CC_COLS), np.float32)
    biasK = s_nat[:, :, None] * sl[None, None, :] + pad[None, :, None]
    cc[:, 0:192] = biasK.reshape(128, 192)
    cc[:, 192:320] = np.repeat(pad[None, :, None], 8, axis=2).repeat(128, axis=0).reshape(128, 128)
    cc[:, 320:336] = pad[None, :]
    nat = GROUPS[half]
    m = np.zeros((4, 4))
    for g in range(4):
        for g2 in range(4):
            m[g, g2] = 1.0 if nat[g2] < nat[g] else 0.0
    cc[:, 336:352] = m.reshape(1, 16)
    cc[:, 352] = 1.0 if half == 0 else 0.0
    cc[:, 353] = 0.0 if half == 0 else 1.0
    rtab = (-sl[:, None] * tp[None, 0:1024].astype(np.float64)).astype(np.float32)
    return cc, rtab


IN_OFF = {"fq": 0, "fk": 512, "fv": 1024, "fz": 1536, "sq": 1544, "sk": 2056, "sv": 2568,
          "dq": 3080, "dk": 3592, "dv": 4104, "gq": 4616, "gk": 5128, "gv": 5640}
MIX = (("fq", "fk", "fv"), ("sq", "sk", "sv"), ("dq", "dk", "dv"), ("gq", "gk", "gv"))


def _wtile(w_cols):
    return np.ascontiguousarray(w_cols.reshape(16, 128, 128).transpose(1, 0, 2).reshape(128, 2048))


def _layer_weights(w_in, w_out, w1, w2):
    W = np.empty((NWT, 128, 2048), np.float32)
    t = 0
    for u in range(16):
        m, i = divmod(u, 4)
        for nm in MIX[m]:
            c0 = IN_OFF[nm] + i * 128
            W[t] = _wtile(w_in[:, c0:c0 + 128])
            t += 1
    for oc in range(16):
        W[t] = _wtile(w_out[:, oc * 128:(oc + 1) * 128])
        t += 1
    for qf in range(4):
        for fcl in range(16):
            fc = qf * 16 + fcl
            W[t] = _wtile(w1[:, fc * 128:(fc + 1) * 128])
            t += 1
        for c in range(16):
            W[t] = _wtile(w2[qf * 2048:(qf + 1) * 2048, c * 128:(c + 1) * 128])
            t += 1
    assert t == NWT
    wfz = np.ascontiguousarray(
        w_in[:, 1536:1544].reshape(16, 128, 8).transpose(1, 0, 2).reshape(128, 128))
    return W, wfz


def _layer_params(l, inp):
    pp = np.zeros((128, PP_COLS), np.float32)
    pp[:, 0:16] = inp["ln1_gain"][l].reshape(16, 128).T
    pp[:, 16:32] = inp["ln1_bias"][l].reshape(16, 128).T
    pp[:, 32:48] = inp["ln2_gain"][l].reshape(16, 128).T
    pp[:, 48:64] = inp["ln2_bias"][l].reshape(16, 128).T
    pp[:, 64] = inp["diff_subln_gain"][l]
    pp[:, 65:129] = inp["diff_lambda_q1"][l][None, :]
    pp[:, 129:193] = inp["diff_lambda_k1"][l][None, :]
    pp[:, 193:257] = inp["diff_lambda_q2"][l][None, :]
    pp[:, 257:321] = inp["diff_lambda_k2"][l][None, :]
    pp[0:8, 321] = inp["fox_forget_bias"][l]
    return pp


class Prog:
    def __init__(self, layers, debug=False):
        self.layers = layers
        self.debug = debug
        nc = self.nc = bass.Bass("TRN2", target_bir_lowering=False)
        nL = len(layers)
        self.xT = nc.dram_tensor("xT", [D, SEQ], F32, kind="ExternalInput").ap()
        self.W = [nc.dram_tensor(f"W{i}", [NWT, 128, 2048], F32, kind="ExternalInput").ap() for i in range(nL)]
        self.wfz = [nc.dram_tensor(f"wfz{i}", [128, 128], F32, kind="ExternalInput").ap() for i in range(nL)]
        self.pp = [nc.dram_tensor(f"pp{i}", [128, PP_COLS], F32, kind="ExternalInput").ap() for i in range(nL)]
        self.cc = nc.dram_tensor("cc", [128, CC_COLS], F32, kind="ExternalInput").ap()
        self.rtab = nc.dram_tensor("rtab", [12, 1024], F32, kind="ExternalInput").ap()
        self.masks = nc.dram_tensor("masks", [N_MASK, 128, 512], F32, kind="ExternalInput").ap()
        self.cmat = nc.dram_tensor("cmat", [4, 128, 128], F32, kind="ExternalInput").ap()
        self.yT = nc.dram_tensor("yT", [D, 1024], F32, kind="ExternalOutput").ap()
        if debug:
            self.dbg = nc.dram_tensor("dbg", [D, 1024], F32, kind="ExternalOutput").ap()
        if nL > 1:
            self.xsave = nc.dram_tensor("xsave", [D, 1024], F32).ap()
            self.xs = [nc.dram_tensor(f"xs{i}", [D, 512], BF16) for i in range(2)]
            self.xg = [nc.dram_tensor(f"xg{i}", [2 * D, 512], BF16) for i in range(2)]
        self.sems = {e: nc.alloc_semaphore(f"sem_{e}") for e in Sched.COMPUTE}
        self.sems["_cnt"] = {e: 0 for e in Sched.COMPUTE}
        self.dsems = {"_cnt": {}}
        for nm in ["c0", "c1", "xt", "xres", "yout", "dbg0", "dbg1", "qrow00", "qrow01", "qrow10", "qrow11", "qrowp00", "qrowp01", "qrowp10", "qrowp11", "xch0", "xch1", "xch2", "xch3"] + \
                [f"wr{i}" for i in range(NSLOT)]:
            self.dsems[nm] = nc.alloc_semaphore(f"dsem_{nm}")
            self.dsems["_cnt"][nm] = 0
        self.pst = nc.alloc_psum_tensor("pst", [128, 8, 512], F32)
        self.ps = [self.pst[:, i, :] for i in range(8)]
        R = self.R = {}
        R["R1"] = nc.alloc_sbuf_tensor("R1", [128, 16384], F32)
        R["R2"] = nc.alloc_sbuf_tensor("R2", [128, 8192], F32)
        R["R4"] = nc.alloc_sbuf_tensor("R4", [128, 8192], F32)
        R["R5"] = nc.alloc_sbuf_tensor("R5", [128, 8192], F32)
        self.wr = [nc.alloc_sbuf_tensor(f"wr{i}", [128, 16, 128], BF16) for i in range(NSLOT)]
        self.maskt = nc.alloc_sbuf_tensor("maskt", [128, N_MASK, 512], BF16)
        self.cmb = nc.alloc_sbuf_tensor("cmb", [128, 4, 128], BF16)
        self.cm32 = nc.alloc_sbuf_tensor("cm32", [128, 4, 128], F32)
        self.cct = nc.alloc_sbuf_tensor("cct", [128, CC_COLS], F32)
        self.ppt = [nc.alloc_sbuf_tensor(f"ppt{i}", [128, PP_COLS], F32) for i in range(nL)]
        self.gT = nc.alloc_sbuf_tensor("gT", [128, 16, 8], F32)
        self.small = nc.alloc_sbuf_tensor("small", [128, 160], F32)

        def view(reg, boff, shape, dt):
            n = int(np.prod(shape[1:]))
            esz = 4 if dt == F32 else 2
            assert boff % 4 == 0 and (n * esz) % 4 == 0
            ap = R[reg][:, boff // 4: boff // 4 + (n * esz) // 4]
            if dt != F32:
                ap = ap.bitcast(dt)
            if len(shape) == 3:
                ap = ap.rearrange("p (a b) -> p a b", a=shape[1])
            return ap

        self.view = view
        self.xT_bf = view("R1", 0, [128, 16, 2048], BF16)
        self.xres = view("R1", 0, [128, 16, 1024], F32)
        self.mixed = view("R2", 0, [128, 16, 1024], BF16)
        self.aT = view("R2", 0, [128, 16, 1024], BF16)
        o = 0
        KT0 = [view("R4", o + i * 4096, [128, 2048], BF16) for i in range(2)]
        o += 8192
        QT0 = [view("R4", o + i * 2048, [128, 1024], BF16) for i in range(2)]
        o += 4096
        nQT0 = [view("R4", o + i * 2048, [128, 1024], BF16) for i in range(2)]
        o += 4096
        self.VT = view("R4", o, [128, 2048], BF16)
        o += 4096
        Vtok0 = view("R4", o, [128, 16, 256], BF16)
        o += 8192
        self.E = [view("R4", o + i * 1024, [128, 512], BF16) for i in range(4)]
        self.Et = view("R4", o, [128, 4, 512], BF16)
        o += 4096
        assert o == 32768
        self.tmp32 = [view("R4", i * 2048, [128, 512], F32) for i in range(8)]
        self.fz = view("R5", 0, [128, 2048], F32)
        self.grow = view("R5", 8192, [128, 2048], F32)
        self.ones8 = view("R2", 0, [128, 512], F32)
        self.spt = view("R5", 0, [128, 4, 512], BF16).rearrange("p (a h) c -> p a h c", a=2)
        nQT1 = [view("R5", 4096 + i * 2048, [128, 1024], BF16) for i in range(2)]
        KT1 = [view("R5", 8192 + i * 4096, [128, 2048], BF16) for i in range(2)]
        self.rneg = view("R5", 16384, [128, 1024], BF16)
        self.SS32 = [view("R5", 18432 + i * 2048, [128, 512], F32) for i in range(2)]
        self.SSt = view("R5", 22528, [128, 4, 512], BF16).rearrange("p (a h) c -> p a h c", a=2)
        qt1b = nc.alloc_sbuf_tensor("qt1b", [128, 1024], BF16)
        QT1 = [view("R5", 26624, [128, 1024], BF16), qt1b[:, :]]
        self.rz = [view("R5", 28672 + i * 2048, [128, 512], F32) for i in range(2)]
        vtok1 = nc.alloc_sbuf_tensor("vtok1", [128, 16, 256], BF16)
        self.KT = [KT0, KT1]
        self.QT = [QT0, QT1]
        self.nQT = [nQT0, nQT1]
        self.Vtok = [Vtok0, vtok1[:, :, :]]
        self.x1bf = view("R5", 0, [128, 16, 1024], BF16)
        self.wt = 0
        self.pre_w = {}
        self.pre_xres = False

    def new_sched(self):
        return Sched(self.nc, self.sems, self.dsems)

    def get_w(self, S, L, t):
        if (L, t) in self.pre_w:
            return self.pre_w.pop((L, t))
        return self.load_w(S, L, t)

    def load_w(self, S, L, t):
        slot = self.wt % NSLOT
        self.wt += 1
        dst = self.wr[slot]
        src = self.W[L][t].rearrange("p (a b) -> p a b", a=16)
        S.op("pool", I("dma_start", out=dst[:], in_=src), writes=[f"wr{slot}"], dma=f"wr{slot}")
        return dst, f"wr{slot}"

    def phase_clear(self):
        sems = [self.sems[e] for e in Sched.COMPUTE] + [v for k, v in self.dsems.items() if k != "_cnt"]
        with self.nc.Block() as block:
            def body(engine):
                for s_ in sems:
                    engine.sem_clear(s_)
            block.gpsimd(body)

    def phase_setup(self):
        S = self.new_sched()
        S.op("pool", I("dma_start", out=self.maskt[:], in_=self.masks.rearrange("m p t -> p m t")),
             writes=["maskt"], dma="c0")
        S.op("pool", I("dma_start", out=self.cmb[:], in_=self.cmat.rearrange("m p t -> p m t")),
             writes=["cmb"], dma="c0")
        S.op("sp", I("dma_start", out=self.cm32[:], in_=self.cmat.rearrange("m p t -> p m t")),
             writes=["cm32"], dma="c1")
        S.op("sp", I("dma_start", out=self.cct[:], in_=self.cc), writes=["cct"], dma="c1")
        for i in range(len(self.layers)):
            S.op("sp", I("dma_start", out=self.ppt[i][:], in_=self.pp[i]), writes=["ppt"], dma="c1")
        S.run()

    def phase_attn(self, li, L, first, units=range(16), next_xres=None):
        nc = self.nc
        S = self.new_sched()
        ps = self.ps
        ident = self.cmb[:, 0, :]
        negident = self.cmb[:, 1, :]
        utri = self.cmb[:, 2, :]
        ones_b = self.cmb[:, 3, :]
        ident32 = self.cm32[:, 0, :]
        ones32 = self.cm32[:, 3, :]
        cct, ppt = self.cct, self.ppt[li]
        KT, QT, nQT, VT, Vtok, E = self.KT, self.QT, self.nQT, self.VT, self.Vtok, self.E
        xT_bf = self.xT_bf
        sm = self.small
        lam_init = 0.8 - 0.6 * float(np.exp(-0.3 * L))

        def PE(fn, reads=(), writes=(), sig=True):
            return S.op("pe", fn, reads, writes, sig)

        def ACT(fn, reads=(), writes=()):
            return S.op("act", fn, reads, writes)

        def DVE(fn, reads=(), writes=()):
            return S.op("dve", fn, reads, writes)

        if first:
            for kc in range(KC):
                S.op("pool", I("dma_start", out=xT_bf[:, kc, :], in_=self.xT[kc * 128:(kc + 1) * 128, :]),
                     writes=["xT_bf"], dma="xt")
        for bb in range(2):
            for i in range(2):
                DVE(I("memset", KT[bb][i][64:65, :], 1.0), writes=[f"KT{bb}{i}"])
            DVE(I("memset", Vtok[bb][:, :, :], 1.0), writes=[f"Vtok{bb}"])

        DVE(I("tensor_tensor", out=sm[:, 64:128], in0=ppt[:, 65:129], in1=ppt[:, 129:193], op=ALU.mult),
            reads=["ppt"], writes=["sm_a"])
        DVE(I("tensor_reduce", out=sm[:, 1:2], in_=sm[:, 64:128], axis=AX.X, op=ALU.add), reads=["sm_a"], writes=["sm_b"])
        DVE(I("tensor_tensor", out=sm[:, 64:128], in0=ppt[:, 193:257], in1=ppt[:, 257:321], op=ALU.mult),
            reads=["ppt", "sm_b"], writes=["sm_a"])
        DVE(I("tensor_reduce", out=sm[:, 2:3], in_=sm[:, 64:128], axis=AX.X, op=ALU.add), reads=["sm_a"], writes=["sm_c"])
        ACT(I("activation", out=sm[:, 4:5], in_=sm[:, 1:2], func=AF.Exp), reads=["sm_b"], writes=["sm_d"])
        ACT(I("activation", out=sm[:, 5:6], in_=sm[:, 2:3], func=AF.Exp), reads=["sm_c"], writes=["sm_e"])
        DVE(I("scalar_tensor_tensor", out=sm[:, 6:7], in0=sm[:, 5:6], scalar=-lam_init, in1=sm[:, 4:5],
                                             op0=ALU.add, op1=ALU.subtract), reads=["sm_d", "sm_e"], writes=["neglam"])
        DVE(I("tensor_scalar", out=sm[:, 7:8], in0=ppt[:, 64:65], scalar1=(1.0 - lam_init), scalar2=None, op0=ALU.mult),
            reads=["ppt"], writes=["subs"])
        DVE(I("tensor_scalar", out=sm[:, 8:9], in0=ppt[:, 321:322], scalar1=-1.0, scalar2=None, op0=ALU.mult),
            reads=["ppt"], writes=["negbf"])
        neglam = sm[:, 6:7]
        subs = sm[:, 7:8]
        negbf = sm[0:8, 8:9]

        def evac(out, in_, reads, writes, scale=None, shifted=False, allow_act=True):
            use_act = allow_act and (not shifted) and (evac_toggle[0] % 2 == 1)
            evac_toggle[0] += 1
            if use_act:
                if scale is None:
                    ACT(I("copy", out=out, in_=in_), reads, writes)
                else:
                    ACT(I("activation", out=out, in_=in_, func=AF.Identity, scale=scale), reads, writes)
            else:
                if scale is None:
                    DVE(I("tensor_copy", out=out, in_=in_), reads, writes)
                else:
                    DVE(I("tensor_scalar", out=out, in0=in_, scalar1=scale, scalar2=None, op0=ALU.mult), reads, writes)

        evac_toggle = [0]

        wfz_t = self.view("R4", 0, [128, 16, 8], BF16)
        S.op("pool", I("dma_start", out=wfz_t, in_=self.wfz[li].rearrange("p (a b) -> p a b", a=16)),
             writes=["KT00"], dma="c0")
        fz, grow, rneg = self.fz, self.grow, self.rneg
        for tc in range(4):
            b = 6 + (tc % 2)
            for kc in range(KC):
                PE(I("matmul", ps[b][0:8, :], lhsT=wfz_t[:, kc, :],
                                                               rhs=xT_bf[:, kc, tc * 512:(tc + 1) * 512],
                                                               start=(kc == 0), stop=(kc == KC - 1)),
                   reads=["KT00", "xT_bf"], writes=[f"ps{b}"], sig=(kc == KC - 1))
            ACT(I("activation", out=fz[0:8, tc * 512:(tc + 1) * 512], in_=ps[b][0:8, :], func=AF.Exp,
                                                   bias=negbf, scale=-1.0), reads=[f"ps{b}", "negbf"], writes=["fz"])
        ACT(I("activation", out=fz[0:8, :], in_=fz[0:8, :], func=AF.Ln, bias=1.0, scale=1.0), reads=["fz"], writes=["fz"])
        DVE(I("memset", KT[0][0][64:65, :], 1.0), reads=[], writes=["KT00"])
        DVE(I("memset", self.ones8[0:8, :], 1.0), writes=["ones8"])
        for g in range(4):
            DVE(I("tensor_tensor_scan", out=grow[0:8, g * 512:(g + 1) * 512], data0=self.ones8[0:8, :],
                                                    data1=fz[0:8, g * 512:(g + 1) * 512], initial=0.0,
                                                    op0=ALU.mult, op1=ALU.add), reads=["fz", "ones8"], writes=["grow"])
        off = sm[0:8, 16:20]
        DVE(I("memset", sm[0:8, 16:20], 0.0), writes=["off"])
        for g in range(4):
            for g2 in range(4):
                DVE(I("scalar_tensor_tensor",
                    out=sm[0:8, 16 + g:17 + g], in0=grow[0:8, g2 * 512 + 511:g2 * 512 + 512],
                    scalar=cct[0:8, 336 + g * 4 + g2:337 + g * 4 + g2], in1=sm[0:8, 16 + g:17 + g],
                    op0=ALU.mult, op1=ALU.add), reads=["grow", "cct", "off"], writes=["off"])
        for g in range(4):
            DVE(I("tensor_scalar", out=grow[0:8, g * 512:(g + 1) * 512], in0=grow[0:8, g * 512:(g + 1) * 512],
                                               scalar1=sm[0:8, 16 + g:17 + g], scalar2=None, op0=ALU.add),
                reads=["off", "grow"], writes=["grow"])
        DVE(I("tensor_scalar", out=rneg[0:8, :], in0=grow[0:8, 0:1024], scalar1=-1.0, scalar2=None, op0=ALU.mult),
            reads=["grow"], writes=["rneg"])
        for blk in range(16):
            PE(I("transpose", out=ps[6][:, blk * 8:(blk + 1) * 8], in_=grow[0:8, blk * 128:(blk + 1) * 128],
                                              identity=ident32[0:8, 0:8]), reads=["grow", "cm32"], writes=["ps6"])
        DVE(I("tensor_tensor", out=self.gT[:, :, :].rearrange("p a b -> p (a b)"), in0=ps[6][:, 0:128],
                                      in1=cct[:, 192:320], op=ALU.add), reads=["ps6", "cct"], writes=["gT"])

        deferred = []

        def flush_deferred(bank):
            while deferred:
                deferred.pop(0)(bank)

        def finish64(O, Oname, rzi, dst, dname):
            rz = self.rz[rzi]
            zc = self.SS32[rzi]
            ACT(I("copy", out=rz[0:64, :], in_=O[0:64, :]), reads=[Oname], writes=[f"rz{rzi}"])
            DVE(I("tensor_copy", out=zc[0:64, :], in_=O[64:128, :]), reads=[Oname], writes=[f"SS32{rzi}"])
            DVE(I("reciprocal", out=zc[0:64, :], in_=zc[0:64, :]), reads=[f"SS32{rzi}"], writes=[f"SS32{rzi}"])
            DVE(I("tensor_tensor", out=dst, in0=rz[0:64, :], in1=zc[0:64, :], op=ALU.mult),
                reads=[f"rz{rzi}", f"SS32{rzi}"], writes=[dname])

        def softmax_chains(chains, klist, lookahead=True):
            n = len(klist)

            def emitS(c, j):
                pos, mask = klist[j]
                sb = c["S"][j % 2]
                kk = c["K"]
                PE(I("matmul", ps[sb][:, :], lhsT=c["KT"][0:kk, pos * 128:(pos + 1) * 128], rhs=c["QT"][0:kk, c["qc"]],
                     start=True, stop=(mask is None)),
                   reads=[c["KTn"], c["QTn"]], writes=[f"ps{sb}"], sig=(mask is None))
                if mask is not None:
                    PE(I("matmul", ps[sb][:, :], lhsT=ident, rhs=self.maskt[:, mask, :], start=False, stop=True),
                       reads=["cmb", "maskt"], writes=[f"ps{sb}"])

            def emitExp(c, j):
                pos = klist[j][0]
                sb = c["S"][j % 2]
                et = c["E"][j % 2]
                ACT(I("activation", out=E[et][:, :], in_=ps[sb][:, :], func=AF.Exp, bias=c["bias"](pos), scale=1.0),
                    reads=[f"ps{sb}", "gT", "cct"], writes=[f"E{et}"])

            def emitPV(c, j):
                pos = klist[j][0]
                et = c["E"][j % 2]
                PE(I("matmul", ps[c["O"]][:, :], lhsT=c["vl"](pos), rhs=E[et][:, :], start=(j == 0), stop=(j == n - 1)),
                   reads=[f"E{et}", c["Vn"]], writes=[f"ps{c['O']}"], sig=("Z" not in c))
                if "Z" in c:
                    PE(I("matmul", ps[c["Z"]][:, :], lhsT=ones_b, rhs=E[et][:, :], start=(j == 0), stop=(j == n - 1)),
                       reads=[f"E{et}", "cmb"], writes=[f"ps{c['Z']}"])

            if lookahead:
                for c in chains:
                    emitS(c, 0)
                for j in range(n):
                    for c in chains:
                        if j + 1 < n:
                            emitS(c, j + 1)
                    for c in chains:
                        emitExp(c, j)
                    for c in chains:
                        emitPV(c, j)
                    if j == min(2, n - 1):
                        flush_deferred(chains[0]["S"][j % 2])
                    yield
            else:
                for j in range(n):
                    for c in chains:
                        emitS(c, j)
                    for c in chains:
                        emitExp(c, j)
                    yield
                    for c in chains:
                        emitPV(c, j)
                    if j == min(2, n - 1):
                        flush_deferred(chains[0]["S"][0])

        KL_A = [(0, 0), (1, 1), (2, 2), (3, 3), (12, None), (13, None), (14, None), (15, None)]
        KL_B = [(4, 0), (5, 1), (6, 2), (7, 3)] + [(p, None) for p in (8, 9, 10, 11, 0, 1, 2, 3, 12, 13, 14, 15)]
        KL_A_D = [(0, 8), (1, 9), (2, 10), (3, 11), (12, 12), (13, 13), (14, 14), (15, 15)]
        KL_B_D = [(4, 8), (5, 9), (6, 10), (7, 11), (8, 12), (9, 13), (10, 14), (11, 15)] + \
                 [(p, 16) for p in (0, 1, 2, 3, 12, 13, 14, 15)]
        KL_A_S = [(3, 7), (2, 6), (1, 5), (0, 4), (15, None), (14, None), (13, None), (12, None)]
        KL_B_S = [(7, 7), (6, 6), (5, 5), (4, 4)] + [(p, None) for p in (11, 10, 9, 8, 3, 2, 1, 0, 15, 14, 13, 12)]
        KIND = ("fox", "sb", "diff", "dil")
        vdirty = [False, False]

        def proj_gen(u, buf, banks, overlapped):
            kind = KIND[u // 4]
            i = u % 4
            KTb, QTb, nQTb, Vt = KT[buf], QT[buf], nQT[buf], Vtok[buf]
            kq = [f"KT{buf}0", f"KT{buf}1"]
            qq = [f"QT{buf}0", f"QT{buf}1"]
            nq = [f"nQT{buf}0", f"nQT{buf}1"]
            vn = f"Vtok{buf}"
            wq, wqn = self.get_w(S, li, 3 * u + 0)
            wk, wkn = self.get_w(S, li, 3 * u + 1)
            wv, wvn = self.get_w(S, li, 3 * u + 2)
            aa = not overlapped
            bsel = [0]

            def nextbank():
                b = banks[bsel[0] % 2]
                bsel[0] += 1
                return b

            if kind != "diff" and vdirty[buf]:
                DVE(I("memset", Vt[:, :, :], 1.0), writes=[vn])
                vdirty[buf] = False
            if kind == "diff":
                vdirty[buf] = True

            def group(w, wn, tc, b):
                for kc in range(KC):
                    PE(I("matmul", ps[b][:, :], lhsT=w[:, kc, :], rhs=xT_bf[:, kc, tc * 512:(tc + 1) * 512],
                         start=(kc == 0), stop=(kc == KC - 1)),
                       reads=[wn, "xT_bf"], writes=[f"ps{b}"], sig=(kc == KC - 1))
                    if kc % 4 == 3:
                        yield

            for tc in range(2):
                b = nextbank()
                yield from group(wq, wqn, tc, b)
                p, pn = ps[b], f"ps{b}"
                cols = slice(tc * 512, (tc + 1) * 512)
                evac(QTb[0][0:64, cols], p[0:64, :], [pn], [qq[0]], scale=0.125, allow_act=aa)
                evac(QTb[1][0:64, cols], p[64:128, :], [pn], [qq[1]], scale=0.125, shifted=True)
                if kind == "sb":
                    evac(nQTb[0][0:64, cols], p[0:64, :], [pn], [nq[0]], scale=-0.125, allow_act=aa)
                    evac(nQTb[1][0:64, cols], p[64:128, :], [pn], [nq[1]], scale=-0.125, shifted=True)
            for hh in range(2):
                if kind == "fox":
                    h = 2 * i + hh
                    S.op("sp", I("dma_start", out=QTb[hh][64:65, :], in_=rneg[h:h + 1, :]), reads=["rneg"], writes=[qq[hh]],
                         dma=f"qrow{buf}{hh}")
                elif kind == "sb":
                    DVE(I("memset", QTb[hh][64:65, :], 0.0), writes=[qq[hh]])
                    DVE(I("memset", nQTb[hh][64:65, :], 0.0), writes=[nq[hh]])
                elif kind in ("diff", "dil"):
                    h = i if kind == "diff" else 4 + 2 * i + hh
                    S.op("pool", I("dma_start", out=QTb[hh][64:65, :], in_=self.rtab[h:h + 1, :]), writes=[qq[hh]],
                         dma=f"qrowp{buf}{hh}")
            if not overlapped:
                flush_deferred(banks[0])
            for tc in range(4):
                b = nextbank()
                yield from group(wk, wkn, tc, b)
                p, pn = ps[b], f"ps{b}"
                cols = slice(tc * 512, (tc + 1) * 512)
                evac(KTb[0][0:64, cols], p[0:64, :], [pn], [kq[0]], allow_act=aa)
                evac(KTb[1][0:64, cols], p[64:128, :], [pn], [kq[1]], shifted=True)
            for tc in range(4):
                b = nextbank()
                yield from group(wv, wvn, tc, b)
                p, pn = ps[b], f"ps{b}"
                cols = slice(tc * 512, (tc + 1) * 512)
                evac(VT[:, cols], p[:, :], [pn], ["VT"], allow_act=aa)
            for tg in range(4):
                b = nextbank()
                pbf = ps[b][:, :].bitcast(BF16)
                for t4 in range(4):
                    tb = tg * 4 + t4
                    PE(I("transpose", out=pbf[:, t4 * 128:(t4 + 1) * 128], in_=VT[:, tb * 128:(tb + 1) * 128], identity=ident),
                       reads=["VT", "cmb"], writes=[f"ps{b}"], sig=(t4 == 3))
                if kind == "diff":
                    DVE(I("tensor_copy", out=Vt[:, tg * 4:(tg + 1) * 4, 0:128],
                          in_=pbf[:, 0:512].rearrange("p (a b) -> p a b", a=4)), reads=[f"ps{b}"], writes=[vn])
                else:
                    DVE(I("tensor_copy", out=Vt[:, tg * 4:(tg + 1) * 4, :].rearrange("p a (h c) -> p a h c", h=2)[:, :, :, 0:64],
                          in_=pbf[:, 0:512].rearrange("p (a h c) -> p a h c", a=4, h=2)), reads=[f"ps{b}"], writes=[vn])
                yield

        def chain_gen(u, buf):
            kind = KIND[u // 4]
            i = u % 4
            KTb, QTb, nQTb, Vt = KT[buf], QT[buf], nQT[buf], Vtok[buf]
            kq = [f"KT{buf}0", f"KT{buf}1"]
            qq = [f"QT{buf}0", f"QT{buf}1"]
            nq = [f"nQT{buf}0", f"nQT{buf}1"]
            vn = f"Vtok{buf}"
            mixu = self.mixed[:, u, :]
            if kind in ("fox", "dil"):
                for slot in range(2):
                    chains = []
                    for hh in range(2):
                        if kind == "fox":
                            h = 2 * i + hh
                            bias = (lambda pos, h=h: self.gT[:, pos, h:h + 1])
                        else:
                            h = 4 + 2 * i + hh
                            bias = (lambda pos, h=h: cct[:, pos * 12 + h:pos * 12 + h + 1])
                        chains.append(dict(KT=KTb[hh], KTn=kq[hh], QT=QTb[hh], QTn=qq[hh], K=65, Vn=vn,
                                           qc=slice(slot * 512, (slot + 1) * 512), S=(2 * hh, 2 * hh + 1),
                                           E=(2 * hh, 2 * hh + 1), O=4 + hh, bias=bias,
                                           vl=(lambda pos, hh=hh: Vt[:, pos, hh * 128:(hh + 1) * 128])))
                    if kind == "fox":
                        kl = KL_A if slot == 0 else KL_B
                    else:
                        kl = KL_A_D if slot == 0 else KL_B_D
                    yield from softmax_chains(chains, kl)
                    for hh in range(2):
                        dst = mixu[hh * 64:(hh + 1) * 64, slot * 512:(slot + 1) * 512]
                        finish64(ps[4 + hh], f"ps{4 + hh}", hh, dst, "mixed")
            elif kind == "diff":
                h = i
                for slot in range(2):
                    chains = []
                    for hh in range(2):
                        chains.append(dict(KT=KTb[hh], KTn=kq[hh], QT=QTb[hh], QTn=qq[hh], K=65, Vn=vn,
                                           qc=slice(slot * 512, (slot + 1) * 512), S=(hh, hh),
                                           E=(2 * hh, 2 * hh + 1), O=4 + hh, Z=6 + hh,
                                           bias=(lambda pos, h=h: cct[:, pos * 12 + h:pos * 12 + h + 1]),
                                           vl=(lambda pos: Vt[:, pos, 0:128])))
                    yield from softmax_chains(chains, KL_A if slot == 0 else KL_B, lookahead=False)
                    rz0, rz1 = self.rz
                    zc0, zc1 = self.SS32
                    DVE(I("tensor_copy", out=rz0[:, :], in_=ps[4][:, :]), reads=["ps4"], writes=["rz0"])
                    ACT(I("copy", out=rz1[:, :], in_=ps[5][:, :]), reads=["ps5"], writes=["rz1"])
                    DVE(I("tensor_copy", out=zc0[:, :], in_=ps[6][:, :]), reads=["ps6"], writes=["SS320"])
                    ACT(I("copy", out=zc1[:, :], in_=ps[7][:, :]), reads=["ps7"], writes=["SS321"])
                    DVE(I("reciprocal", out=zc0[:, :], in_=zc0[:, :]), reads=["SS320"], writes=["SS320"])
                    DVE(I("reciprocal", out=zc1[:, :], in_=zc1[:, :]), reads=["SS321"], writes=["SS321"])
                    DVE(I("tensor_tensor", out=rz0[:, :], in0=rz0[:, :], in1=zc0[:, :], op=ALU.mult), reads=["rz0", "SS320"], writes=["rz0"])
                    DVE(I("tensor_tensor", out=rz1[:, :], in0=rz1[:, :], in1=zc1[:, :], op=ALU.mult), reads=["rz1", "SS321"], writes=["rz1"])
                    DVE(I("scalar_tensor_tensor", out=rz0[:, :], in0=rz1[:, :], scalar=neglam, in1=rz0[:, :],
                          op0=ALU.mult, op1=ALU.add), reads=["rz0", "rz1", "neglam"], writes=["rz0"])
                    DVE(I("tensor_tensor", out=zc0[:, :], in0=rz0[:, :], in1=rz0[:, :], op=ALU.mult), reads=["rz0"], writes=["SS320"])

                    def tail(bank, slot=slot, mixu=mixu, rz0=rz0, zc0=zc0, zc1=zc1):
                        PE(I("matmul", ps[bank][:, :], lhsT=ones32, rhs=zc0[:, :], start=True, stop=True),
                           reads=["SS320", "cm32"], writes=[f"ps{bank}"])
                        ACT(I("activation", out=zc1[:, :], in_=ps[bank][:, :], func=AF.Ln, bias=1e-5, scale=1.0 / 128.0),
                            reads=[f"ps{bank}"], writes=["SS321"])
                        ACT(I("activation", out=zc1[:, :], in_=zc1[:, :], func=AF.Exp, scale=-0.5), reads=["SS321"], writes=["SS321"])
                        DVE(I("tensor_tensor", out=rz0[:, :], in0=rz0[:, :], in1=zc1[:, :], op=ALU.mult),
                            reads=["rz0", "SS321"], writes=["rz0"])
                        DVE(I("tensor_scalar", out=mixu[:, slot * 512:(slot + 1) * 512], in0=rz0[:, :], scalar1=subs,
                              scalar2=None, op0=ALU.mult), reads=["rz0", "subs"], writes=["mixed"])

                    deferred.append(tail)
            else:
                spt, SSt = self.spt, self.SSt
                for slot in range(2):
                    kl = KL_A_S if slot == 0 else KL_B_S
                    n = len(kl)
                    qc = slice(slot * 512, (slot + 1) * 512)

                    def emit_z(hh, j):
                        pos, mask = kl[j]
                        zb = hh
                        PE(I("matmul", ps[zb][:, :], lhsT=KTb[hh][0:65, pos * 128:(pos + 1) * 128], rhs=QTb[hh][0:65, qc],
                             start=True, stop=(mask is None)),
                           reads=[kq[hh], qq[hh]], writes=[f"ps{zb}"], sig=(mask is None))
                        if mask is not None:
                            PE(I("matmul", ps[zb][:, :], lhsT=ident, rhs=self.maskt[:, mask, :], start=False, stop=True),
                               reads=["cmb", "maskt"], writes=[f"ps{zb}"])

                    def emit_softplus(j, hh):
                        pos, mask = kl[j]
                        p = j % 2
                        padb = cct[:, 320 + pos:321 + pos]
                        zb = hh
                        et = 2 * hh
                        ACT(I("activation", out=E[et][:, :], in_=ps[zb][:, :], func=AF.Exp, bias=padb, scale=1.0),
                            reads=[f"ps{zb}", "cct"], writes=[f"E{et}"])
                        ACT(I("activation", out=spt[:, p, hh, :], in_=E[et][:, :], func=AF.Ln, bias=1.0, scale=1.0),
                            reads=[f"E{et}"], writes=[f"sp{p}{hh}"])

                    def emit_T(j, hh):
                        pos, mask = kl[j]
                        p = j % 2
                        tb = 6 + hh
                        PE(I("matmul", ps[tb][:, :], lhsT=utri, rhs=spt[:, p, hh, :], start=True, stop=False),
                           reads=[f"sp{p}{hh}", "cmb"], writes=[f"ps{tb}"], sig=False)
                        if j > 0:
                            PE(I("matmul", ps[tb][:, :], lhsT=ones_b, rhs=SSt[:, p, hh, :], start=False, stop=False),
                               reads=[f"SSt{p}{hh}", "cmb"], writes=[f"ps{tb}"], sig=False)
                        PE(I("matmul", ps[tb][:, :], lhsT=KTb[hh][0:65, pos * 128:(pos + 1) * 128],
                             rhs=nQTb[hh][0:65, qc], start=False, stop=(mask is None)),
                           reads=[kq[hh], nq[hh]], writes=[f"ps{tb}"], sig=(mask is None))
                        if mask is not None:
                            PE(I("matmul", ps[tb][:, :], lhsT=negident, rhs=self.maskt[:, mask, :], start=False, stop=True),
                               reads=["cmb", "maskt"], writes=[f"ps{tb}"])

                    def emit_A(j, hh):
                        pos, mask = kl[j]
                        padb = cct[:, 320 + pos:321 + pos]
                        tb = 6 + hh
                        at = 2 * hh + 1
                        ACT(I("activation", out=E[at][:, :], in_=ps[tb][:, :], func=AF.Exp, bias=padb, scale=-1.0),
                            reads=[f"ps{tb}", "cct"], writes=[f"E{at}"])

                    def emit_PV(j, hh):
                        pos, mask = kl[j]
                        at = 2 * hh + 1
                        PE(I("matmul", ps[4 + hh][:, :], lhsT=Vt[:, pos, hh * 128:(hh + 1) * 128],
                             rhs=E[at][:, :], start=(j == 0), stop=(j == n - 1)),
                           reads=[f"E{at}", vn], writes=[f"ps{4 + hh}"])

                    for hh in range(2):
                        emit_z(hh, 0)
                    for hh in range(2):
                        emit_softplus(0, hh)
                    for j in range(n):
                        p = j % 2
                        for hh in range(2):
                            if j + 1 < n:
                                emit_z(hh, j + 1)
                            emit_T(j, hh)
                            emit_A(j, hh)
                        yield
                        if j + 1 < n:
                            for hh in range(2):
                                emit_softplus(j + 1, hh)
                        for hh in range(2):
                            emit_PV(j, hh)
                        if j + 1 < n:
                            q = (j + 1) % 2
                            for hh in range(2):
                                if j == 0:
                                    DVE(I("tensor_copy", out=SSt[:, q, hh, :], in_=spt[:, p, hh, :]), reads=[f"sp{p}{hh}"], writes=[f"SSt{q}{hh}"])
                                else:
                                    DVE(I("tensor_tensor", out=SSt[:, q, hh, :], in0=SSt[:, p, hh, :], in1=spt[:, p, hh, :], op=ALU.add),
                                        reads=[f"sp{p}{hh}", f"SSt{p}{hh}"], writes=[f"SSt{q}{hh}"])
                        yield
                    for hh in range(2):
                        dst = mixu[hh * 64:(hh + 1) * 64, qc]
                        DVE(I("tensor_copy", out=dst, in_=ps[4 + hh][0:64, :]), reads=[f"ps{4 + hh}"], writes=["mixed"])

        order = list(units)
        for _ in proj_gen(order[0], 0, (6, 7), False):
            pass
        for idx, u in enumerate(order):
            kind = KIND[u // 4]
            nxt = order[idx + 1] if idx + 1 < len(order) else None
            cg = chain_gen(u, idx % 2)
            if nxt is not None and OVERLAP:
                pg = proj_gen(nxt, (idx + 1) % 2, {"sb": (2, 3), "diff": (2, 3)}.get(kind, (6, 7)), True)
                r = 1 if kind == "sb" else 2
                for _ in cg:
                    for _k in range(r):
                        next(pg, None)
                for _ in pg:
                    pass
            else:
                for _ in cg:
                    pass
                if nxt is not None:
                    for _ in proj_gen(nxt, (idx + 1) % 2, (6, 7), False):
                        pass
        flush_deferred(0)
        if next_xres is not None:
            for t in range(48, 48 + NSLOT):
                self.pre_w[(li, t)] = self.load_w(S, li, t)
            for kc in range(KC):
                S.op("sp", I("dma_start", out=self.xres[:, kc, :], in_=next_xres[kc * 128:(kc + 1) * 128, :]),
                     writes=["xT_bf"], dma="xres")
            self.pre_xres = True
        if self.debug and li == len(self.layers) - 1 and self.debug == "mixed":
            for kc in range(KC):
                DVE(I("tensor_copy", out=self.tmp32[0][:, :], in_=self.mixed[:, kc, 0:512]), reads=["mixed"], writes=["dbgt0"])
                S.op("sp", I("dma_start", out=self.dbg[kc * 128:(kc + 1) * 128, 0:512], in_=self.tmp32[0][:, :]),
                     reads=["dbgt0"], dma="dbg0")
                DVE(I("tensor_copy", out=self.tmp32[1][:, :], in_=self.mixed[:, kc, 512:1024]), reads=["mixed"], writes=["dbgt1"])
                S.op("sp", I("dma_start", out=self.dbg[kc * 128:(kc + 1) * 128, 512:1024], in_=self.tmp32[1][:, :]),
                     reads=["dbgt1"], dma="dbg1")
        S.run()

    def phase_post(self, li, L, xres_src, last):
        S = self.new_sched()
        ps = self.ps
        ones_b = self.cmb[:, 3, :]
        ppt = self.ppt[li]
        cct = self.cct
        xres, mixed, x1bf, aT, xT_bf = self.xres, self.mixed, self.x1bf, self.aT, self.xT_bf
        tmp = self.tmp32
        tz = [self.view("R4", 16384 + i * 1024, [128, 512], BF16) for i in range(4)]
        tq = [self.view("R4", 16384 + 4096 + i * 1024, [128, 512], BF16) for i in range(4)]

        def PE(fn, reads=(), writes=(), sig=True):
            return S.op("pe", fn, reads, writes, sig)

        def ACT(fn, reads=(), writes=()):
            return S.op("act", fn, reads, writes)

        def DVE(fn, reads=(), writes=()):
            return S.op("dve", fn, reads, writes)

        def POOL(fn, reads=(), writes=()):
            return S.op("pool", fn, reads, writes)

        def xk(kc, half):
            return f"xres{kc}h{half}"

        if self.pre_xres:
            self.pre_xres = False
        else:
            for kc in range(KC):
                S.op("sp", I("dma_start", out=xres[:, kc, :], in_=xres_src[kc * 128:(kc + 1) * 128, :]),
                     writes=[xk(kc, 0), xk(kc, 1), "xres_ld"], dma="xres")
        bank = [0]

        def nb():
            b = bank[0] % 4
            bank[0] += 1
            return b

        wbase = 48
        for oc in range(16):
            w, wn = self.get_w(S, li, wbase + oc)
            for half in range(2):
                cols = slice(half * 512, (half + 1) * 512)
                b = nb()
                for kc in range(KC):
                    PE(I("matmul", ps[b][:, :], lhsT=w[:, kc, :], rhs=mixed[:, kc, cols], start=(kc == 0), stop=(kc == KC - 1)),
                       reads=[wn, "mixed"], writes=[f"ps{b}"], sig=(kc == KC - 1))
                DVE(I("scalar_tensor_tensor", out=xres[:, oc, cols], in0=xres[:, oc, cols], scalar=ALPHA, in1=ps[b][:, :],
                      op0=ALU.mult, op1=ALU.add), reads=[f"ps{b}", xk(oc, half), "xres_ld"], writes=[xk(oc, half)])

        wq_pref = []

        def next_w(t):
            if wq_pref and wq_pref[0][0] == t:
                return wq_pref.pop(0)[1]
            return self.load_w(S, li, t)

        def layer_norm(goff, boff, write_bf, prefetch=(), after_half=None):
            stat = {}
            for half in range(2):
                cols = slice(half * 512, (half + 1) * 512)
                b1, b2, bm, br = (4, 5, 6, 7) if half == 0 else (0, 1, 2, 3)
                for kc in range(KC):
                    z_ = tz[(half * 2 + kc) % 4]
                    q_ = tq[(half * 2 + kc) % 4]
                    zi, qi = (half * 2 + kc) % 4, (half * 2 + kc) % 4
                    DVE(I("tensor_copy", out=z_[:, :], in_=xres[:, kc, cols]), reads=[xk(kc, half)], writes=[f"tz{zi}"])
                    ACT(I("activation", out=q_[:, :], in_=xres[:, kc, cols], func=AF.Square), reads=[xk(kc, half)], writes=[f"tq{qi}"])
                    PE(I("matmul", ps[b1][:, :], lhsT=ones_b, rhs=z_[:, :], start=(kc == 0), stop=(kc == KC - 1)),
                       reads=[f"tz{zi}", "cmb"], writes=[f"ps{b1}"])
                    PE(I("matmul", ps[b2][:, :], lhsT=ones_b, rhs=q_[:, :], start=(kc == 0), stop=(kc == KC - 1)),
                       reads=[f"tq{qi}", "cmb"], writes=[f"ps{b2}"])
                if half == 0:
                    for t in prefetch:
                        wq_pref.append((t, self.load_w(S, li, t)))
                msq, lnv = tmp[half * 2], tmp[half * 2 + 1]
                mk, lk = f"tmp{half * 2}", f"tmp{half * 2 + 1}"
                ACT(I("activation", out=ps[bm][:, :], in_=ps[b1][:, :], func=AF.Identity, scale=1.0 / D), reads=[f"ps{b1}"], writes=[f"ps{bm}"])
                ACT(I("activation", out=msq[:, :], in_=ps[b1][:, :], func=AF.Square, scale=1.0 / D), reads=[f"ps{b1}"], writes=[mk])
                DVE(I("scalar_tensor_tensor", out=msq[:, :], in0=ps[b2][:, :], scalar=1.0 / D, in1=msq[:, :],
                      op0=ALU.mult, op1=ALU.subtract), reads=[f"ps{b2}", mk], writes=[mk])
                ACT(I("activation", out=lnv[:, :], in_=msq[:, :], func=AF.Ln, bias=1e-5, scale=1.0), reads=[mk], writes=[lk])
                ACT(I("activation", out=ps[br][:, :], in_=lnv[:, :], func=AF.Exp, scale=-0.5), reads=[lk], writes=[f"ps{br}"])
                stat[half] = (bm, br)
            for half in range(2):
                cols = slice(half * 512, (half + 1) * 512)
                bm, br = stat[half]
                for kc in range(KC):
                    ti = 4 + (kc % 4)
                    t = tmp[ti]
                    tn = f"tmp{ti}"
                    DVE(I("tensor_tensor", out=t[:, :], in0=xres[:, kc, cols], in1=ps[bm][:, :], op=ALU.subtract),
                        reads=[xk(kc, half), f"ps{bm}"], writes=[tn])
                    DVE(I("tensor_tensor", out=t[:, :], in0=t[:, :], in1=ps[br][:, :], op=ALU.mult), reads=[tn, f"ps{br}"], writes=[tn])
                    ACT(I("activation", out=xres[:, kc, cols], in_=t[:, :], func=AF.Identity,
                          bias=ppt[:, boff + kc:boff + kc + 1], scale=ppt[:, goff + kc:goff + kc + 1]),
                        reads=[tn, "ppt"], writes=[xk(kc, half)])
                    if write_bf:
                        ACT(I("activation", out=x1bf[:, kc, cols], in_=t[:, :], func=AF.Identity,
                              bias=ppt[:, boff + kc:boff + kc + 1], scale=ppt[:, goff + kc:goff + kc + 1]),
                            reads=[tn, "ppt"], writes=[f"x1bf{half}"])
                if after_half is not None:
                    after_half(half)

        layer_norm(0, 16, True, prefetch=[64 + i for i in range(NSLOT)])

        wbase = 64
        for qf in range(4):
            for fcl in range(16):
                w, wn = next_w(wbase + qf * 32 + fcl)
                for half in range(2):
                    cols = slice(half * 512, (half + 1) * 512)
                    b = nb()
                    for kc in range(KC):
                        PE(I("matmul", ps[b][:, :], lhsT=w[:, kc, :], rhs=x1bf[:, kc, cols], start=(kc == 0), stop=(kc == KC - 1)),
                           reads=[wn, f"x1bf{half}"], writes=[f"ps{b}"], sig=(kc == KC - 1))
                    t = tmp[4 + b]
                    ACT(I("activation", out=t[:, :], in_=ps[b][:, :], func=AF.Relu), reads=[f"ps{b}"], writes=[f"tmp{4 + b}"])
                    DVE(I("tensor_tensor", out=aT[:, fcl, cols], in0=ps[b][:, :], in1=t[:, :], op=ALU.mult),
                        reads=[f"ps{b}", f"tmp{4 + b}"], writes=[f"aT{fcl}h{half}"])
            for c in range(16):
                w, wn = next_w(wbase + qf * 32 + 16 + c)
                for half in range(2):
                    cols = slice(half * 512, (half + 1) * 512)
                    b = nb()
                    for fcl in range(16):
                        PE(I("matmul", ps[b][:, :], lhsT=w[:, fcl, :], rhs=aT[:, fcl, cols], start=(fcl == 0), stop=(fcl == 15)),
                           reads=[wn, f"aT{fcl}h{half}"], writes=[f"ps{b}"], sig=(fcl == 15))
                    DVE(I("scalar_tensor_tensor", out=xres[:, c, cols], in0=xres[:, c, cols], scalar=(ALPHA if qf == 0 else 1.0),
                          in1=ps[b][:, :], op0=ALU.mult, op1=ALU.add), reads=[f"ps{b}", xk(c, half)], writes=[xk(c, half)])
        if last:
            def out_half(half):
                cols = slice(half * 512, (half + 1) * 512)
                for kc in range(KC):
                    S.op("sp", I("dma_start", out=self.yT[kc * 128:(kc + 1) * 128, cols], in_=xres[:, kc, cols]),
                         reads=[xk(kc, half)], dma="yout")
            layer_norm(32, 48, False, after_half=out_half)
        else:
            def xchg_half(half):
                cols = slice(half * 512, (half + 1) * 512)
                for kc in range(KC):
                    S.op("sp", I("dma_start", out=self.xsave[kc * 128:(kc + 1) * 128, cols], in_=xres[:, kc, cols]),
                         reads=[xk(kc, half)], writes=["xsave"], dma="yout")
                S.op("sp", I("dma_start", out=self.xs[half].ap().rearrange("(a p) c -> p a c", p=128),
                             in_=x1bf[:, :, cols]), reads=[f"x1bf{half}"], writes=[f"xs{half}"], dma=f"xch{half}")
                S.op("pool", I("collective_compute", "AllGather", ALU.bypass,
                               replica_groups=[[0, 1], [2, 3], [4, 5], [6, 7]],
                               ins=[self.xs[half].ap().opt()], outs=[self.xg[half].ap().opt()]),
                     reads=[f"xs{half}"], writes=[f"xg{half}"])
            layer_norm(32, 48, True, after_half=xchg_half)
            for kc in range(KC):
                if kc % 2 == 0:
                    DVE(I("tensor_copy", out=xT_bf[:, kc, 0:1024], in_=x1bf[:, kc, :]), reads=["x1bf0", "x1bf1"],
                        writes=[xk(kc, 0), xk(kc, 1)])
                else:
                    ACT(I("copy", out=xT_bf[:, kc, 0:1024], in_=x1bf[:, kc, :]), reads=["x1bf0", "x1bf1"],
                        writes=[xk(kc, 0), xk(kc, 1)])
            fsel = cct[:, 352:353]
            gsel = cct[:, 353:354]
            stg = [[self.view(reg, i * 16384, [128, 16, 512], BF16).rearrange("p (a r) c -> p a r c", r=2) for i in range(2)]
                   for reg in ("R2", "R4")]
            tmpb = [self.view("R5", i * 1024, [128, 512], BF16) for i in range(4)]
            for r in range(2):
                for i in range(2):
                    for rk in range(2):
                        src = self.xg[i].ap().rearrange("(r a p) c -> r p a c", r=2, p=128)[rk][:, r * 8:(r + 1) * 8]
                        S.op("sp", I("dma_start", out=stg[r][i][:, :, rk, :], in_=src), reads=[f"xg{i}"],
                             writes=[f"stg{r}{i}"] + ([f"tz{j}" for j in range(4)] + [f"tq{j}" for j in range(4)] + [f"tmp{j}" for j in range(8)]
                                                      if r == 1 else [f"aT{j}h{h}" for j in range(16) for h in range(2)]),
                             dma=f"xch{2 + r}")
            cnt = 0
            for r in range(2):
                sA, sB = stg[r]
                for k8 in range(8):
                    kc = r * 8 + k8
                    for (s1, s0, c0) in ((sA, sB, 1024), (sB, sA, 1536)):
                        ti = cnt % 4
                        cnt += 1
                        ACT(I("activation", out=tmpb[ti][:, :], in_=s1[:, k8, 1, :], func=AF.Identity, scale=fsel),
                            reads=[f"stg{r}0", f"stg{r}1", "cct"], writes=[f"tmpb{ti}"] + (["x1bf0", "x1bf1"] if cnt <= 4 else []))
                        DVE(I("scalar_tensor_tensor", out=xT_bf[:, kc, c0:c0 + 512], in0=s0[:, k8, 0, :], scalar=gsel, in1=tmpb[ti][:, :],
                              op0=ALU.mult, op1=ALU.add), reads=[f"stg{r}0", f"stg{r}1", f"tmpb{ti}", "cct"], writes=[xk(kc, 0), xk(kc, 1)])
        if not last:
            for t in range(3):
                self.pre_w[(li + 1, t)] = self.load_w(S, li + 1, t)
        S.run()


def build_single_layer(L, debug=False, units=range(16), do_post=True):
    p = Prog([L], debug=debug)
    p.phase_clear()
    p.phase_setup()
    p.phase_attn(0, L, True, units=units, next_xres=(p.xT[:, 0:1024] if do_post else None))
    if do_post:
        p.phase_post(0, L, p.xT[:, 0:1024], True)
    return p.nc


def build_fused():
    p = Prog([0, 1])
    p.phase_clear()
    p.phase_setup()
    p.phase_attn(0, 0, True, next_xres=p.xT[:, 0:1024])
    p.phase_post(0, 0, p.xT[:, 0:1024], False)
    p.phase_attn(1, 1, False, next_xres=p.xsave)
    p.phase_post(1, 1, p.xsave, True)
    return p.nc


_CACHE = {}


def _run_layer(L, x, inp, debug=False):
    if ("nc", L, debug) not in _CACHE:
        _CACHE[("nc", L, debug)] = build_single_layer(L, debug)
    nc = _CACHE[("nc", L, debug)]
    W, wfz = _layer_weights(inp["w_in"][L], inp["w_out"][L], inp["w_mlp_in"][L], inp["w_mlp_out"][L])
    pp = _layer_params(L, inp)
    masks = _mask_tables()
    cmat = _const_mats()
    in_maps = []
    for c in range(8):
        b, half = divmod(c, 2)
        tp = _tokperm(half)
        cc, rtab = _core_consts(half)
        xT = np.ascontiguousarray(x[b][tp, :].T)
        in_maps.append({"xT": xT, "W0": W, "wfz0": wfz, "pp0": pp, "cc": cc, "rtab": rtab, "masks": masks, "cmat": cmat})
    res = run_bass_kernel_spmd(nc, in_maps, core_ids=list(range(8)))
    out = np.empty_like(x)
    dbg = None
    if debug:
        dbg = np.empty((4, 2048, 2048), np.float32)
    for c in range(8):
        b, half = divmod(c, 2)
        tp = _tokperm(half)
        out[b][tp[0:1024], :] = res.results[c]["yT"].T
        if debug:
            dbg[b][tp[0:1024], :] = res.results[c]["dbg"].T
    return (out, dbg) if debug else out


def kernel_unfused(**inputs):
    inp = {k: np.asarray(v) for k, v in inputs.items()}
    x = np.ascontiguousarray(inp["x"], dtype=np.float32)
    for L in range(DEPTH):
        x = _run_layer(L, x, inp)
    return x


def kernel(**inputs):
    inp = {k: np.asarray(v) for k, v in inputs.items()}
    x = np.ascontiguousarray(inp["x"], dtype=np.float32)
    if "fused" not in _CACHE:
        _CACHE["fused"] = build_fused()
    nc = _CACHE["fused"]
    shared = {"masks": _mask_tables(), "cmat": _const_mats()}
    for L in range(DEPTH):
        W, wfz = _layer_weights(inp["w_in"][L], inp["w_out"][L], inp["w_mlp_in"][L], inp["w_mlp_out"][L])
        shared[f"W{L}"] = W
        shared[f"wfz{L}"] = wfz
        shared[f"pp{L}"] = _layer_params(L, inp)
    in_maps = []
    for c in range(8):
        b, half = divmod(c, 2)
        tp = _tokperm(half)
        cc, rtab = _core_consts(half)
        m = dict(shared)
        m.update({"xT": np.ascontiguousarray(x[b][tp, :].T), "cc": cc, "rtab": rtab})
        in_maps.append(m)
    res = run_bass_kernel_spmd(nc, in_maps, core_ids=list(range(8)))
    out = np.empty_like(x)
    for c in range(8):
        b, half = divmod(c, 2)
        tp = _tokperm(half)
        out[b][tp[0:1024], :] = res.results[c]["yT"].T
    return out
```

```python
import numpy as np
import concourse.bass as bass
import concourse.mybir as mybir
from concourse.bass_utils import run_bass_kernel_spmd

F32 = mybir.dt.float32
BF16 = mybir.dt.bfloat16
AF = mybir.ActivationFunctionType
ALU = mybir.AluOpType
AX = mybir.AxisListType

D = 2048
SEQ = 2048
KC = 16
DEPTH = 2
NEG = -30000.0
ALPHA = (2 * DEPTH) ** 0.25
NWT = 192
NSLOT = 3
OVERLAP = True
GROUPS = ([0, 2, 1, 3], [1, 3, 2, 0])
N_MASK = 17
CC_COLS = 192 + 128 + 16 + 16 + 2
PP_COLS = 64 + 1 + 256 + 1


def I(name, *args, **kw):
    return (name, args, kw)


class Tok:
    __slots__ = ("sem", "val", "eng")

    def __init__(self, sem, val, eng):
        self.sem, self.val, self.eng = sem, val, eng


class Sched:
    COMPUTE = ("pe", "act", "dve", "pool")

    def __init__(self, nc, sems, dsems):
        self.nc = nc
        self.sem = sems
        self.cnt = sems["_cnt"]
        self.dsem = dsems
        self.dcnt = dsems["_cnt"]
        self.pending = {e: [] for e in self.COMPUTE}
        self.streams = {e: [] for e in ("pe", "act", "dve", "pool", "sp")}
        self.res = {}
        self.dma_toks = []

    def _r(self, key):
        r = self.res.get(key)
        if r is None:
            r = self.res[key] = [None, []]
        return r

    def op(self, eng, fn, reads=(), writes=(), sig=True, dma=None):
        assert sig or eng == "pe"
        deps = []
        for k in reads:
            r = self._r(k)
            if r[0] is not None:
                deps.append(r[0])
        for k in writes:
            r = self._r(k)
            if r[0] is not None:
                deps.append(r[0])
            deps.extend(r[1])
        if dma is None and eng == "pe":
            deps = [d for d in deps if d.eng != "pe"]
        if dma is not None:
            assert dma in self.dsem, dma
            self.dcnt[dma] += 16
            tok = Tok(self.dsem[dma], self.dcnt[dma], "dma")
            inc = (self.dsem[dma], 16)
            self.dma_toks.append(tok)
        else:
            tok = Tok(self.sem[eng], None, eng)
            self.pending[eng].append(tok)
            inc = None
            if sig:
                self.cnt[eng] += 1
                for t in self.pending[eng]:
                    t.val = self.cnt[eng]
                self.pending[eng] = []
                inc = (self.sem[eng], 1)
        self.streams[eng].append((deps, fn, inc))
        for k in reads:
            self._r(k)[1].append(tok)
        for k in writes:
            r = self._r(k)
            r[0] = tok
            r[1] = []
        return tok

    def run(self):
        for e in self.COMPUTE:
            assert not self.pending[e], f"unsignalled tail on {e}"
        self.streams["sp"].append((list(self.dma_toks), None, None))
        with self.nc.Block() as block:
            binder = {"pe": block.tensor, "act": block.scalar, "dve": block.vector,
                      "pool": block.gpsimd, "sp": block.sync}
            for e, stream in self.streams.items():
                if not stream:
                    continue

                def body(engine, stream=stream):
                    waited = {}
                    for deps, fn, inc in stream:
                        need = {}
                        for d in deps:
                            assert d.val is not None
                            key = id(d.sem)
                            if waited.get(key, 0) >= d.val:
                                continue
                            if key not in need or need[key][1] < d.val:
                                need[key] = (d.sem, d.val)
                        for key, (s, v) in need.items():
                            engine.wait_ge(s, v)
                            waited[key] = v
                        if fn is None:
                            continue
                        inst = getattr(engine, fn[0])(*fn[1], **fn[2])
                        if inc is not None:
                            inst.then_inc(inc[0], inc[1])

                binder[e](body)


def _slopes():
    n = 12
    return np.exp2(-8.0 * np.arange(1, n + 1, dtype=np.float64) / n)


def _tokperm(half):
    return np.concatenate([np.arange(512) + 512 * g for g in GROUPS[half]])


def _mask_tables():
    sl = np.arange(128)[:, None]
    tl = np.arange(512)[None, :]
    m = np.zeros((N_MASK, 128, 512), np.float32)
    for i in range(4):
        m[i] = np.where(128 * i + sl <= tl, 0.0, NEG)
        m[4 + i] = np.where(128 * i + sl < tl, 0.0, NEG)

    def dil(rel):
        d = 128 * rel + tl - sl
        mult = ((d >= 0) & (d <= 128)).astype(np.int64) + ((d >= 0) & (d % 4 == 0) & (d <= 512)) \
            + ((d >= 0) & (d % 16 == 0) & (d <= 2048))
        return np.where(mult > 0, np.log(np.maximum(mult, 1)), NEG).astype(np.float32)

    for i in range(4):
        m[8 + i] = dil(-i)
    for r in range(1, 5):
        m[12 + (4 - r)] = dil(r)
    m[16] = dil(8)
    return m


def _const_mats():
    c = np.zeros((4, 128, 128), np.float32)
    c[0] = np.eye(128)
    c[1] = -np.eye(128)
    kk = np.arange(128)[:, None]
    mm = np.arange(128)[None, :]
    c[2] = (kk >= mm).astype(np.float32)
    c[3] = 1.0
    return c


def _core_consts(half):
    tp = _tokperm(half)
    s_nat = tp.reshape(16, 128).T.astype(np.float64)
    pad = np.zeros(16)
    if half == 0:
        pad[12:16] = NEG
    sl = _slopes()
    cc = np.zeros((128, CC_COLS), np.float32)
    biasK = s_nat[:, :, None] * sl[None, None, :] + pad[None, :, None]
    cc[:, 0:192] = biasK.reshape(128, 192)
    cc[:, 192:320] = np.repeat(pad[None, :, None], 8, axis=2).repeat(128, axis=0).reshape(128, 128)
    cc[:, 320:336] = pad[None, :]
    nat = GROUPS[half]
    m = np.zeros((4, 4))
    for g in range(4):
        for g2 in range(4):
            m[g, g2] = 1.0 if nat[g2] < nat[g] else 0.0
    cc[:, 336:352] = m.reshape(1, 16)
    cc[:, 352] = 1.0 if half == 0 else 0.0
    cc[:, 353] = 0.0 if half == 0 else 1.0
    rtab = (-sl[:, None] * tp[None, 0:1024].astype(np.float64)).astype(np.float32)
    return cc, rtab


IN_OFF = {"fq": 0, "fk": 512, "fv": 1024, "fz": 1536, "sq": 1544, "sk": 2056, "sv": 2568,
          "dq": 3080, "dk": 3592, "dv": 4104, "gq": 4616, "gk": 5128, "gv": 5640}
MIX = (("fq", "fk", "fv"), ("sq", "sk", "sv"), ("dq", "dk", "dv"), ("gq", "gk", "gv"))


def _wtile(w_cols):
    return np.ascontiguousarray(w_cols.reshape(16, 128, 128).transpose(1, 0, 2).reshape(128, 2048))


def _layer_weights(w_in, w_out, w1, w2):
    W = np.empty((NWT, 128, 2048), np.float32)
    t = 0
    for u in range(16):
        m, i = divmod(u, 4)
        for nm in MIX[m]:
            c0 = IN_OFF[nm] + i * 128
            W[t] = _wtile(w_in[:, c0:c0 + 128])
            t += 1
    for oc in range(16):
        W[t] = _wtile(w_out[:, oc * 128:(oc + 1) * 128])
        t += 1
    for qf in range(4):
        for fcl in range(16):
            fc = qf * 16 + fcl
            W[t] = _wtile(w1[:, fc * 128:(fc + 1) * 128])
            t += 1
        for c in range(16):
            W[t] = _wtile(w2[qf * 2048:(qf + 1) * 2048, c * 128:(c + 1) * 128])
            t += 1
    assert t == NWT
    wfz = np.ascontiguousarray(
        w_in[:, 1536:1544].reshape(16, 128, 8).transpose(1, 0, 2).reshape(128, 128))
    return W, wfz


def _layer_params(l, inp):
    pp = np.zeros((128, PP_COLS), np.float32)
    pp[:, 0:16] = inp["ln1_gain"][l].reshape(16, 128).T
    pp[:, 16:32] = inp["ln1_bias"][l].reshape(16, 128).T
    pp[:, 32:48] = inp["ln2_gain"][l].reshape(16, 128).T
    pp[:, 48:64] = inp["ln2_bias"][l].reshape(16, 128).T
    pp[:, 64] = inp["diff_subln_gain"][l]
    pp[:, 65:129] = inp["diff_lambda_q1"][l][None, :]
    pp[:, 129:193] = inp["diff_lambda_k1"][l][None, :]
    pp[:, 193:257] = inp["diff_lambda_q2"][l][None, :]
    pp[:, 257:321] = inp["diff_lambda_k2"][l][None, :]
    pp[0:8, 321] = inp["fox_forget_bias"][l]
    return pp


class Prog:
    def __init__(self, layers, debug=False):
        self.layers = layers
        self.debug = debug
        nc = self.nc = bass.Bass("TRN2", target_bir_lowering=False)
        nL = len(layers)
        self.xT = nc.dram_tensor("xT", [D, SEQ], F32, kind="ExternalInput").ap()
        self.W = [nc.dram_tensor(f"W{i}", [NWT, 128, 2048], F32, kind="ExternalInput").ap() for i in range(nL)]
        self.wfz = [nc.dram_tensor(f"wfz{i}", [128, 128], F32, kind="ExternalInput").ap() for i in range(nL)]
        self.pp = [nc.dram_tensor(f"pp{i}", [128, PP_COLS], F32, kind="ExternalInput").ap() for i in range(nL)]
        self.cc = nc.dram_tensor("cc", [128, CC_COLS], F32, kind="ExternalInput").ap()
        self.rtab = nc.dram_tensor("rtab", [12, 1024], F32, kind="ExternalInput").ap()
        self.masks = nc.dram_tensor("masks", [N_MASK, 128, 512], F32, kind="ExternalInput").ap()
        self.cmat = nc.dram_tensor("cmat", [4, 128, 128], F32, kind="ExternalInput").ap()
        self.yT = nc.dram_tensor("yT", [D, 1024], F32, kind="ExternalOutput").ap()
        if debug:
            self.dbg = nc.dram_tensor("dbg", [D, 1024], F32, kind="ExternalOutput").ap()
        if nL > 1:
            self.xsave = nc.dram_tensor("xsave", [D, 1024], F32).ap()
            self.xs = [nc.dram_tensor(f"xs{i}", [D, 512], BF16) for i in range(2)]
            self.xg = [nc.dram_tensor(f"xg{i}", [2 * D, 512], BF16) for i in range(2)]
        self.sems = {e: nc.alloc_semaphore(f"sem_{e}") for e in Sched.COMPUTE}
        self.sems["_cnt"] = {e: 0 for e in Sched.COMPUTE}
        self.dsems = {"_cnt": {}}
        for nm in ["c0", "c1", "xt", "xres", "yout", "dbg0", "dbg1", "qrow00", "qrow01", "qrow10", "qrow11", "qrowp00", "qrowp01", "qrowp10", "qrowp11", "xch0", "xch1", "xch2", "xch3"] + \
                [f"wr{i}" for i in range(NSLOT)]:
            self.dsems[nm] = nc.alloc_semaphore(f"dsem_{nm}")
            self.dsems["_cnt"][nm] = 0
        self.pst = nc.alloc_psum_tensor("pst", [128, 8, 512], F32)
        self.ps = [self.pst[:, i, :] for i in range(8)]
        R = self.R = {}
        R["R1"] = nc.alloc_sbuf_tensor("R1", [128, 16384], F32)
        R["R2"] = nc.alloc_sbuf_tensor("R2", [128, 8192], F32)
        R["R4"] = nc.alloc_sbuf_tensor("R4", [128, 8192], F32)
        R["R5"] = nc.alloc_sbuf_tensor("R5", [128, 8192], F32)
        self.wr = [nc.alloc_sbuf_tensor(f"wr{i}", [128, 16, 128], BF16) for i in range(NSLOT)]
        self.maskt = nc.alloc_sbuf_tensor("maskt", [128, N_MASK, 512], BF16)
        self.cmb = nc.alloc_sbuf_tensor("cmb", [128, 4, 128], BF16)
        self.cm32 = nc.alloc_sbuf_tensor("cm32", [128, 4, 128], F32)
        self.cct = nc.alloc_sbuf_tensor("cct", [128, CC_COLS], F32)
        self.ppt = [nc.alloc_sbuf_tensor(f"ppt{i}", [128, PP_COLS], F32) for i in range(nL)]
        self.gT = nc.alloc_sbuf_tensor("gT", [128, 16, 8], F32)
        self.small = nc.alloc_sbuf_tensor("small", [128, 160], F32)

        def view(reg, boff, shape, dt):
            n = int(np.prod(shape[1:]))
            esz = 4 if dt == F32 else 2
            assert boff % 4 == 0 and (n * esz) % 4 == 0
            ap = R[reg][:, boff // 4: boff // 4 + (n * esz) // 4]
            if dt != F32:
                ap = ap.bitcast(dt)
            if len(shape) == 3:
                ap = ap.rearrange("p (a b) -> p a b", a=shape[1])
            return ap

        self.view = view
        self.xT_bf = view("R1", 0, [128, 16, 2048], BF16)
        self.xres = view("R1", 0, [128, 16, 1024], F32)
        self.mixed = view("R2", 0, [128, 16, 1024], BF16)
        self.aT = view("R2", 0, [128, 16, 1024], BF16)
        o = 0
        KT0 = [view("R4", o + i * 4096, [128, 2048], BF16) for i in range(2)]
        o += 8192
        QT0 = [view("R4", o + i * 2048, [128, 1024], BF16) for i in range(2)]
        o += 4096
        nQT0 = [view("R4", o + i * 2048, [128, 1024], BF16) for i in range(2)]
        o += 4096
        self.VT = view("R4", o, [128, 2048], BF16)
        o += 4096
        Vtok0 = view("R4", o, [128, 16, 256], BF16)
        o += 8192
        self.E = [view("R4", o + i * 1024, [128, 512], BF16) for i in range(4)]
        self.Et = view("R4", o, [128, 4, 512], BF16)
        o += 4096
        assert o == 32768
        self.tmp32 = [view("R4", i * 2048, [128, 512], F32) for i in range(8)]
        self.fz = view("R5", 0, [128, 2048], F32)
        self.grow = view("R5", 8192, [128, 2048], F32)
        self.ones8 = view("R2", 0, [128, 512], F32)
        self.spt = view("R5", 0, [128, 4, 512], BF16).rearrange("p (a h) c -> p a h c", a=2)
        nQT1 = [view("R5", 4096 + i * 2048, [128, 1024], BF16) for i in range(2)]
        KT1 = [view("R5", 8192 + i * 4096, [128, 2048], BF16) for i in range(2)]
        self.rneg = view("R5", 16384, [128, 1024], BF16)
        self.SS32 = [view("R5", 18432 + i * 2048, [128, 512], F32) for i in range(2)]
        self.SSt = view("R5", 22528, [128, 4, 512], BF16).rearrange("p (a h) c -> p a h c", a=2)
        qt1b = nc.alloc_sbuf_tensor("qt1b", [128, 1024], BF16)
        QT1 = [view("R5", 26624, [128, 1024], BF16), qt1b[:, :]]
        self.rz = [view("R5", 28672 + i * 2048, [128, 512], F32) for i in range(2)]
        vtok1 = nc.alloc_sbuf_tensor("vtok1", [128, 16, 256], BF16)
        self.KT = [KT0, KT1]
        self.QT = [QT0, QT1]
        self.nQT = [nQT0, nQT1]
        self.Vtok = [Vtok0, vtok1[:, :, :]]
        self.x1bf = view("R5", 0, [128, 16, 1024], BF16)
        self.wt = 0
        self.pre_w = {}
        self.pre_xres = False

    def new_sched(self):
        return Sched(self.nc, self.sems, self.dsems)

    def get_w(self, S, L, t):
        if (L, t) in self.pre_w:
            return self.pre_w.pop((L, t))
        return self.load_w(S, L, t)

    def load_w(self, S, L, t):
        slot = self.wt % NSLOT
        self.wt += 1
        dst = self.wr[slot]
        src = self.W[L][t].rearrange("p (a b) -> p a b", a=16)
        S.op("pool", I("dma_start", out=dst[:], in_=src), writes=[f"wr{slot}"], dma=f"wr{slot}")
        return dst, f"wr{slot}"

    def phase_clear(self):
        sems = [self.sems[e] for e in Sched.COMPUTE] + [v for k, v in self.dsems.items() if k != "_cnt"]
        with self.nc.Block() as block:
            def body(engine):
                for s_ in sems:
                    engine.sem_clear(s_)
            block.gpsimd(body)

    def phase_setup(self):
        S = self.new_sched()
        S.op("pool", I("dma_start", out=self.maskt[:], in_=self.masks.rearrange("m p t -> p m t")),
             writes=["maskt"], dma="c0")
        S.op("pool", I("dma_start", out=self.cmb[:], in_=self.cmat.rearrange("m p t -> p m t")),
             writes=["cmb"], dma="c0")
        S.op("sp", I("dma_start", out=self.cm32[:], in_=self.cmat.rearrange("m p t -> p m t")),
             writes=["cm32"], dma="c1")
        S.op("sp", I("dma_start", out=self.cct[:], in_=self.cc), writes=["cct"], dma="c1")
        for i in range(len(self.layers)):
            S.op("sp", I("dma_start", out=self.ppt[i][:], in_=self.pp[i]), writes=["ppt"], dma="c1")
        S.run()

    def phase_attn(self, li, L, first, units=range(16), next_xres=None):
        nc = self.nc
        S = self.new_sched()
        ps = self.ps
        ident = self.cmb[:, 0, :]
        negident = self.cmb[:, 1, :]
        utri = self.cmb[:, 2, :]
        ones_b = self.cmb[:, 3, :]
        ident32 = self.cm32[:, 0, :]
        ones32 = self.cm32[:, 3, :]
        cct, ppt = self.cct, self.ppt[li]
        KT, QT, nQT, VT, Vtok, E = self.KT, self.QT, self.nQT, self.VT, self.Vtok, self.E
        xT_bf = self.xT_bf
        sm = self.small
        lam_init = 0.8 - 0.6 * float(np.exp(-0.3 * L))

        def PE(fn, reads=(), writes=(), sig=True):
            return S.op("pe", fn, reads, writes, sig)

        def ACT(fn, reads=(), writes=()):
            return S.op("act", fn, reads, writes)

        def DVE(fn, reads=(), writes=()):
            return S.op("dve", fn, reads, writes)

        if first:
            for kc in range(KC):
                S.op("pool", I("dma_start", out=xT_bf[:, kc, :], in_=self.xT[kc * 128:(kc + 1) * 128, :]),
                     writes=["xT_bf"], dma="xt")
        for bb in range(2):
            for i in range(2):
                DVE(I("memset", KT[bb][i][64:65, :], 1.0), writes=[f"KT{bb}{i}"])
            DVE(I("memset", Vtok[bb][:, :, :], 1.0), writes=[f"Vtok{bb}"])

        DVE(I("tensor_tensor", out=sm[:, 64:128], in0=ppt[:, 65:129], in1=ppt[:, 129:193], op=ALU.mult),
            reads=["ppt"], writes=["sm_a"])
        DVE(I("tensor_reduce", out=sm[:, 1:2], in_=sm[:, 64:128], axis=AX.X, op=ALU.add), reads=["sm_a"], writes=["sm_b"])
        DVE(I("tensor_tensor", out=sm[:, 64:128], in0=ppt[:, 193:257], in1=ppt[:, 257:321], op=ALU.mult),
            reads=["ppt", "sm_b"], writes=["sm_a"])
        DVE(I("tensor_reduce", out=sm[:, 2:3], in_=sm[:, 64:128], axis=AX.X, op=ALU.add), reads=["sm_a"], writes=["sm_c"])
        ACT(I("activation", out=sm[:, 4:5], in_=sm[:, 1:2], func=AF.Exp), reads=["sm_b"], writes=["sm_d"])
        ACT(I("activation", out=sm[:, 5:6], in_=sm[:, 2:3], func=AF.Exp), reads=["sm_c"], writes=["sm_e"])
        DVE(I("scalar_tensor_tensor", out=sm[:, 6:7], in0=sm[:, 5:6], scalar=-lam_init, in1=sm[:, 4:5],
                                             op0=ALU.add, op1=ALU.subtract), reads=["sm_d", "sm_e"], writes=["neglam"])
        DVE(I("tensor_scalar", out=sm[:, 7:8], in0=ppt[:, 64:65], scalar1=(1.0 - lam_init), scalar2=None, op0=ALU.mult),
            reads=["ppt"], writes=["subs"])
        DVE(I("tensor_scalar", out=sm[:, 8:9], in0=ppt[:, 321:322], scalar1=-1.0, scalar2=None, op0=ALU.mult),
            reads=["ppt"], writes=["negbf"])
        neglam = sm[:, 6:7]
        subs = sm[:, 7:8]
        negbf = sm[0:8, 8:9]

        def evac(out, in_, reads, writes, scale=None, shifted=False, allow_act=True):
            use_act = allow_act and (not shifted) and (evac_toggle[0] % 2 == 1)
            evac_toggle[0] += 1
            if use_act:
                if scale is None:
                    ACT(I("copy", out=out, in_=in_), reads, writes)
                else:
                    ACT(I("activation", out=out, in_=in_, func=AF.Identity, scale=scale), reads, writes)
            else:
                if scale is None:
                    DVE(I("tensor_copy", out=out, in_=in_), reads, writes)
                else:
                    DVE(I("tensor_scalar", out=out, in0=in_, scalar1=scale, scalar2=None, op0=ALU.mult), reads, writes)

        evac_toggle = [0]

        wfz_t = self.view("R4", 0, [128, 16, 8], BF16)
        S.op("pool", I("dma_start", out=wfz_t, in_=self.wfz[li].rearrange("p (a b) -> p a b", a=16)),
             writes=["KT00"], dma="c0")
        fz, grow, rneg = self.fz, self.grow, self.rneg
        for tc in range(4):
            b = 6 + (tc % 2)
            for kc in range(KC):
                PE(I("matmul", ps[b][0:8, :], lhsT=wfz_t[:, kc, :],
                                                               rhs=xT_bf[:, kc, tc * 512:(tc + 1) * 512],
                                                               start=(kc == 0), stop=(kc == KC - 1)),
                   reads=["KT00", "xT_bf"], writes=[f"ps{b}"], sig=(kc == KC - 1))
            ACT(I("activation", out=fz[0:8, tc * 512:(tc + 1) * 512], in_=ps[b][0:8, :], func=AF.Exp,
                                                   bias=negbf, scale=-1.0), reads=[f"ps{b}", "negbf"], writes=["fz"])
        ACT(I("activation", out=fz[0:8, :], in_=fz[0:8, :], func=AF.Ln, bias=1.0, scale=1.0), reads=["fz"], writes=["fz"])
        DVE(I("memset", KT[0][0][64:65, :], 1.0), reads=[], writes=["KT00"])
        DVE(I("memset", self.ones8[0:8, :], 1.0), writes=["ones8"])
        for g in range(4):
            DVE(I("tensor_tensor_scan", out=grow[0:8, g * 512:(g + 1) * 512], data0=self.ones8[0:8, :],
                                                    data1=fz[0:8, g * 512:(g + 1) * 512], initial=0.0,
                                                    op0=ALU.mult, op1=ALU.add), reads=["fz", "ones8"], writes=["grow"])
        off = sm[0:8, 16:20]
        DVE(I("memset", sm[0:8, 16:20], 0.0), writes=["off"])
        for g in range(4):
            for g2 in range(4):
                DVE(I("scalar_tensor_tensor",
                    out=sm[0:8, 16 + g:17 + g], in0=grow[0:8, g2 * 512 + 511:g2 * 512 + 512],
                    scalar=cct[0:8, 336 + g * 4 + g2:337 + g * 4 + g2], in1=sm[0:8, 16 + g:17 + g],
                    op0=ALU.mult, op1=ALU.add), reads=["grow", "cct", "off"], writes=["off"])
        for g in range(4):
            DVE(I("tensor_scalar", out=grow[0:8, g * 512:(g + 1) * 512], in0=grow[0:8, g * 512:(g + 1) * 512],
                                               scalar1=sm[0:8, 16 + g:17 + g], scalar2=None, op0=ALU.add),
                reads=["off", "grow"], writes=["grow"])
        DVE(I("tensor_scalar", out=rneg[0:8, :], in0=grow[0:8, 0:1024], scalar1=-1.0, scalar2=None, op0=ALU.mult),
            reads=["grow"], writes=["rneg"])
        def emit_gT():
            for blk in range(16):
                PE(I("transpose", out=ps[6][:, blk * 8:(blk + 1) * 8], in_=grow[0:8, blk * 128:(blk + 1) * 128],
                     identity=ident32[0:8, 0:8]), reads=["grow", "cm32"], writes=["ps6"])
            DVE(I("tensor_tensor", out=self.gT[:, :, :].rearrange("p a b -> p (a b)"), in0=ps[6][:, 0:128],
                  in1=cct[:, 192:320], op=ALU.add), reads=["ps6", "cct"], writes=["gT"])

        deferred = []

        def flush_deferred(bank):
            while deferred:
                deferred.pop(0)(bank)

        def finish64(O, Oname, rzi, dst, dname):
            rz = self.rz[rzi]
            zc = self.SS32[rzi]
            ACT(I("copy", out=rz[0:64, :], in_=O[0:64, :]), reads=[Oname], writes=[f"rz{rzi}"])
            DVE(I("tensor_copy", out=zc[0:64, :], in_=O[64:128, :]), reads=[Oname], writes=[f"SS32{rzi}"])
            DVE(I("reciprocal", out=zc[0:64, :], in_=zc[0:64, :]), reads=[f"SS32{rzi}"], writes=[f"SS32{rzi}"])
            DVE(I("tensor_tensor", out=dst, in0=rz[0:64, :], in1=zc[0:64, :], op=ALU.mult),
                reads=[f"rz{rzi}", f"SS32{rzi}"], writes=[dname])

        def softmax_chains(chains, klist, lookahead=True):
            n = len(klist)

            def emitS(c, j):
                pos, mask = klist[j]
                sb = c["S"][j % 2]
                kk = c["K"]
                PE(I("matmul", ps[sb][:, :], lhsT=c["KT"][0:kk, pos * 128:(pos + 1) * 128], rhs=c["QT"][0:kk, c["qc"]],
                     start=True, stop=(mask is None)),
                   reads=[c["KTn"], c["QTn"]], writes=[f"ps{sb}"], sig=(mask is None))
                if mask is not None:
                    PE(I("matmul", ps[sb][:, :], lhsT=ident, rhs=self.maskt[:, mask, :], start=False, stop=True),
                       reads=["cmb", "maskt"], writes=[f"ps{sb}"])

            def emitExp(c, j):
                pos = klist[j][0]
                sb = c["S"][j % 2]
                et = c["E"][j % 2]
                ACT(I("activation", out=E[et][:, :], in_=ps[sb][:, :], func=AF.Exp, bias=c["bias"](pos), scale=1.0),
                    reads=[f"ps{sb}", "gT", "cct"], writes=[f"E{et}"])

            def emitPV(c, j):
                pos = klist[j][0]
                et = c["E"][j % 2]
                PE(I("matmul", ps[c["O"]][:, :], lhsT=c["vl"](pos), rhs=E[et][:, :], start=(j == 0), stop=(j == n - 1)),
                   reads=[f"E{et}", c["Vn"]], writes=[f"ps{c['O']}"], sig=("Z" not in c))
                if "Z" in c:
                    PE(I("matmul", ps[c["Z"]][:, :], lhsT=ones_b, rhs=E[et][:, :], start=(j == 0), stop=(j == n - 1)),
                       reads=[f"E{et}", "cmb"], writes=[f"ps{c['Z']}"])

            if lookahead:
                for c in chains:
                    emitS(c, 0)
                for j in range(n):
                    for c in chains:
                        if j + 1 < n:
                            emitS(c, j + 1)
                    for c in chains:
                        emitExp(c, j)
                    for c in chains:
                        emitPV(c, j)
                    if j == min(2, n - 1):
                        flush_deferred(chains[0]["S"][j % 2])
                    yield
            else:
                for j in range(n):
                    for c in chains:
                        emitS(c, j)
                    for c in chains:
                        emitExp(c, j)
                    yield
                    for c in chains:
                        emitPV(c, j)
                    if j == min(2, n - 1):
                        flush_deferred(chains[0]["S"][0])

        KL_A = [(0, 0), (1, 1), (2, 2), (3, 3), (12, None), (13, None), (14, None), (15, None)]
        KL_B = [(4, 0), (5, 1), (6, 2), (7, 3)] + [(p, None) for p in (8, 9, 10, 11, 0, 1, 2, 3, 12, 13, 14, 15)]
        KL_A_D = [(0, 8), (1, 9), (2, 10), (3, 11), (12, 12), (13, 13), (14, 14), (15, 15)]
        KL_B_D = [(4, 8), (5, 9), (6, 10), (7, 11), (8, 12), (9, 13), (10, 14), (11, 15)] + \
                 [(p, 16) for p in (0, 1, 2, 3, 12, 13, 14, 15)]
        KL_A_S = [(3, 7), (2, 6), (1, 5), (0, 4), (15, None), (14, None), (13, None), (12, None)]
        KL_B_S = [(7, 7), (6, 6), (5, 5), (4, 4)] + [(p, None) for p in (11, 10, 9, 8, 3, 2, 1, 0, 15, 14, 13, 12)]
        KIND = ("fox", "sb", "diff", "dil")
        vdirty = [False, False]

        def proj_gen(u, buf, banks, overlapped):
            kind = KIND[u // 4]
            i = u % 4
            KTb, QTb, nQTb, Vt = KT[buf], QT[buf], nQT[buf], Vtok[buf]
            kq = [f"KT{buf}0", f"KT{buf}1"]
            qq = [f"QT{buf}0", f"QT{buf}1"]
            nq = [f"nQT{buf}0", f"nQT{buf}1"]
            vn = f"Vtok{buf}"
            wq, wqn = self.get_w(S, li, 3 * u + 0)
            wk, wkn = self.get_w(S, li, 3 * u + 1)
            wv, wvn = self.get_w(S, li, 3 * u + 2)
            aa = not overlapped
            bsel = [0]

            def nextbank():
                b = banks[bsel[0] % 2]
                bsel[0] += 1
                return b

            if kind != "diff" and vdirty[buf]:
                DVE(I("memset", Vt[:, :, :], 1.0), writes=[vn])
                vdirty[buf] = False
            if kind == "diff":
                vdirty[buf] = True

            def group(w, wn, tc, b):
                for kc in range(KC):
                    PE(I("matmul", ps[b][:, :], lhsT=w[:, kc, :], rhs=xT_bf[:, kc, tc * 512:(tc + 1) * 512],
                         start=(kc == 0), stop=(kc == KC - 1)),
                       reads=[wn, "xT_bf"], writes=[f"ps{b}"], sig=(kc == KC - 1))
                    if kc % 4 == 3:
                        yield

            for tc in range(2):
                b = nextbank()
                yield from group(wq, wqn, tc, b)
                p, pn = ps[b], f"ps{b}"
                cols = slice(tc * 512, (tc + 1) * 512)
                evac(QTb[0][0:64, cols], p[0:64, :], [pn], [qq[0]], scale=0.125, allow_act=aa)
                evac(QTb[1][0:64, cols], p[64:128, :], [pn], [qq[1]], scale=0.125, shifted=True)
                if kind == "sb":
                    evac(nQTb[0][0:64, cols], p[0:64, :], [pn], [nq[0]], scale=-0.125, allow_act=aa)
                    evac(nQTb[1][0:64, cols], p[64:128, :], [pn], [nq[1]], scale=-0.125, shifted=True)
            for hh in range(2):
                if kind == "fox":
                    h = 2 * i + hh
                    S.op("sp", I("dma_start", out=QTb[hh][64:65, :], in_=rneg[h:h + 1, :]), reads=["rneg"], writes=[qq[hh]],
                         dma=f"qrow{buf}{hh}")
                elif kind == "sb":
                    DVE(I("memset", QTb[hh][64:65, :], 0.0), writes=[qq[hh]])
                    DVE(I("memset", nQTb[hh][64:65, :], 0.0), writes=[nq[hh]])
                elif kind in ("diff", "dil"):
                    h = i if kind == "diff" else 4 + 2 * i + hh
                    S.op("pool", I("dma_start", out=QTb[hh][64:65, :], in_=self.rtab[h:h + 1, :]), writes=[qq[hh]],
                         dma=f"qrowp{buf}{hh}")
            if not overlapped:
                flush_deferred(banks[0])
            for tc in range(4):
                b = nextbank()
                yield from group(wk, wkn, tc, b)
                p, pn = ps[b], f"ps{b}"
                cols = slice(tc * 512, (tc + 1) * 512)
                evac(KTb[0][0:64, cols], p[0:64, :], [pn], [kq[0]], allow_act=aa)
                evac(KTb[1][0:64, cols], p[64:128, :], [pn], [kq[1]], shifted=True)
            for tc in range(4):
                b = nextbank()
                yield from group(wv, wvn, tc, b)
                p, pn = ps[b], f"ps{b}"
                cols = slice(tc * 512, (tc + 1) * 512)
                evac(VT[:, cols], p[:, :], [pn], ["VT"], allow_act=aa)
            for tg in range(4):
                b = nextbank()
                pbf = ps[b][:, :].bitcast(BF16)
                for t4 in range(4):
                    tb = tg * 4 + t4
                    PE(I("transpose", out=pbf[:, t4 * 128:(t4 + 1) * 128], in_=VT[:, tb * 128:(tb + 1) * 128], identity=ident),
                       reads=["VT", "cmb"], writes=[f"ps{b}"], sig=(t4 == 3))
                if kind == "diff":
                    DVE(I("tensor_copy", out=Vt[:, tg * 4:(tg + 1) * 4, 0:128],
                          in_=pbf[:, 0:512].rearrange("p (a b) -> p a b", a=4)), reads=[f"ps{b}"], writes=[vn])
                else:
                    DVE(I("tensor_copy", out=Vt[:, tg * 4:(tg + 1) * 4, :].rearrange("p a (h c) -> p a h c", h=2)[:, :, :, 0:64],
                          in_=pbf[:, 0:512].rearrange("p (a h c) -> p a h c", a=4, h=2)), reads=[f"ps{b}"], writes=[vn])
                yield

        def chain_gen(u, buf):
            kind = KIND[u // 4]
            i = u % 4
            KTb, QTb, nQTb, Vt = KT[buf], QT[buf], nQT[buf], Vtok[buf]
            kq = [f"KT{buf}0", f"KT{buf}1"]
            qq = [f"QT{buf}0", f"QT{buf}1"]
            nq = [f"nQT{buf}0", f"nQT{buf}1"]
            vn = f"Vtok{buf}"
            mixu = self.mixed[:, u, :]
            if kind in ("fox", "dil"):
                for slot in range(2):
                    chains = []
                    for hh in range(2):
                        if kind == "fox":
                            h = 2 * i + hh
                            bias = (lambda pos, h=h: self.gT[:, pos, h:h + 1])
                        else:
                            h = 4 + 2 * i + hh
                            bias = (lambda pos, h=h: cct[:, pos * 12 + h:pos * 12 + h + 1])
                        chains.append(dict(KT=KTb[hh], KTn=kq[hh], QT=QTb[hh], QTn=qq[hh], K=65, Vn=vn,
                                           qc=slice(slot * 512, (slot + 1) * 512), S=(2 * hh, 2 * hh + 1),
                                           E=(2 * hh, 2 * hh + 1), O=4 + hh, bias=bias,
                                           vl=(lambda pos, hh=hh: Vt[:, pos, hh * 128:(hh + 1) * 128])))
                    if kind == "fox":
                        kl = KL_A if slot == 0 else KL_B
                    else:
                        kl = KL_A_D if slot == 0 else KL_B_D
                    yield from softmax_chains(chains, kl)
                    for hh in range(2):
                        dst = mixu[hh * 64:(hh + 1) * 64, slot * 512:(slot + 1) * 512]
                        finish64(ps[4 + hh], f"ps{4 + hh}", hh, dst, "mixed")
            elif kind == "diff":
                h = i
                for slot in range(2):
                    chains = []
                    for hh in range(2):
                        chains.append(dict(KT=KTb[hh], KTn=kq[hh], QT=QTb[hh], QTn=qq[hh], K=65, Vn=vn,
                                           qc=slice(slot * 512, (slot + 1) * 512), S=(hh, hh),
                                           E=(2 * hh, 2 * hh + 1), O=4 + hh, Z=6 + hh,
                                           bias=(lambda pos, h=h: cct[:, pos * 12 + h:pos * 12 + h + 1]),
                                           vl=(lambda pos: Vt[:, pos, 0:128])))
                    yield from softmax_chains(chains, KL_A if slot == 0 else KL_B, lookahead=False)
                    rz0, rz1 = self.rz
                    zc0, zc1 = self.SS32
                    DVE(I("tensor_copy", out=rz0[:, :], in_=ps[4][:, :]), reads=["ps4"], writes=["rz0"])
                    ACT(I("copy", out=rz1[:, :], in_=ps[5][:, :]), reads=["ps5"], writes=["rz1"])
                    DVE(I("tensor_copy", out=zc0[:, :], in_=ps[6][:, :]), reads=["ps6"], writes=["SS320"])
                    ACT(I("copy", out=zc1[:, :], in_=ps[7][:, :]), reads=["ps7"], writes=["SS321"])
                    DVE(I("reciprocal", out=zc0[:, :], in_=zc0[:, :]), reads=["SS320"], writes=["SS320"])
                    DVE(I("reciprocal", out=zc1[:, :], in_=zc1[:, :]), reads=["SS321"], writes=["SS321"])
                    DVE(I("tensor_tensor", out=rz0[:, :], in0=rz0[:, :], in1=zc0[:, :], op=ALU.mult), reads=["rz0", "SS320"], writes=["rz0"])
                    DVE(I("tensor_tensor", out=rz1[:, :], in0=rz1[:, :], in1=zc1[:, :], op=ALU.mult), reads=["rz1", "SS321"], writes=["rz1"])
                    DVE(I("scalar_tensor_tensor", out=rz0[:, :], in0=rz1[:, :], scalar=neglam, in1=rz0[:, :],
                          op0=ALU.mult, op1=ALU.add), reads=["rz0", "rz1", "neglam"], writes=["rz0"])
                    DVE(I("tensor_tensor", out=zc0[:, :], in0=rz0[:, :], in1=rz0[:, :], op=ALU.mult), reads=["rz0"], writes=["SS320"])

                    def tail(bank, slot=slot, mixu=mixu, rz0=rz0, zc0=zc0, zc1=zc1):
                        PE(I("matmul", ps[bank][:, :], lhsT=ones32, rhs=zc0[:, :], start=True, stop=True),
                           reads=["SS320", "cm32"], writes=[f"ps{bank}"])
                        ACT(I("activation", out=zc1[:, :], in_=ps[bank][:, :], func=AF.Ln, bias=1e-5, scale=1.0 / 128.0),
                            reads=[f"ps{bank}"], writes=["SS321"])
                        ACT(I("activation", out=zc1[:, :], in_=zc1[:, :], func=AF.Exp, scale=-0.5), reads=["SS321"], writes=["SS321"])
                        DVE(I("tensor_tensor", out=rz0[:, :], in0=rz0[:, :], in1=zc1[:, :], op=ALU.mult),
                            reads=["rz0", "SS321"], writes=["rz0"])
                        DVE(I("tensor_scalar", out=mixu[:, slot * 512:(slot + 1) * 512], in0=rz0[:, :], scalar1=subs,
                              scalar2=None, op0=ALU.mult), reads=["rz0", "subs"], writes=["mixed"])

                    deferred.append(tail)
            else:
                spt, SSt = self.spt, self.SSt
                for slot in range(2):
                    kl = KL_A_S if slot == 0 else KL_B_S
                    n = len(kl)
                    qc = slice(slot * 512, (slot + 1) * 512)

                    def emit_z(hh, j):
                        pos, mask = kl[j]
                        zb = hh
                        PE(I("matmul", ps[zb][:, :], lhsT=KTb[hh][0:65, pos * 128:(pos + 1) * 128], rhs=QTb[hh][0:65, qc],
                             start=True, stop=(mask is None)),
                           reads=[kq[hh], qq[hh]], writes=[f"ps{zb}"], sig=(mask is None))
                        if mask is not None:
                            PE(I("matmul", ps[zb][:, :], lhsT=ident, rhs=self.maskt[:, mask, :], start=False, stop=True),
                               reads=["cmb", "maskt"], writes=[f"ps{zb}"])

                    def emit_softplus(j, hh):
                        pos, mask = kl[j]
                        p = j % 2
                        padb = cct[:, 320 + pos:321 + pos]
                        zb = hh
                        et = 2 * hh
                        ACT(I("activation", out=E[et][:, :], in_=ps[zb][:, :], func=AF.Exp, bias=padb, scale=1.0),
                            reads=[f"ps{zb}", "cct"], writes=[f"E{et}"])
                        ACT(I("activation", out=spt[:, p, hh, :], in_=E[et][:, :], func=AF.Ln, bias=1.0, scale=1.0),
                            reads=[f"E{et}"], writes=[f"sp{p}{hh}"])

                    def emit_T(j, hh):
                        pos, mask = kl[j]
                        p = j % 2
                        tb = 6 + hh
                        PE(I("matmul", ps[tb][:, :], lhsT=utri, rhs=spt[:, p, hh, :], start=True, stop=False),
                           reads=[f"sp{p}{hh}", "cmb"], writes=[f"ps{tb}"], sig=False)
                        if j > 0:
                            PE(I("matmul", ps[tb][:, :], lhsT=ones_b, rhs=SSt[:, p, hh, :], start=False, stop=False),
                               reads=[f"SSt{p}{hh}", "cmb"], writes=[f"ps{tb}"], sig=False)
                        PE(I("matmul", ps[tb][:, :], lhsT=KTb[hh][0:65, pos * 128:(pos + 1) * 128],
                             rhs=nQTb[hh][0:65, qc], start=False, stop=(mask is None)),
                           reads=[kq[hh], nq[hh]], writes=[f"ps{tb}"], sig=(mask is None))
                        if mask is not None:
                            PE(I("matmul", ps[tb][:, :], lhsT=negident, rhs=self.maskt[:, mask, :], start=False, stop=True),
                               reads=["cmb", "maskt"], writes=[f"ps{tb}"])

                    def emit_A(j, hh):
                        pos, mask = kl[j]
                        padb = cct[:, 320 + pos:321 + pos]
                        tb = 6 + hh
                        at = 2 * hh + 1
                        ACT(I("activation", out=E[at][:, :], in_=ps[tb][:, :], func=AF.Exp, bias=padb, scale=-1.0),
                            reads=[f"ps{tb}", "cct"], writes=[f"E{at}"])

                    def emit_PV(j, hh):
                        pos, mask = kl[j]
                        at = 2 * hh + 1
                        PE(I("matmul", ps[4 + hh][:, :], lhsT=Vt[:, pos, hh * 128:(hh + 1) * 128],
                             rhs=E[at][:, :], start=(j == 0), stop=(j == n - 1)),
                           reads=[f"E{at}", vn], writes=[f"ps{4 + hh}"])

                    for hh in range(2):
                        emit_z(hh, 0)
                    for hh in range(2):
                        emit_softplus(0, hh)
                    for j in range(n):
                        p = j % 2
                        for hh in range(2):
                            if j + 1 < n:
                                emit_z(hh, j + 1)
                            emit_T(j, hh)
                            emit_A(j, hh)
                        yield
                        if j + 1 < n:
                            for hh in range(2):
                                emit_softplus(j + 1, hh)
                        for hh in range(2):
                            emit_PV(j, hh)
                        if j + 1 < n:
                            q = (j + 1) % 2
                            for hh in range(2):
                                if j == 0:
                                    DVE(I("tensor_copy", out=SSt[:, q, hh, :], in_=spt[:, p, hh, :]), reads=[f"sp{p}{hh}"], writes=[f"SSt{q}{hh}"])
                                else:
                                    DVE(I("tensor_tensor", out=SSt[:, q, hh, :], in0=SSt[:, p, hh, :], in1=spt[:, p, hh, :], op=ALU.add),
                                        reads=[f"sp{p}{hh}", f"SSt{p}{hh}"], writes=[f"SSt{q}{hh}"])
                        yield
                    for hh in range(2):
                        dst = mixu[hh * 64:(hh + 1) * 64, qc]
                        DVE(I("tensor_copy", out=dst, in_=ps[4 + hh][0:64, :]), reads=[f"ps{4 + hh}"], writes=["mixed"])

        order = list(units)
        for _ in proj_gen(order[0], 0, (6, 7), False):
            pass
        emit_gT()
        for idx, u in enumerate(order):
            kind = KIND[u // 4]
            nxt = order[idx + 1] if idx + 1 < len(order) else None
            cg = chain_gen(u, idx % 2)
            if nxt is not None and OVERLAP:
                pg = proj_gen(nxt, (idx + 1) % 2, {"sb": (2, 3), "diff": (2, 3)}.get(kind, (6, 7)), True)
                r = 1 if kind == "sb" else 2
                for _ in cg:
                    for _k in range(r):
                        next(pg, None)
                for _ in pg:
                    pass
            else:
                for _ in cg:
                    pass
                if nxt is not None:
                    for _ in proj_gen(nxt, (idx + 1) % 2, (6, 7), False):
                        pass
        flush_deferred(0)
        if next_xres is not None:
            for t in range(48, 48 + NSLOT):
                self.pre_w[(li, t)] = self.load_w(S, li, t)
            for kc in range(KC):
                S.op("sp", I("dma_start", out=self.xres[:, kc, :], in_=next_xres[kc * 128:(kc + 1) * 128, :]),
                     writes=["xT_bf"], dma="xres")
            self.pre_xres = True
        if self.debug and li == len(self.layers) - 1 and self.debug == "mixed":
            for kc in range(KC):
                DVE(I("tensor_copy", out=self.tmp32[0][:, :], in_=self.mixed[:, kc, 0:512]), reads=["mixed"], writes=["dbgt0"])
                S.op("sp", I("dma_start", out=self.dbg[kc * 128:(kc + 1) * 128, 0:512], in_=self.tmp32[0][:, :]),
                     reads=["dbgt0"], dma="dbg0")
                DVE(I("tensor_copy", out=self.tmp32[1][:, :], in_=self.mixed[:, kc, 512:1024]), reads=["mixed"], writes=["dbgt1"])
                S.op("sp", I("dma_start", out=self.dbg[kc * 128:(kc + 1) * 128, 512:1024], in_=self.tmp32[1][:, :]),
                     reads=["dbgt1"], dma="dbg1")
        S.run()

    def phase_post(self, li, L, xres_src, last):
        S = self.new_sched()
        ps = self.ps
        ones_b = self.cmb[:, 3, :]
        ppt = self.ppt[li]
        cct = self.cct
        xres, mixed, x1bf, aT, xT_bf = self.xres, self.mixed, self.x1bf, self.aT, self.xT_bf
        tmp = self.tmp32
        tz = [self.view("R4", 16384 + i * 1024, [128, 512], BF16) for i in range(4)]
        tq = [self.view("R4", 16384 + 4096 + i * 1024, [128, 512], BF16) for i in range(4)]

        def PE(fn, reads=(), writes=(), sig=True):
            return S.op("pe", fn, reads, writes, sig)

        def ACT(fn, reads=(), writes=()):
            return S.op("act", fn, reads, writes)

        def DVE(fn, reads=(), writes=()):
            return S.op("dve", fn, reads, writes)

        def POOL(fn, reads=(), writes=()):
            return S.op("pool", fn, reads, writes)

        def xk(kc, half):
            return f"xres{kc}h{half}"

        if self.pre_xres:
            self.pre_xres = False
        else:
            for kc in range(KC):
                S.op("sp", I("dma_start", out=xres[:, kc, :], in_=xres_src[kc * 128:(kc + 1) * 128, :]),
                     writes=[xk(kc, 0), xk(kc, 1), "xres_ld"], dma="xres")
        bank = [0]

        def nb():
            b = bank[0] % 4
            bank[0] += 1
            return b

        stat_cnt = [0]

        def ln_stats(kc, half, first, last):
            cols = slice(half * 512, (half + 1) * 512)
            b1, b2 = 4 + 2 * half, 5 + 2 * half
            zi = stat_cnt[0] % 4
            stat_cnt[0] += 1
            z_, q_ = tz[zi], tq[zi]
            DVE(I("tensor_copy", out=z_[:, :], in_=xres[:, kc, cols]), reads=[xk(kc, half)], writes=[f"tz{zi}"])
            ACT(I("activation", out=q_[:, :], in_=xres[:, kc, cols], func=AF.Square), reads=[xk(kc, half)], writes=[f"tq{zi}"])
            PE(I("matmul", ps[b1][:, :], lhsT=ones_b, rhs=z_[:, :], start=first, stop=last),
               reads=[f"tz{zi}", "cmb"], writes=[f"ps{b1}"])
            PE(I("matmul", ps[b2][:, :], lhsT=ones_b, rhs=q_[:, :], start=first, stop=last),
               reads=[f"tq{zi}", "cmb"], writes=[f"ps{b2}"])

        def ln_finish(half):
            b1, b2 = 4 + 2 * half, 5 + 2 * half
            msq, lnv = tmp[half * 2], tmp[half * 2 + 1]
            mk, lk = f"tmp{half * 2}", f"tmp{half * 2 + 1}"
            ACT(I("activation", out=msq[:, :], in_=ps[b1][:, :], func=AF.Square, scale=1.0 / D), reads=[f"ps{b1}"], writes=[mk])
            ACT(I("activation", out=ps[b1][:, :], in_=ps[b1][:, :], func=AF.Identity, scale=1.0 / D), reads=[f"ps{b1}"], writes=[f"ps{b1}"])
            DVE(I("scalar_tensor_tensor", out=msq[:, :], in0=ps[b2][:, :], scalar=1.0 / D, in1=msq[:, :],
                  op0=ALU.mult, op1=ALU.subtract), reads=[f"ps{b2}", mk], writes=[mk])
            ACT(I("activation", out=lnv[:, :], in_=msq[:, :], func=AF.Ln, bias=1e-5, scale=1.0), reads=[mk], writes=[lk])
            ACT(I("activation", out=ps[b2][:, :], in_=lnv[:, :], func=AF.Exp, scale=-0.5), reads=[lk, f"ps{b2}"], writes=[f"ps{b2}"])

        def ln_normalize(half, goff, boff, write_bf, after_half=None):
            cols = slice(half * 512, (half + 1) * 512)
            bm, br = 4 + 2 * half, 5 + 2 * half
            for kc in range(KC):
                ti = 4 + (kc % 4)
                t = tmp[ti]
                tn = f"tmp{ti}"
                DVE(I("tensor_tensor", out=t[:, :], in0=xres[:, kc, cols], in1=ps[bm][:, :], op=ALU.subtract),
                    reads=[xk(kc, half), f"ps{bm}"], writes=[tn])
                DVE(I("tensor_tensor", out=t[:, :], in0=t[:, :], in1=ps[br][:, :], op=ALU.mult), reads=[tn, f"ps{br}"], writes=[tn])
                ACT(I("activation", out=xres[:, kc, cols], in_=t[:, :], func=AF.Identity,
                      bias=ppt[:, boff + kc:boff + kc + 1], scale=ppt[:, goff + kc:goff + kc + 1]),
                    reads=[tn, "ppt"], writes=[xk(kc, half)])
                if write_bf:
                    ACT(I("activation", out=x1bf[:, kc, cols], in_=t[:, :], func=AF.Identity,
                          bias=ppt[:, boff + kc:boff + kc + 1], scale=ppt[:, goff + kc:goff + kc + 1]),
                        reads=[tn, "ppt"], writes=[f"x1bf{half}"])
            if after_half is not None:
                after_half(half)

        def layer_norm_tail(goff, boff, write_bf, prefetch=(), after_half=None):
            ln_finish(0)
            ln_finish(1)
            for t in prefetch:
                wq_pref.append((t, self.load_w(S, li, t)))
            ln_normalize(0, goff, boff, write_bf, after_half)
            ln_normalize(1, goff, boff, write_bf, after_half)

        wbase = 48
        for oc in range(16):
            w, wn = self.get_w(S, li, wbase + oc)
            for half in range(2):
                cols = slice(half * 512, (half + 1) * 512)
                b = nb()
                for kc in range(KC):
                    PE(I("matmul", ps[b][:, :], lhsT=w[:, kc, :], rhs=mixed[:, kc, cols], start=(kc == 0), stop=(kc == KC - 1)),
                       reads=[wn, "mixed"], writes=[f"ps{b}"], sig=(kc == KC - 1))
                DVE(I("scalar_tensor_tensor", out=xres[:, oc, cols], in0=xres[:, oc, cols], scalar=ALPHA, in1=ps[b][:, :],
                      op0=ALU.mult, op1=ALU.add), reads=[f"ps{b}", xk(oc, half), "xres_ld"], writes=[xk(oc, half)])
                ln_stats(oc, half, oc == 0, oc == 15)

        wq_pref = []

        def next_w(t):
            if wq_pref and wq_pref[0][0] == t:
                return wq_pref.pop(0)[1]
            return self.load_w(S, li, t)

        layer_norm_tail(0, 16, True, prefetch=[64 + i for i in range(NSLOT)])

        wbase = 64
        for qf in range(4):
            for fcl in range(16):
                w, wn = next_w(wbase + qf * 32 + fcl)
                for half in range(2):
                    cols = slice(half * 512, (half + 1) * 512)
                    b = nb()
                    for kc in range(KC):
                        PE(I("matmul", ps[b][:, :], lhsT=w[:, kc, :], rhs=x1bf[:, kc, cols], start=(kc == 0), stop=(kc == KC - 1)),
                           reads=[wn, f"x1bf{half}"], writes=[f"ps{b}"], sig=(kc == KC - 1))
                    t = tmp[4 + b]
                    ACT(I("activation", out=t[:, :], in_=ps[b][:, :], func=AF.Relu), reads=[f"ps{b}"], writes=[f"tmp{4 + b}"])
                    DVE(I("tensor_tensor", out=aT[:, fcl, cols], in0=ps[b][:, :], in1=t[:, :], op=ALU.mult),
                        reads=[f"ps{b}", f"tmp{4 + b}"], writes=[f"aT{fcl}h{half}"])
            for c in range(16):
                w, wn = next_w(wbase + qf * 32 + 16 + c)
                for half in range(2):
                    cols = slice(half * 512, (half + 1) * 512)
                    b = nb()
                    for fcl in range(16):
                        PE(I("matmul", ps[b][:, :], lhsT=w[:, fcl, :], rhs=aT[:, fcl, cols], start=(fcl == 0), stop=(fcl == 15)),
                           reads=[wn, f"aT{fcl}h{half}"], writes=[f"ps{b}"], sig=(fcl == 15))
                    DVE(I("scalar_tensor_tensor", out=xres[:, c, cols], in0=xres[:, c, cols], scalar=(ALPHA if qf == 0 else 1.0),
                          in1=ps[b][:, :], op0=ALU.mult, op1=ALU.add), reads=[f"ps{b}", xk(c, half)], writes=[xk(c, half)])
                    if qf == 3:
                        ln_stats(c, half, c == 0, c == 15)
        if last:
            def out_half(half):
                cols = slice(half * 512, (half + 1) * 512)
                for kc in range(KC):
                    S.op("sp", I("dma_start", out=self.yT[kc * 128:(kc + 1) * 128, cols], in_=xres[:, kc, cols]),
                         reads=[xk(kc, half)], dma="yout")
            layer_norm_tail(32, 48, False, after_half=out_half)
        else:
            def xchg_half(half):
                cols = slice(half * 512, (half + 1) * 512)
                for kc in range(KC):
                    S.op("sp", I("dma_start", out=self.xsave[kc * 128:(kc + 1) * 128, cols], in_=xres[:, kc, cols]),
                         reads=[xk(kc, half)], writes=["xsave"], dma="yout")
                S.op("sp", I("dma_start", out=self.xs[half].ap().rearrange("(a p) c -> p a c", p=128),
                             in_=x1bf[:, :, cols]), reads=[f"x1bf{half}"], writes=[f"xs{half}"], dma=f"xch{half}")
                S.op("pool", I("collective_compute", "AllGather", ALU.bypass,
                               replica_groups=[[0, 1], [2, 3], [4, 5], [6, 7]],
                               ins=[self.xs[half].ap().opt()], outs=[self.xg[half].ap().opt()]),
                     reads=[f"xs{half}"], writes=[f"xg{half}"])
            layer_norm_tail(32, 48, True, after_half=xchg_half)
            for kc in range(KC):
                if kc % 2 == 0:
                    DVE(I("tensor_copy", out=xT_bf[:, kc, 0:1024], in_=x1bf[:, kc, :]), reads=["x1bf0", "x1bf1"],
                        writes=[xk(kc, 0), xk(kc, 1)])
                else:
                    ACT(I("copy", out=xT_bf[:, kc, 0:1024], in_=x1bf[:, kc, :]), reads=["x1bf0", "x1bf1"],
                        writes=[xk(kc, 0), xk(kc, 1)])
            fsel = cct[:, 352:353]
            gsel = cct[:, 353:354]
            stg = [[self.view(reg, i * 16384, [128, 16, 512], BF16).rearrange("p (a r) c -> p a r c", r=2) for i in range(2)]
                   for reg in ("R2", "R4")]
            tmpb = [self.view("R5", i * 1024, [128, 512], BF16) for i in range(4)]
            for r in range(2):
                for i in range(2):
                    for rk in range(2):
                        src = self.xg[i].ap().rearrange("(r a p) c -> r p a c", r=2, p=128)[rk][:, r * 8:(r + 1) * 8]
                        S.op("sp", I("dma_start", out=stg[r][i][:, :, rk, :], in_=src), reads=[f"xg{i}"],
                             writes=[f"stg{r}{i}"] + ([f"tz{j}" for j in range(4)] + [f"tq{j}" for j in range(4)] + [f"tmp{j}" for j in range(8)]
                                                      if r == 1 else [f"aT{j}h{h}" for j in range(16) for h in range(2)]),
                             dma=f"xch{2 + r}")
            cnt = 0
            for r in range(2):
                sA, sB = stg[r]
                for k8 in range(8):
                    kc = r * 8 + k8
                    for (s1, s0, c0) in ((sA, sB, 1024), (sB, sA, 1536)):
                        ti = cnt % 4
                        cnt += 1
                        ACT(I("activation", out=tmpb[ti][:, :], in_=s1[:, k8, 1, :], func=AF.Identity, scale=fsel),
                            reads=[f"stg{r}0", f"stg{r}1", "cct"], writes=[f"tmpb{ti}"] + (["x1bf0", "x1bf1"] if cnt <= 4 else []))
                        DVE(I("scalar_tensor_tensor", out=xT_bf[:, kc, c0:c0 + 512], in0=s0[:, k8, 0, :], scalar=gsel, in1=tmpb[ti][:, :],
                              op0=ALU.mult, op1=ALU.add), reads=[f"stg{r}0", f"stg{r}1", f"tmpb{ti}", "cct"], writes=[xk(kc, 0), xk(kc, 1)])
        if not last:
            for t in range(3):
                self.pre_w[(li + 1, t)] = self.load_w(S, li + 1, t)
        S.run()


def build_single_layer(L, debug=False, units=range(16), do_post=True):
    p = Prog([L], debug=debug)
    p.phase_clear()
    p.phase_setup()
    p.phase_attn(0, L, True, units=units, next_xres=(p.xT[:, 0:1024] if do_post else None))
    if do_post:
        p.phase_post(0, L, p.xT[:, 0:1024], True)
    return p.nc


def build_fused():
    p = Prog([0, 1])
    p.phase_clear()
    p.phase_setup()
    p.phase_attn(0, 0, True, next_xres=p.xT[:, 0:1024])
    p.phase_post(0, 0, p.xT[:, 0:1024], False)
    p.phase_attn(1, 1, False, next_xres=p.xsave)
    p.phase_post(1, 1, p.xsave, True)
    return p.nc


_CACHE = {}


def _run_layer(L, x, inp, debug=False):
    if ("nc", L, debug) not in _CACHE:
        _CACHE[("nc", L, debug)] = build_single_layer(L, debug)
    nc = _CACHE[("nc", L, debug)]
    W, wfz = _layer_weights(inp["w_in"][L], inp["w_out"][L], inp["w_mlp_in"][L], inp["w_mlp_out"][L])
    pp = _layer_params(L, inp)
    masks = _mask_tables()
    cmat = _const_mats()
    in_maps = []
    for c in range(8):
        b, half = divmod(c, 2)
        tp = _tokperm(half)
        cc, rtab = _core_consts(half)
        xT = np.ascontiguousarray(x[b][tp, :].T)
        in_maps.append({"xT": xT, "W0": W, "wfz0": wfz, "pp0": pp, "cc": cc, "rtab": rtab, "masks": masks, "cmat": cmat})
    res = run_bass_kernel_spmd(nc, in_maps, core_ids=list(range(8)))
    out = np.empty_like(x)
    dbg = None
    if debug:
        dbg = np.empty((4, 2048, 2048), np.float32)
    for c in range(8):
        b, half = divmod(c, 2)
        tp = _tokperm(half)
        out[b][tp[0:1024], :] = res.results[c]["yT"].T
        if debug:
            dbg[b][tp[0:1024], :] = res.results[c]["dbg"].T
    return (out, dbg) if debug else out


def kernel_unfused(**inputs):
    inp = {k: np.asarray(v) for k, v in inputs.items()}
    x = np.ascontiguousarray(inp["x"], dtype=np.float32)
    for L in range(DEPTH):
        x = _run_layer(L, x, inp)
    return x


def kernel(**inputs):
    inp = {k: np.asarray(v) for k, v in inputs.items()}
    x = np.ascontiguousarray(inp["x"], dtype=np.float32)
    if "fused" not in _CACHE:
        _CACHE["fused"] = build_fused()
    nc = _CACHE["fused"]
    shared = {"masks": _mask_tables(), "cmat": _const_mats()}
    for L in range(DEPTH):
        W, wfz = _layer_weights(inp["w_in"][L], inp["w_out"][L], inp["w_mlp_in"][L], inp["w_mlp_out"][L])
        shared[f"W{L}"] = W
        shared[f"wfz{L}"] = wfz
        shared[f"pp{L}"] = _layer_params(L, inp)
    in_maps = []
    for c in range(8):
        b, half = divmod(c, 2)
        tp = _tokperm(half)
        cc, rtab = _core_consts(half)
        m = dict(shared)
        m.update({"xT": np.ascontiguousarray(x[b][tp, :].T), "cc": cc, "rtab": rtab})
        in_maps.append(m)
    res = run_bass_kernel_spmd(nc, in_maps, core_ids=list(range(8)))
    out = np.empty_like(x)
    for c in range(8):
        b, half = divmod(c, 2)
        tp = _tokperm(half)
        out[b][tp[0:1024], :] = res.results[c]["yT"].T
    return out
```

```python
import numpy as np
import concourse.bass as bass
import concourse.mybir as mybir
from concourse.bass_utils import run_bass_kernel_spmd

F32 = mybir.dt.float32
BF16 = mybir.dt.bfloat16
AF = mybir.ActivationFunctionType
ALU = mybir.AluOpType
AX = mybir.AxisListType

D = 2048
SEQ = 2048
KC = 16
DEPTH = 2
NEG = -30000.0
ALPHA = (2 * DEPTH) ** 0.25
NWT = 192
NSLOT = 3
OVERLAP = True
GROUPS = ([0, 2, 1, 3], [1, 3, 2, 0])
N_MASK = 17
CC_COLS = 192 + 128 + 16 + 16 + 2
PP_COLS = 64 + 1 + 256 + 1


def I(name, *args, **kw):
    return (name, args, kw)


class Tok:
    __slots__ = ("sem", "val", "eng")

    def __init__(self, sem, val, eng):
        self.sem, self.val, self.eng = sem, val, eng


class Sched:
    COMPUTE = ("pe", "act", "dve", "pool")

    def __init__(self, nc, sems, dsems):
        self.nc = nc
        self.sem = sems
        self.cnt = sems["_cnt"]
        self.dsem = dsems
        self.dcnt = dsems["_cnt"]
        self.pending = {e: [] for e in self.COMPUTE}
        self.streams = {e: [] for e in ("pe", "act", "dve", "pool", "sp")}
        self.res = {}
        self.dma_toks = []

    def _r(self, key):
        r = self.res.get(key)
        if r is None:
            r = self.res[key] = [None, []]
        return r

    def op(self, eng, fn, reads=(), writes=(), sig=True, dma=None):
        assert sig or eng == "pe"
        deps = []
        for k in reads:
            r = self._r(k)
            if r[0] is not None:
                deps.append(r[0])
        for k in writes:
            r = self._r(k)
            if r[0] is not None:
                deps.append(r[0])
            deps.extend(r[1])
        if dma is None and eng == "pe":
            deps = [d for d in deps if d.eng != "pe"]
        if dma is not None:
            assert dma in self.dsem, dma
            self.dcnt[dma] += 16
            tok = Tok(self.dsem[dma], self.dcnt[dma], "dma")
            inc = (self.dsem[dma], 16)
            self.dma_toks.append(tok)
        else:
            tok = Tok(self.sem[eng], None, eng)
            self.pending[eng].append(tok)
            inc = None
            if sig:
                self.cnt[eng] += 1
                for t in self.pending[eng]:
                    t.val = self.cnt[eng]
                self.pending[eng] = []
                inc = (self.sem[eng], 1)
        self.streams[eng].append((deps, fn, inc))
        for k in reads:
            self._r(k)[1].append(tok)
        for k in writes:
            r = self._r(k)
            r[0] = tok
            r[1] = []
        return tok

    def run(self):
        for e in self.COMPUTE:
            assert not self.pending[e], f"unsignalled tail on {e}"
        self.streams["sp"].append((list(self.dma_toks), None, None))
        with self.nc.Block() as block:
            binder = {"pe": block.tensor, "act": block.scalar, "dve": block.vector,
                      "pool": block.gpsimd, "sp": block.sync}
            for e, stream in self.streams.items():
                if not stream:
                    continue

                def body(engine, stream=stream):
                    waited = {}
                    for deps, fn, inc in stream:
                        need = {}
                        for d in deps:
                            assert d.val is not None
                            key = id(d.sem)
                            if waited.get(key, 0) >= d.val:
                                continue
                            if key not in need or need[key][1] < d.val:
                                need[key] = (d.sem, d.val)
                        for key, (s, v) in need.items():
                            engine.wait_ge(s, v)
                            waited[key] = v
                        if fn is None:
                            continue
                        inst = getattr(engine, fn[0])(*fn[1], **fn[2])
                        if inc is not None:
                            inst.then_inc(inc[0], inc[1])

                binder[e](body)


def _slopes():
    n = 12
    return np.exp2(-8.0 * np.arange(1, n + 1, dtype=np.float64) / n)


def _tokperm(half):
    return np.concatenate([np.arange(512) + 512 * g for g in GROUPS[half]])


def _mask_tables():
    sl = np.arange(128)[:, None]
    tl = np.arange(512)[None, :]
    m = np.zeros((N_MASK, 128, 512), np.float32)
    for i in range(4):
        m[i] = np.where(128 * i + sl <= tl, 0.0, NEG)
        m[4 + i] = np.where(128 * i + sl < tl, 0.0, NEG)

    def dil(rel):
        d = 128 * rel + tl - sl
        mult = ((d >= 0) & (d <= 128)).astype(np.int64) + ((d >= 0) & (d % 4 == 0) & (d <= 512)) \
            + ((d >= 0) & (d % 16 == 0) & (d <= 2048))
        return np.where(mult > 0, np.log(np.maximum(mult, 1)), NEG).astype(np.float32)

    for i in range(4):
        m[8 + i] = dil(-i)
    for r in range(1, 5):
        m[12 + (4 - r)] = dil(r)
    m[16] = dil(8)
    return m


def _const_mats():
    c = np.zeros((4, 128, 128), np.float32)
    c[0] = np.eye(128)
    c[1] = -np.eye(128)
    kk = np.arange(128)[:, None]
    mm = np.arange(128)[None, :]
    c[2] = (kk >= mm).astype(np.float32)
    c[3] = 1.0
    return c


def _core_consts(half):
    tp = _tokperm(half)
    s_nat = tp.reshape(16, 128).T.astype(np.float64)
    pad = np.zeros(16)
    if half == 0:
        pad[12:16] = NEG
    sl = _slopes()
    cc = np.zeros((128, CC_COLS), np.float32)
    biasK = s_nat[:, :, None] * sl[None, None, :] + pad[None, :, None]
    cc[:, 0:192] = biasK.reshape(128, 192)
    cc[:, 192:320] = np.repeat(pad[None, :, None], 8, axis=2).repeat(128, axis=0).reshape(128, 128)
    cc[:, 320:336] = pad[None, :]
    nat = GROUPS[half]
    m = np.zeros((4, 4))
    for g in range(4):
        for g2 in range(4):
            m[g, g2] = 1.0 if nat[g2] < nat[g] else 0.0
    cc[:, 336:352] = m.reshape(1, 16)
    cc[:, 352] = 1.0 if half == 0 else 0.0
    cc[:, 353] = 0.0 if half == 0 else 1.0
    rtab = (-sl[:, None] * tp[None, 0:1024].astype(np.float64)).astype(np.float32)
    return cc, rtab


IN_OFF = {"fq": 0, "fk": 512, "fv": 1024, "fz": 1536, "sq": 1544, "sk": 2056, "sv": 2568,
          "dq": 3080, "dk": 3592, "dv": 4104, "gq": 4616, "gk": 5128, "gv": 5640}
MIX = (("fq", "fk", "fv"), ("sq", "sk", "sv"), ("dq", "dk", "dv"), ("gq", "gk", "gv"))


def _wtile(w_cols):
    return np.ascontiguousarray(w_cols.reshape(16, 128, 128).transpose(1, 0, 2).reshape(128, 2048))


def _layer_weights(w_in, w_out, w1, w2):
    W = np.empty((NWT, 128, 2048), np.float32)
    t = 0
    for u in range(16):
        m, i = divmod(u, 4)
        for nm in MIX[m]:
            c0 = IN_OFF[nm] + i * 128
            W[t] = _wtile(w_in[:, c0:c0 + 128])
            t += 1
    for oc in range(16):
        W[t] = _wtile(w_out[:, oc * 128:(oc + 1) * 128])
        t += 1
    for qf in range(4):
        for fcl in range(16):
            fc = qf * 16 + fcl
            W[t] = _wtile(w1[:, fc * 128:(fc + 1) * 128])
            t += 1
        for c in range(16):
            W[t] = _wtile(w2[qf * 2048:(qf + 1) * 2048, c * 128:(c + 1) * 128])
            t += 1
    assert t == NWT
    wfz = np.ascontiguousarray(
        w_in[:, 1536:1544].reshape(16, 128, 8).transpose(1, 0, 2).reshape(128, 128))
    return W, wfz


def _layer_params(l, inp):
    pp = np.zeros((128, PP_COLS), np.float32)
    pp[:, 0:16] = inp["ln1_gain"][l].reshape(16, 128).T
    pp[:, 16:32] = inp["ln1_bias"][l].reshape(16, 128).T
    pp[:, 32:48] = inp["ln2_gain"][l].reshape(16, 128).T
    pp[:, 48:64] = inp["ln2_bias"][l].reshape(16, 128).T
    pp[:, 64] = inp["diff_subln_gain"][l]
    pp[:, 65:129] = inp["diff_lambda_q1"][l][None, :]
    pp[:, 129:193] = inp["diff_lambda_k1"][l][None, :]
    pp[:, 193:257] = inp["diff_lambda_q2"][l][None, :]
    pp[:, 257:321] = inp["diff_lambda_k2"][l][None, :]
    pp[0:8, 321] = inp["fox_forget_bias"][l]
    return pp


class Prog:
    def __init__(self, layers, debug=False):
        self.layers = layers
        self.debug = debug
        nc = self.nc = bass.Bass("TRN2", target_bir_lowering=False)
        nL = len(layers)
        self.xT = nc.dram_tensor("xT", [D, SEQ], F32, kind="ExternalInput").ap()
        self.W = [nc.dram_tensor(f"W{i}", [NWT, 128, 2048], F32, kind="ExternalInput").ap() for i in range(nL)]
        self.wfz = [nc.dram_tensor(f"wfz{i}", [128, 128], F32, kind="ExternalInput").ap() for i in range(nL)]
        self.pp = [nc.dram_tensor(f"pp{i}", [128, PP_COLS], F32, kind="ExternalInput").ap() for i in range(nL)]
        self.cc = nc.dram_tensor("cc", [128, CC_COLS], F32, kind="ExternalInput").ap()
        self.rtab = nc.dram_tensor("rtab", [12, 1024], F32, kind="ExternalInput").ap()
        self.masks = nc.dram_tensor("masks", [N_MASK, 128, 512], F32, kind="ExternalInput").ap()
        self.cmat = nc.dram_tensor("cmat", [4, 128, 128], F32, kind="ExternalInput").ap()
        self.yT = nc.dram_tensor("yT", [D, 1024], F32, kind="ExternalOutput").ap()
        if debug:
            self.dbg = nc.dram_tensor("dbg", [D, 1024], F32, kind="ExternalOutput").ap()
        if nL > 1:
            self.xsave = nc.dram_tensor("xsave", [D, 1024], F32).ap()
            self.xs = [nc.dram_tensor(f"xs{i}", [D, 512], BF16) for i in range(2)]
            self.xg = [nc.dram_tensor(f"xg{i}", [2 * D, 512], BF16) for i in range(2)]
        self.sems = {e: nc.alloc_semaphore(f"sem_{e}") for e in Sched.COMPUTE}
        self.sems["_cnt"] = {e: 0 for e in Sched.COMPUTE}
        self.dsems = {"_cnt": {}}
        for nm in ["c0", "c1", "xt", "xres", "yout", "dbg0", "dbg1", "qrow00", "qrow01", "qrow10", "qrow11", "qrowp00", "qrowp01", "qrowp10", "qrowp11", "xch0", "xch1", "xch2", "xch3"] + \
                [f"wr{i}" for i in range(NSLOT)]:
            self.dsems[nm] = nc.alloc_semaphore(f"dsem_{nm}")
            self.dsems["_cnt"][nm] = 0
        self.pst = nc.alloc_psum_tensor("pst", [128, 8, 512], F32)
        self.ps = [self.pst[:, i, :] for i in range(8)]
        R = self.R = {}
        R["R1"] = nc.alloc_sbuf_tensor("R1", [128, 16384], F32)
        R["R2"] = nc.alloc_sbuf_tensor("R2", [128, 8192], F32)
        R["R4"] = nc.alloc_sbuf_tensor("R4", [128, 8192], F32)
        R["R5"] = nc.alloc_sbuf_tensor("R5", [128, 8192], F32)
        self.wr = [nc.alloc_sbuf_tensor(f"wr{i}", [128, 16, 128], BF16) for i in range(NSLOT)]
        self.maskt = nc.alloc_sbuf_tensor("maskt", [128, N_MASK, 512], BF16)
        self.cmb = nc.alloc_sbuf_tensor("cmb", [128, 4, 128], BF16)
        self.cm32 = nc.alloc_sbuf_tensor("cm32", [128, 4, 128], F32)
        self.cct = nc.alloc_sbuf_tensor("cct", [128, CC_COLS], F32)
        self.ppt = [nc.alloc_sbuf_tensor(f"ppt{i}", [128, PP_COLS], F32) for i in range(nL)]
        self.gT = nc.alloc_sbuf_tensor("gT", [128, 16, 8], F32)
        self.small = nc.alloc_sbuf_tensor("small", [128, 160], F32)

        def view(reg, boff, shape, dt):
            n = int(np.prod(shape[1:]))
            esz = 4 if dt == F32 else 2
            assert boff % 4 == 0 and (n * esz) % 4 == 0
            ap = R[reg][:, boff // 4: boff // 4 + (n * esz) // 4]
            if dt != F32:
                ap = ap.bitcast(dt)
            if len(shape) == 3:
                ap = ap.rearrange("p (a b) -> p a b", a=shape[1])
            return ap

        self.view = view
        self.xT_bf = view("R1", 0, [128, 16, 2048], BF16)
        self.xres = view("R1", 0, [128, 16, 1024], F32)
        self.mixed = view("R2", 0, [128, 16, 1024], BF16)
        self.aT = view("R2", 0, [128, 16, 1024], BF16)
        o = 0
        KT0 = [view("R4", o + i * 4096, [128, 2048], BF16) for i in range(2)]
        o += 8192
        QT0 = [view("R4", o + i * 2048, [128, 1024], BF16) for i in range(2)]
        o += 4096
        nQT0 = [view("R4", o + i * 2048, [128, 1024], BF16) for i in range(2)]
        o += 4096
        self.VT = view("R4", o, [128, 2048], BF16)
        o += 4096
        Vtok0 = view("R4", o, [128, 16, 256], BF16)
        o += 8192
        self.E = [view("R4", o + i * 1024, [128, 512], BF16) for i in range(4)]
        self.Et = view("R4", o, [128, 4, 512], BF16)
        o += 4096
        assert o == 32768
        self.tmp32 = [view("R4", i * 2048, [128, 512], F32) for i in range(8)]
        self.fz = view("R5", 0, [128, 2048], F32)
        self.grow = view("R5", 8192, [128, 2048], F32)
        self.ones8 = view("R2", 0, [128, 512], F32)
        self.spt = view("R5", 0, [128, 4, 512], BF16).rearrange("p (a h) c -> p a h c", a=2)
        nQT1 = [view("R5", 4096 + i * 2048, [128, 1024], BF16) for i in range(2)]
        KT1 = [view("R5", 8192 + i * 4096, [128, 2048], BF16) for i in range(2)]
        self.rneg = view("R5", 16384, [128, 1024], BF16)
        self.SS32 = [view("R5", 18432 + i * 2048, [128, 512], F32) for i in range(2)]
        self.SSt = view("R5", 22528, [128, 4, 512], BF16).rearrange("p (a h) c -> p a h c", a=2)
        qt1b = nc.alloc_sbuf_tensor("qt1b", [128, 1024], BF16)
        QT1 = [view("R5", 26624, [128, 1024], BF16), qt1b[:, :]]
        self.rz = [view("R5", 28672 + i * 2048, [128, 512], F32) for i in range(2)]
        vtok1 = nc.alloc_sbuf_tensor("vtok1", [128, 16, 256], BF16)
        self.KT = [KT0, KT1]
        self.QT = [QT0, QT1]
        self.nQT = [nQT0, nQT1]
        self.Vtok = [Vtok0, vtok1[:, :, :]]
        self.x1bf = view("R5", 0, [128, 16, 1024], BF16)
        self.wt = 0
        self.pre_w = {}
        self.pre_xres = False

    def new_sched(self):
        return Sched(self.nc, self.sems, self.dsems)

    def get_w(self, S, L, t):
        if (L, t) in self.pre_w:
            return self.pre_w.pop((L, t))
        return self.load_w(S, L, t)

    def load_w(self, S, L, t):
        slot = self.wt % NSLOT
        self.wt += 1
        dst = self.wr[slot]
        src = self.W[L][t].rearrange("p (a b) -> p a b", a=16)
        S.op("pool", I("dma_start", out=dst[:], in_=src), writes=[f"wr{slot}"], dma=f"wr{slot}")
        return dst, f"wr{slot}"

    def phase_clear(self):
        sems = [self.sems[e] for e in Sched.COMPUTE] + [v for k, v in self.dsems.items() if k != "_cnt"]
        with self.nc.Block() as block:
            def body(engine):
                for s_ in sems:
                    engine.sem_clear(s_)
            block.gpsimd(body)

    def phase_setup(self):
        S = self.new_sched()
        S.op("pool", I("dma_start", out=self.maskt[:], in_=self.masks.rearrange("m p t -> p m t")),
             writes=["maskt"], dma="c0")
        S.op("pool", I("dma_start", out=self.cmb[:], in_=self.cmat.rearrange("m p t -> p m t")),
             writes=["cmb"], dma="c0")
        S.op("sp", I("dma_start", out=self.cm32[:], in_=self.cmat.rearrange("m p t -> p m t")),
             writes=["cm32"], dma="c1")
        S.op("sp", I("dma_start", out=self.cct[:], in_=self.cc), writes=["cct"], dma="c1")
        for i in range(len(self.layers)):
            S.op("sp", I("dma_start", out=self.ppt[i][:], in_=self.pp[i]), writes=["ppt"], dma="c1")
        S.run()

    def phase_attn(self, li, L, first, units=range(16), next_xres=None):
        nc = self.nc
        S = self.new_sched()
        ps = self.ps
        ident = self.cmb[:, 0, :]
        negident = self.cmb[:, 1, :]
        utri = self.cmb[:, 2, :]
        ones_b = self.cmb[:, 3, :]
        ident32 = self.cm32[:, 0, :]
        ones32 = self.cm32[:, 3, :]
        cct, ppt = self.cct, self.ppt[li]
        KT, QT, nQT, VT, Vtok, E = self.KT, self.QT, self.nQT, self.VT, self.Vtok, self.E
        xT_bf = self.xT_bf
        sm = self.small
        lam_init = 0.8 - 0.6 * float(np.exp(-0.3 * L))

        def PE(fn, reads=(), writes=(), sig=True):
            return S.op("pe", fn, reads, writes, sig)

        def ACT(fn, reads=(), writes=()):
            return S.op("act", fn, reads, writes)

        def DVE(fn, reads=(), writes=()):
            return S.op("dve", fn, reads, writes)

        if first:
            for kc in range(KC):
                S.op("pool", I("dma_start", out=xT_bf[:, kc, :], in_=self.xT[kc * 128:(kc + 1) * 128, :]),
                     writes=["xT_bf"], dma="xt")
        for bb in range(2):
            for i in range(2):
                DVE(I("memset", KT[bb][i][64:65, :], 1.0), writes=[f"KT{bb}{i}"])
            DVE(I("memset", Vtok[bb][:, :, :], 1.0), writes=[f"Vtok{bb}"])

        DVE(I("tensor_tensor", out=sm[:, 64:128], in0=ppt[:, 65:129], in1=ppt[:, 129:193], op=ALU.mult),
            reads=["ppt"], writes=["sm_a"])
        DVE(I("tensor_reduce", out=sm[:, 1:2], in_=sm[:, 64:128], axis=AX.X, op=ALU.add), reads=["sm_a"], writes=["sm_b"])
        DVE(I("tensor_tensor", out=sm[:, 64:128], in0=ppt[:, 193:257], in1=ppt[:, 257:321], op=ALU.mult),
            reads=["ppt", "sm_b"], writes=["sm_a"])
        DVE(I("tensor_reduce", out=sm[:, 2:3], in_=sm[:, 64:128], axis=AX.X, op=ALU.add), reads=["sm_a"], writes=["sm_c"])
        ACT(I("activation", out=sm[:, 4:5], in_=sm[:, 1:2], func=AF.Exp), reads=["sm_b"], writes=["sm_d"])
        ACT(I("activation", out=sm[:, 5:6], in_=sm[:, 2:3], func=AF.Exp), reads=["sm_c"], writes=["sm_e"])
        DVE(I("scalar_tensor_tensor", out=sm[:, 6:7], in0=sm[:, 5:6], scalar=-lam_init, in1=sm[:, 4:5],
                                             op0=ALU.add, op1=ALU.subtract), reads=["sm_d", "sm_e"], writes=["neglam"])
        DVE(I("tensor_scalar", out=sm[:, 7:8], in0=ppt[:, 64:65], scalar1=(1.0 - lam_init), scalar2=None, op0=ALU.mult),
            reads=["ppt"], writes=["subs"])
        DVE(I("tensor_scalar", out=sm[:, 8:9], in0=ppt[:, 321:322], scalar1=-1.0, scalar2=None, op0=ALU.mult),
            reads=["ppt"], writes=["negbf"])
        neglam = sm[:, 6:7]
        subs = sm[:, 7:8]
        negbf = sm[0:8, 8:9]

        def evac(out, in_, reads, writes, scale=None, shifted=False, allow_act=True):
            use_act = allow_act and (not shifted) and (evac_toggle[0] % 2 == 1)
            evac_toggle[0] += 1
            if use_act:
                if scale is None:
                    ACT(I("copy", out=out, in_=in_), reads, writes)
                else:
                    ACT(I("activation", out=out, in_=in_, func=AF.Identity, scale=scale), reads, writes)
            else:
                if scale is None:
                    DVE(I("tensor_copy", out=out, in_=in_), reads, writes)
                else:
                    DVE(I("tensor_scalar", out=out, in0=in_, scalar1=scale, scalar2=None, op0=ALU.mult), reads, writes)

        evac_toggle = [0]

        wfz_t = self.view("R4", 0, [128, 16, 8], BF16)
        S.op("pool", I("dma_start", out=wfz_t, in_=self.wfz[li].rearrange("p (a b) -> p a b", a=16)),
             writes=["KT00"], dma="c0")
        fz, grow, rneg = self.fz, self.grow, self.rneg
        for tc in range(4):
            b = 6 + (tc % 2)
            for kc in range(KC):
                PE(I("matmul", ps[b][0:8, :], lhsT=wfz_t[:, kc, :],
                                                               rhs=xT_bf[:, kc, tc * 512:(tc + 1) * 512],
                                                               start=(kc == 0), stop=(kc == KC - 1)),
                   reads=["KT00", "xT_bf"], writes=[f"ps{b}"], sig=(kc == KC - 1))
            ACT(I("activation", out=fz[0:8, tc * 512:(tc + 1) * 512], in_=ps[b][0:8, :], func=AF.Exp,
                                                   bias=negbf, scale=-1.0), reads=[f"ps{b}", "negbf"], writes=["fz"])
        ACT(I("activation", out=fz[0:8, :], in_=fz[0:8, :], func=AF.Ln, bias=1.0, scale=1.0), reads=["fz"], writes=["fz"])
        DVE(I("memset", KT[0][0][64:65, :], 1.0), reads=[], writes=["KT00"])
        DVE(I("memset", self.ones8[0:8, :], 1.0), writes=["ones8"])
        for g in range(4):
            DVE(I("tensor_tensor_scan", out=grow[0:8, g * 512:(g + 1) * 512], data0=self.ones8[0:8, :],
                                                    data1=fz[0:8, g * 512:(g + 1) * 512], initial=0.0,
                                                    op0=ALU.mult, op1=ALU.add), reads=["fz", "ones8"], writes=["grow"])
        off = sm[0:8, 16:20]
        DVE(I("memset", sm[0:8, 16:20], 0.0), writes=["off"])
        for g in range(4):
            for g2 in range(4):
                DVE(I("scalar_tensor_tensor",
                    out=sm[0:8, 16 + g:17 + g], in0=grow[0:8, g2 * 512 + 511:g2 * 512 + 512],
                    scalar=cct[0:8, 336 + g * 4 + g2:337 + g * 4 + g2], in1=sm[0:8, 16 + g:17 + g],
                    op0=ALU.mult, op1=ALU.add), reads=["grow", "cct", "off"], writes=["off"])
        for g in range(4):
            DVE(I("tensor_scalar", out=grow[0:8, g * 512:(g + 1) * 512], in0=grow[0:8, g * 512:(g + 1) * 512],
                                               scalar1=sm[0:8, 16 + g:17 + g], scalar2=None, op0=ALU.add),
                reads=["off", "grow"], writes=["grow"])
        DVE(I("tensor_scalar", out=rneg[0:8, :], in0=grow[0:8, 0:1024], scalar1=-1.0, scalar2=None, op0=ALU.mult),
            reads=["grow"], writes=["rneg"])
        def emit_gT():
            for blk in range(16):
                PE(I("transpose", out=ps[6][:, blk * 8:(blk + 1) * 8], in_=grow[0:8, blk * 128:(blk + 1) * 128],
                     identity=ident32[0:8, 0:8]), reads=["grow", "cm32"], writes=["ps6"])
            DVE(I("tensor_tensor", out=self.gT[:, :, :].rearrange("p a b -> p (a b)"), in0=ps[6][:, 0:128],
                  in1=cct[:, 192:320], op=ALU.add), reads=["ps6", "cct"], writes=["gT"])

        deferred = []

        def flush_deferred(bank):
            while deferred:
                deferred.pop(0)(bank)

        def finish64(O, Oname, rzi, dst, dname):
            rz = self.rz[rzi]
            zc = self.SS32[rzi]
            ACT(I("copy", out=rz[0:64, :], in_=O[0:64, :]), reads=[Oname], writes=[f"rz{rzi}"])
            DVE(I("tensor_copy", out=zc[0:64, :], in_=O[64:128, :]), reads=[Oname], writes=[f"SS32{rzi}"])
            DVE(I("reciprocal", out=zc[0:64, :], in_=zc[0:64, :]), reads=[f"SS32{rzi}"], writes=[f"SS32{rzi}"])
            DVE(I("tensor_tensor", out=dst, in0=rz[0:64, :], in1=zc[0:64, :], op=ALU.mult),
                reads=[f"rz{rzi}", f"SS32{rzi}"], writes=[dname])

        def softmax_chains(chains, klist, lookahead=True):
            n = len(klist)

            def emitS(c, j):
                pos, mask = klist[j]
                sb = c["S"][j % 2]
                kk = c["K"]
                PE(I("matmul", ps[sb][:, :], lhsT=c["KT"][0:kk, pos * 128:(pos + 1) * 128], rhs=c["QT"][0:kk, c["qc"]],
                     start=True, stop=(mask is None)),
                   reads=[c["KTn"], c["QTn"]], writes=[f"ps{sb}"], sig=(mask is None))
                if mask is not None:
                    PE(I("matmul", ps[sb][:, :], lhsT=ident, rhs=self.maskt[:, mask, :], start=False, stop=True),
                       reads=["cmb", "maskt"], writes=[f"ps{sb}"])

            def emitExp(c, j):
                pos = klist[j][0]
                sb = c["S"][j % 2]
                et = c["E"][j % 2]
                ACT(I("activation", out=E[et][:, :], in_=ps[sb][:, :], func=AF.Exp, bias=c["bias"](pos), scale=1.0),
                    reads=[f"ps{sb}", "gT", "cct"], writes=[f"E{et}"])

            def emitPV(c, j):
                pos = klist[j][0]
                et = c["E"][j % 2]
                PE(I("matmul", ps[c["O"]][:, :], lhsT=c["vl"](pos), rhs=E[et][:, :], start=(j == 0), stop=(j == n - 1)),
                   reads=[f"E{et}", c["Vn"]], writes=[f"ps{c['O']}"], sig=("Z" not in c))
                if "Z" in c:
                    PE(I("matmul", ps[c["Z"]][:, :], lhsT=ones_b, rhs=E[et][:, :], start=(j == 0), stop=(j == n - 1)),
                       reads=[f"E{et}", "cmb"], writes=[f"ps{c['Z']}"])

            if lookahead:
                for c in chains:
                    emitS(c, 0)
                for j in range(n):
                    for c in chains:
                        if j + 1 < n:
                            emitS(c, j + 1)
                    for c in chains:
                        emitExp(c, j)
                    for c in chains:
                        emitPV(c, j)
                    if j == min(2, n - 1):
                        flush_deferred(chains[0]["S"][j % 2])
                    yield
            else:
                for j in range(n):
                    for c in chains:
                        emitS(c, j)
                    for c in chains:
                        emitExp(c, j)
                    yield
                    for c in chains:
                        emitPV(c, j)
                    if j == min(2, n - 1):
                        flush_deferred(chains[0]["S"][0])

        KL_A = [(0, 0), (1, 1), (2, 2), (3, 3), (12, None), (13, None), (14, None), (15, None)]
        KL_B = [(4, 0), (5, 1), (6, 2), (7, 3)] + [(p, None) for p in (8, 9, 10, 11, 0, 1, 2, 3, 12, 13, 14, 15)]
        KL_A_D = [(0, 8), (1, 9), (2, 10), (3, 11), (12, 12), (13, 13), (14, 14), (15, 15)]
        KL_B_D = [(4, 8), (5, 9), (6, 10), (7, 11), (8, 12), (9, 13), (10, 14), (11, 15)] + \
                 [(p, 16) for p in (0, 1, 2, 3, 12, 13, 14, 15)]
        KL_A_S = [(3, 7), (2, 6), (1, 5), (0, 4), (15, None), (14, None), (13, None), (12, None)]
        KL_B_S = [(7, 7), (6, 6), (5, 5), (4, 4)] + [(p, None) for p in (11, 10, 9, 8, 3, 2, 1, 0, 15, 14, 13, 12)]
        KIND = ("fox", "sb", "diff", "dil")
        vdirty = [False, False]

        def proj_gen(u, buf, banks, overlapped):
            kind = KIND[u // 4]
            i = u % 4
            KTb, QTb, nQTb, Vt = KT[buf], QT[buf], nQT[buf], Vtok[buf]
            kq = [f"KT{buf}0", f"KT{buf}1"]
            qq = [f"QT{buf}0", f"QT{buf}1"]
            nq = [f"nQT{buf}0", f"nQT{buf}1"]
            vn = f"Vtok{buf}"
            wq, wqn = self.get_w(S, li, 3 * u + 0)
            wk, wkn = self.get_w(S, li, 3 * u + 1)
            wv, wvn = self.get_w(S, li, 3 * u + 2)
            aa = not overlapped
            bsel = [0]

            def nextbank():
                b = banks[bsel[0] % 2]
                bsel[0] += 1
                return b

            if kind != "diff" and vdirty[buf]:
                DVE(I("memset", Vt[:, :, :], 1.0), writes=[vn])
                vdirty[buf] = False
            if kind == "diff":
                vdirty[buf] = True

            def group(w, wn, tc, b):
                for kc in range(KC):
                    PE(I("matmul", ps[b][:, :], lhsT=w[:, kc, :], rhs=xT_bf[:, kc, tc * 512:(tc + 1) * 512],
                         start=(kc == 0), stop=(kc == KC - 1)),
                       reads=[wn, "xT_bf"], writes=[f"ps{b}"], sig=(kc == KC - 1))
                    if kc % 4 == 3:
                        yield

            for tc in range(2):
                b = nextbank()
                yield from group(wq, wqn, tc, b)
                p, pn = ps[b], f"ps{b}"
                cols = slice(tc * 512, (tc + 1) * 512)
                evac(QTb[0][0:64, cols], p[0:64, :], [pn], [qq[0]], scale=0.125, allow_act=aa)
                evac(QTb[1][0:64, cols], p[64:128, :], [pn], [qq[1]], scale=0.125, shifted=True)
                if kind == "sb":
                    evac(nQTb[0][0:64, cols], p[0:64, :], [pn], [nq[0]], scale=-0.125, allow_act=aa)
                    evac(nQTb[1][0:64, cols], p[64:128, :], [pn], [nq[1]], scale=-0.125, shifted=True)
            for hh in range(2):
                if kind == "fox":
                    h = 2 * i + hh
                    S.op("sp", I("dma_start", out=QTb[hh][64:65, :], in_=rneg[h:h + 1, :]), reads=["rneg"], writes=[qq[hh]],
                         dma=f"qrow{buf}{hh}")
                elif kind == "sb":
                    DVE(I("memset", QTb[hh][64:65, :], 0.0), writes=[qq[hh]])
                    DVE(I("memset", nQTb[hh][64:65, :], 0.0), writes=[nq[hh]])
                elif kind in ("diff", "dil"):
                    h = i if kind == "diff" else 4 + 2 * i + hh
                    S.op("pool", I("dma_start", out=QTb[hh][64:65, :], in_=self.rtab[h:h + 1, :]), writes=[qq[hh]],
                         dma=f"qrowp{buf}{hh}")
            if not overlapped:
                flush_deferred(banks[0])
            for tc in range(4):
                b = nextbank()
                yield from group(wk, wkn, tc, b)
                p, pn = ps[b], f"ps{b}"
                cols = slice(tc * 512, (tc + 1) * 512)
                evac(KTb[0][0:64, cols], p[0:64, :], [pn], [kq[0]], allow_act=aa)
                evac(KTb[1][0:64, cols], p[64:128, :], [pn], [kq[1]], shifted=True)
            for tc in range(4):
                b = nextbank()
                yield from group(wv, wvn, tc, b)
                p, pn = ps[b], f"ps{b}"
                cols = slice(tc * 512, (tc + 1) * 512)
                evac(VT[:, cols], p[:, :], [pn], ["VT"], allow_act=aa)
            for tg in range(4):
                b = nextbank()
                pbf = ps[b][:, :].bitcast(BF16)
                for t4 in range(4):
                    tb = tg * 4 + t4
                    PE(I("transpose", out=pbf[:, t4 * 128:(t4 + 1) * 128], in_=VT[:, tb * 128:(tb + 1) * 128], identity=ident),
                       reads=["VT", "cmb"], writes=[f"ps{b}"], sig=(t4 == 3))
                if kind == "diff":
                    DVE(I("tensor_copy", out=Vt[:, tg * 4:(tg + 1) * 4, 0:128],
                          in_=pbf[:, 0:512].rearrange("p (a b) -> p a b", a=4)), reads=[f"ps{b}"], writes=[vn])
                else:
                    DVE(I("tensor_copy", out=Vt[:, tg * 4:(tg + 1) * 4, :].rearrange("p a (h c) -> p a h c", h=2)[:, :, :, 0:64],
                          in_=pbf[:, 0:512].rearrange("p (a h c) -> p a h c", a=4, h=2)), reads=[f"ps{b}"], writes=[vn])
                yield

        def chain_gen(u, buf):
            kind = KIND[u // 4]
            i = u % 4
            KTb, QTb, nQTb, Vt = KT[buf], QT[buf], nQT[buf], Vtok[buf]
            kq = [f"KT{buf}0", f"KT{buf}1"]
            qq = [f"QT{buf}0", f"QT{buf}1"]
            nq = [f"nQT{buf}0", f"nQT{buf}1"]
            vn = f"Vtok{buf}"
            mixu = self.mixed[:, u, :]
            if kind in ("fox", "dil"):
                for slot in range(2):
                    chains = []
                    for hh in range(2):
                        if kind == "fox":
                            h = 2 * i + hh
                            bias = (lambda pos, h=h: self.gT[:, pos, h:h + 1])
                        else:
                            h = 4 + 2 * i + hh
                            bias = (lambda pos, h=h: cct[:, pos * 12 + h:pos * 12 + h + 1])
                        chains.append(dict(KT=KTb[hh], KTn=kq[hh], QT=QTb[hh], QTn=qq[hh], K=65, Vn=vn,
                                           qc=slice(slot * 512, (slot + 1) * 512), S=(2 * hh, 2 * hh + 1),
                                           E=(2 * hh, 2 * hh + 1), O=4 + hh, bias=bias,
                                           vl=(lambda pos, hh=hh: Vt[:, pos, hh * 128:(hh + 1) * 128])))
                    if kind == "fox":
                        kl = KL_A if slot == 0 else KL_B
                    else:
                        kl = KL_A_D if slot == 0 else KL_B_D
                    yield from softmax_chains(chains, kl)
                    for hh in range(2):
                        dst = mixu[hh * 64:(hh + 1) * 64, slot * 512:(slot + 1) * 512]
                        finish64(ps[4 + hh], f"ps{4 + hh}", hh, dst, "mixed")
            elif kind == "diff":
                h = i
                for slot in range(2):
                    chains = []
                    for hh in range(2):
                        chains.append(dict(KT=KTb[hh], KTn=kq[hh], QT=QTb[hh], QTn=qq[hh], K=65, Vn=vn,
                                           qc=slice(slot * 512, (slot + 1) * 512), S=(hh, hh),
                                           E=(2 * hh, 2 * hh + 1), O=4 + hh, Z=6 + hh,
                                           bias=(lambda pos, h=h: cct[:, pos * 12 + h:pos * 12 + h + 1]),
                                           vl=(lambda pos: Vt[:, pos, 0:128])))
                    yield from softmax_chains(chains, KL_A if slot == 0 else KL_B, lookahead=False)
                    rz0, rz1 = self.rz
                    zc0, zc1 = self.SS32
                    DVE(I("tensor_copy", out=rz0[:, :], in_=ps[4][:, :]), reads=["ps4"], writes=["rz0"])
                    ACT(I("copy", out=rz1[:, :], in_=ps[5][:, :]), reads=["ps5"], writes=["rz1"])
                    DVE(I("tensor_copy", out=zc0[:, :], in_=ps[6][:, :]), reads=["ps6"], writes=["SS320"])
                    ACT(I("copy", out=zc1[:, :], in_=ps[7][:, :]), reads=["ps7"], writes=["SS321"])
                    DVE(I("reciprocal", out=zc0[:, :], in_=zc0[:, :]), reads=["SS320"], writes=["SS320"])
                    DVE(I("reciprocal", out=zc1[:, :], in_=zc1[:, :]), reads=["SS321"], writes=["SS321"])
                    DVE(I("tensor_tensor", out=rz0[:, :], in0=rz0[:, :], in1=zc0[:, :], op=ALU.mult), reads=["rz0", "SS320"], writes=["rz0"])
                    DVE(I("tensor_tensor", out=rz1[:, :], in0=rz1[:, :], in1=zc1[:, :], op=ALU.mult), reads=["rz1", "SS321"], writes=["rz1"])
                    DVE(I("scalar_tensor_tensor", out=rz0[:, :], in0=rz1[:, :], scalar=neglam, in1=rz0[:, :],
                          op0=ALU.mult, op1=ALU.add), reads=["rz0", "rz1", "neglam"], writes=["rz0"])
                    DVE(I("tensor_tensor", out=zc0[:, :], in0=rz0[:, :], in1=rz0[:, :], op=ALU.mult), reads=["rz0"], writes=["SS320"])

                    def tail(bank, slot=slot, mixu=mixu, rz0=rz0, zc0=zc0, zc1=zc1):
                        PE(I("matmul", ps[bank][:, :], lhsT=ones32, rhs=zc0[:, :], start=True, stop=True),
                           reads=["SS320", "cm32"], writes=[f"ps{bank}"])
                        ACT(I("activation", out=zc1[:, :], in_=ps[bank][:, :], func=AF.Ln, bias=1e-5, scale=1.0 / 128.0),
                            reads=[f"ps{bank}"], writes=["SS321"])
                        ACT(I("activation", out=zc1[:, :], in_=zc1[:, :], func=AF.Exp, scale=-0.5), reads=["SS321"], writes=["SS321"])
                        DVE(I("tensor_tensor", out=rz0[:, :], in0=rz0[:, :], in1=zc1[:, :], op=ALU.mult),
                            reads=["rz0", "SS321"], writes=["rz0"])
                        DVE(I("tensor_scalar", out=mixu[:, slot * 512:(slot + 1) * 512], in0=rz0[:, :], scalar1=subs,
                              scalar2=None, op0=ALU.mult), reads=["rz0", "subs"], writes=["mixed"])

                    deferred.append(tail)
            else:
                spt, SSt = self.spt, self.SSt
                for slot in range(2):
                    kl = KL_A_S if slot == 0 else KL_B_S
                    n = len(kl)
                    qc = slice(slot * 512, (slot + 1) * 512)

                    def emit_z(hh, j):
                        pos, mask = kl[j]
                        zb = hh
                        PE(I("matmul", ps[zb][:, :], lhsT=KTb[hh][0:65, pos * 128:(pos + 1) * 128], rhs=QTb[hh][0:65, qc],
                             start=True, stop=(mask is None)),
                           reads=[kq[hh], qq[hh]], writes=[f"ps{zb}"], sig=(mask is None))
                        if mask is not None:
                            PE(I("matmul", ps[zb][:, :], lhsT=ident, rhs=self.maskt[:, mask, :], start=False, stop=True),
                               reads=["cmb", "maskt"], writes=[f"ps{zb}"])

                    def emit_softplus(j, hh):
                        pos, mask = kl[j]
                        p = j % 2
                        padb = cct[:, 320 + pos:321 + pos]
                        zb = hh
                        et = 2 * hh
                        ACT(I("activation", out=E[et][:, :], in_=ps[zb][:, :], func=AF.Exp, bias=padb, scale=1.0),
                            reads=[f"ps{zb}", "cct"], writes=[f"E{et}"])
                        ACT(I("activation", out=spt[:, p, hh, :], in_=E[et][:, :], func=AF.Ln, bias=1.0, scale=1.0),
                            reads=[f"E{et}"], writes=[f"sp{p}{hh}"])

                    def emit_T(j, hh):
                        pos, mask = kl[j]
                        p = j % 2
                        tb = 6 + hh
                        PE(I("matmul", ps[tb][:, :], lhsT=utri, rhs=spt[:, p, hh, :], start=True, stop=False),
                           reads=[f"sp{p}{hh}", "cmb"], writes=[f"ps{tb}"], sig=False)
                        if j > 0:
                            PE(I("matmul", ps[tb][:, :], lhsT=ones_b, rhs=SSt[:, p, hh, :], start=False, stop=False),
                               reads=[f"SSt{p}{hh}", "cmb"], writes=[f"ps{tb}"], sig=False)
                        PE(I("matmul", ps[tb][:, :], lhsT=KTb[hh][0:65, pos * 128:(pos + 1) * 128],
                             rhs=nQTb[hh][0:65, qc], start=False, stop=(mask is None)),
                           reads=[kq[hh], nq[hh]], writes=[f"ps{tb}"], sig=(mask is None))
                        if mask is not None:
                            PE(I("matmul", ps[tb][:, :], lhsT=negident, rhs=self.maskt[:, mask, :], start=False, stop=True),
                               reads=["cmb", "maskt"], writes=[f"ps{tb}"])

                    def emit_A(j, hh):
                        pos, mask = kl[j]
                        padb = cct[:, 320 + pos:321 + pos]
                        tb = 6 + hh
                        at = 2 * hh + 1
                        ACT(I("activation", out=E[at][:, :], in_=ps[tb][:, :], func=AF.Exp, bias=padb, scale=-1.0),
                            reads=[f"ps{tb}", "cct"], writes=[f"E{at}"])

                    def emit_PV(j, hh):
                        pos, mask = kl[j]
                        at = 2 * hh + 1
                        PE(I("matmul", ps[4 + hh][:, :], lhsT=Vt[:, pos, hh * 128:(hh + 1) * 128],
                             rhs=E[at][:, :], start=(j == 0), stop=(j == n - 1)),
                           reads=[f"E{at}", vn], writes=[f"ps{4 + hh}"])

                    for hh in range(2):
                        emit_z(hh, 0)
                    for hh in range(2):
                        emit_softplus(0, hh)
                    for j in range(n):
                        p = j % 2
                        for hh in range(2):
                            if j + 1 < n:
                                emit_z(hh, j + 1)
                            emit_T(j, hh)
                            emit_A(j, hh)
                        yield
                        if j + 1 < n:
                            for hh in range(2):
                                emit_softplus(j + 1, hh)
                        for hh in range(2):
                            emit_PV(j, hh)
                        if j + 1 < n:
                            q = (j + 1) % 2
                            for hh in range(2):
                                if j == 0:
                                    DVE(I("tensor_copy", out=SSt[:, q, hh, :], in_=spt[:, p, hh, :]), reads=[f"sp{p}{hh}"], writes=[f"SSt{q}{hh}"])
                                else:
                                    DVE(I("tensor_tensor", out=SSt[:, q, hh, :], in0=SSt[:, p, hh, :], in1=spt[:, p, hh, :], op=ALU.add),
                                        reads=[f"sp{p}{hh}", f"SSt{p}{hh}"], writes=[f"SSt{q}{hh}"])
                        yield
                    for hh in range(2):
                        dst = mixu[hh * 64:(hh + 1) * 64, qc]
                        DVE(I("tensor_copy", out=dst, in_=ps[4 + hh][0:64, :]), reads=[f"ps{4 + hh}"], writes=["mixed"])

        order = list(units)
        for _ in proj_gen(order[0], 0, (6, 7), False):
            pass
        emit_gT()
        for idx, u in enumerate(order):
            kind = KIND[u // 4]
            nxt = order[idx + 1] if idx + 1 < len(order) else None
            cg = chain_gen(u, idx % 2)
            if nxt is not None and OVERLAP:
                pg = proj_gen(nxt, (idx + 1) % 2, {"sb": (2, 3), "diff": (2, 3)}.get(kind, (6, 7)), True)
                r = 1 if kind == "sb" else 2
                for _ in cg:
                    for _k in range(r):
                        next(pg, None)
                for _ in pg:
                    pass
            else:
                for _ in cg:
                    pass
                if nxt is not None:
                    for _ in proj_gen(nxt, (idx + 1) % 2, (6, 7), False):
                        pass
        flush_deferred(0)
        if next_xres is not None:
            for t in range(48, 48 + NSLOT):
                self.pre_w[(li, t)] = self.load_w(S, li, t)
            for kc in range(KC):
                S.op("sp", I("dma_start", out=self.xres[:, kc, :], in_=next_xres[kc * 128:(kc + 1) * 128, :]),
                     writes=["xT_bf"], dma="xres")
            self.pre_xres = True
        if self.debug and li == len(self.layers) - 1 and self.debug == "mixed":
            for kc in range(KC):
                DVE(I("tensor_copy", out=self.tmp32[0][:, :], in_=self.mixed[:, kc, 0:512]), reads=["mixed"], writes=["dbgt0"])
                S.op("sp", I("dma_start", out=self.dbg[kc * 128:(kc + 1) * 128, 0:512], in_=self.tmp32[0][:, :]),
                     reads=["dbgt0"], dma="dbg0")
                DVE(I("tensor_copy", out=self.tmp32[1][:, :], in_=self.mixed[:, kc, 512:1024]), reads=["mixed"], writes=["dbgt1"])
                S.op("sp", I("dma_start", out=self.dbg[kc * 128:(kc + 1) * 128, 512:1024], in_=self.tmp32[1][:, :]),
                     reads=["dbgt1"], dma="dbg1")
        S.run()

    def phase_post(self, li, L, xres_src, last):
        S = self.new_sched()
        ps = self.ps
        ones_b = self.cmb[:, 3, :]
        ppt = self.ppt[li]
        cct = self.cct
        xres, mixed, x1bf, aT, xT_bf = self.xres, self.mixed, self.x1bf, self.aT, self.xT_bf
        tmp = self.tmp32
        tz = [self.view("R4", 16384 + i * 1024, [128, 512], BF16) for i in range(4)]
        tq = [self.view("R4", 16384 + 4096 + i * 1024, [128, 512], BF16) for i in range(4)]

        def PE(fn, reads=(), writes=(), sig=True):
            return S.op("pe", fn, reads, writes, sig)

        def ACT(fn, reads=(), writes=()):
            return S.op("act", fn, reads, writes)

        def DVE(fn, reads=(), writes=()):
            return S.op("dve", fn, reads, writes)

        def POOL(fn, reads=(), writes=()):
            return S.op("pool", fn, reads, writes)

        def xk(kc, half):
            return f"xres{kc}h{half}"

        if self.pre_xres:
            self.pre_xres = False
        else:
            for kc in range(KC):
                S.op("sp", I("dma_start", out=xres[:, kc, :], in_=xres_src[kc * 128:(kc + 1) * 128, :]),
                     writes=[xk(kc, 0), xk(kc, 1), "xres_ld"], dma="xres")
        bank = [0]

        def nb():
            b = bank[0] % 4
            bank[0] += 1
            return b

        stat_cnt = [0]
        pend = []

        def ln_stats(kc, half, first, last):
            cols = slice(half * 512, (half + 1) * 512)
            b1, b2 = 4 + 2 * half, 5 + 2 * half
            zi = stat_cnt[0] % 4
            stat_cnt[0] += 1
            z_, q_ = tz[zi], tq[zi]
            DVE(I("tensor_copy", out=z_[:, :], in_=xres[:, kc, cols]), reads=[xk(kc, half)], writes=[f"tz{zi}"])
            ACT(I("activation", out=q_[:, :], in_=xres[:, kc, cols], func=AF.Square), reads=[xk(kc, half)], writes=[f"tq{zi}"])
            PE(I("matmul", ps[b1][:, :], lhsT=ones_b, rhs=z_[:, :], start=first, stop=last),
               reads=[f"tz{zi}", "cmb"], writes=[f"ps{b1}"])
            PE(I("matmul", ps[b2][:, :], lhsT=ones_b, rhs=q_[:, :], start=first, stop=last),
               reads=[f"tq{zi}", "cmb"], writes=[f"ps{b2}"])

        def ln_finish(half):
            b1, b2 = 4 + 2 * half, 5 + 2 * half
            msq, lnv = tmp[half * 2], tmp[half * 2 + 1]
            mk, lk = f"tmp{half * 2}", f"tmp{half * 2 + 1}"
            ACT(I("activation", out=msq[:, :], in_=ps[b1][:, :], func=AF.Square, scale=1.0 / D), reads=[f"ps{b1}"], writes=[mk])
            ACT(I("activation", out=ps[b1][:, :], in_=ps[b1][:, :], func=AF.Identity, scale=1.0 / D), reads=[f"ps{b1}"], writes=[f"ps{b1}"])
            DVE(I("scalar_tensor_tensor", out=msq[:, :], in0=ps[b2][:, :], scalar=1.0 / D, in1=msq[:, :],
                  op0=ALU.mult, op1=ALU.subtract), reads=[f"ps{b2}", mk], writes=[mk])
            ACT(I("activation", out=lnv[:, :], in_=msq[:, :], func=AF.Ln, bias=1e-5, scale=1.0), reads=[mk], writes=[lk])
            ACT(I("activation", out=ps[b2][:, :], in_=lnv[:, :], func=AF.Exp, scale=-0.5), reads=[lk, f"ps{b2}"], writes=[f"ps{b2}"])

        def ln_normalize(half, goff, boff, write_bf, after_half=None):
            cols = slice(half * 512, (half + 1) * 512)
            bm, br = 4 + 2 * half, 5 + 2 * half
            for kc in range(KC):
                ti = 4 + (kc % 4)
                t = tmp[ti]
                tn = f"tmp{ti}"
                DVE(I("tensor_tensor", out=t[:, :], in0=xres[:, kc, cols], in1=ps[bm][:, :], op=ALU.subtract),
                    reads=[xk(kc, half), f"ps{bm}"], writes=[tn])
                DVE(I("tensor_tensor", out=t[:, :], in0=t[:, :], in1=ps[br][:, :], op=ALU.mult), reads=[tn, f"ps{br}"], writes=[tn])
                ACT(I("activation", out=xres[:, kc, cols], in_=t[:, :], func=AF.Identity,
                      bias=ppt[:, boff + kc:boff + kc + 1], scale=ppt[:, goff + kc:goff + kc + 1]),
                    reads=[tn, "ppt"], writes=[xk(kc, half)])
                if write_bf:
                    ACT(I("activation", out=x1bf[:, kc, cols], in_=t[:, :], func=AF.Identity,
                          bias=ppt[:, boff + kc:boff + kc + 1], scale=ppt[:, goff + kc:goff + kc + 1]),
                        reads=[tn, "ppt"], writes=[f"x1bf{half}"])
            if after_half is not None:
                after_half(half)

        def layer_norm_tail(goff, boff, write_bf, prefetch=(), after_half=None):
            ln_finish(0)
            ln_finish(1)
            for t in prefetch:
                wq_pref.append((t, self.load_w(S, li, t)))
            ln_normalize(0, goff, boff, write_bf, after_half)
            ln_normalize(1, goff, boff, write_bf, after_half)

        wbase = 48
        for oc in range(16):
            w, wn = self.get_w(S, li, wbase + oc)
            for half in range(2):
                cols = slice(half * 512, (half + 1) * 512)
                b = nb()
                for kc in range(KC):
                    PE(I("matmul", ps[b][:, :], lhsT=w[:, kc, :], rhs=mixed[:, kc, cols], start=(kc == 0), stop=(kc == KC - 1)),
                       reads=[wn, "mixed"], writes=[f"ps{b}"], sig=(kc == KC - 1))
                DVE(I("scalar_tensor_tensor", out=xres[:, oc, cols], in0=xres[:, oc, cols], scalar=ALPHA, in1=ps[b][:, :],
                      op0=ALU.mult, op1=ALU.add), reads=[f"ps{b}", xk(oc, half), "xres_ld"], writes=[xk(oc, half)])
                pend.append((oc, half, oc == 0, oc == 15))
                if len(pend) > 3:
                    ln_stats(*pend.pop(0))

        wq_pref = []

        def next_w(t):
            if wq_pref and wq_pref[0][0] == t:
                return wq_pref.pop(0)[1]
            return self.load_w(S, li, t)

        while pend:
            ln_stats(*pend.pop(0))
        layer_norm_tail(0, 16, True, prefetch=[64 + i for i in range(NSLOT)])

        wbase = 64
        for qf in range(4):
            for fcl in range(16):
                w, wn = next_w(wbase + qf * 32 + fcl)
                for half in range(2):
                    cols = slice(half * 512, (half + 1) * 512)
                    b = nb()
                    for kc in range(KC):
                        PE(I("matmul", ps[b][:, :], lhsT=w[:, kc, :], rhs=x1bf[:, kc, cols], start=(kc == 0), stop=(kc == KC - 1)),
                           reads=[wn, f"x1bf{half}"], writes=[f"ps{b}"], sig=(kc == KC - 1))
                    t = tmp[4 + b]
                    ACT(I("activation", out=t[:, :], in_=ps[b][:, :], func=AF.Relu), reads=[f"ps{b}"], writes=[f"tmp{4 + b}"])
                    DVE(I("tensor_tensor", out=aT[:, fcl, cols], in0=ps[b][:, :], in1=t[:, :], op=ALU.mult),
                        reads=[f"ps{b}", f"tmp{4 + b}"], writes=[f"aT{fcl}h{half}"])
            for c in range(16):
                w, wn = next_w(wbase + qf * 32 + 16 + c)
                for half in range(2):
                    cols = slice(half * 512, (half + 1) * 512)
                    b = nb()
                    for fcl in range(16):
                        PE(I("matmul", ps[b][:, :], lhsT=w[:, fcl, :], rhs=aT[:, fcl, cols], start=(fcl == 0), stop=(fcl == 15)),
                           reads=[wn, f"aT{fcl}h{half}"], writes=[f"ps{b}"], sig=(fcl == 15))
                    DVE(I("scalar_tensor_tensor", out=xres[:, c, cols], in0=xres[:, c, cols], scalar=(ALPHA if qf == 0 else 1.0),
                          in1=ps[b][:, :], op0=ALU.mult, op1=ALU.add), reads=[f"ps{b}", xk(c, half)], writes=[xk(c, half)])
                    if qf == 3:
                        pend.append((c, half, c == 0, c == 15))
                        if len(pend) > 3:
                            ln_stats(*pend.pop(0))
        while pend:
            ln_stats(*pend.pop(0))
        if last:
            def out_half(half):
                cols = slice(half * 512, (half + 1) * 512)
                for kc in range(KC):
                    S.op("sp", I("dma_start", out=self.yT[kc * 128:(kc + 1) * 128, cols], in_=xres[:, kc, cols]),
                         reads=[xk(kc, half)], dma="yout")
            layer_norm_tail(32, 48, False, after_half=out_half)
        else:
            def xchg_half(half):
                cols = slice(half * 512, (half + 1) * 512)
                for kc in range(KC):
                    S.op("sp", I("dma_start", out=self.xsave[kc * 128:(kc + 1) * 128, cols], in_=xres[:, kc, cols]),
                         reads=[xk(kc, half)], writes=["xsave"], dma="yout")
                S.op("sp", I("dma_start", out=self.xs[half].ap().rearrange("(a p) c -> p a c", p=128),
                             in_=x1bf[:, :, cols]), reads=[f"x1bf{half}"], writes=[f"xs{half}"], dma=f"xch{half}")
                S.op("pool", I("collective_compute", "AllGather", ALU.bypass,
                               replica_groups=[[0, 1], [2, 3], [4, 5], [6, 7]],
                               ins=[self.xs[half].ap().opt()], outs=[self.xg[half].ap().opt()]),
                     reads=[f"xs{half}"], writes=[f"xg{half}"])
            layer_norm_tail(32, 48, True, after_half=xchg_half)
            for kc in range(KC):
                if kc % 2 == 0:
                    DVE(I("tensor_copy", out=xT_bf[:, kc, 0:1024], in_=x1bf[:, kc, :]), reads=["x1bf0", "x1bf1"],
                        writes=[xk(kc, 0), xk(kc, 1)])
                else:
                    ACT(I("copy", out=xT_bf[:, kc, 0:1024], in_=x1bf[:, kc, :]), reads=["x1bf0", "x1bf1"],
                        writes=[xk(kc, 0), xk(kc, 1)])
            fsel = cct[:, 352:353]
            gsel = cct[:, 353:354]
            stg = [[self.view(reg, i * 16384, [128, 16, 512], BF16).rearrange("p (a r) c -> p a r c", r=2) for i in range(2)]
                   for reg in ("R2", "R4")]
            tmpb = [self.view("R5", i * 1024, [128, 512], BF16) for i in range(4)]
            for r in range(2):
                for i in range(2):
                    for rk in range(2):
                        src = self.xg[i].ap().rearrange("(r a p) c -> r p a c", r=2, p=128)[rk][:, r * 8:(r + 1) * 8]
                        S.op("sp", I("dma_start", out=stg[r][i][:, :, rk, :], in_=src), reads=[f"xg{i}"],
                             writes=[f"stg{r}{i}"] + ([f"tz{j}" for j in range(4)] + [f"tq{j}" for j in range(4)] + [f"tmp{j}" for j in range(8)]
                                                      if r == 1 else [f"aT{j}h{h}" for j in range(16) for h in range(2)]),
                             dma=f"xch{2 + r}")
            cnt = 0
            for r in range(2):
                sA, sB = stg[r]
                for k8 in range(8):
                    kc = r * 8 + k8
                    for (s1, s0, c0) in ((sA, sB, 1024), (sB, sA, 1536)):
                        ti = cnt % 4
                        cnt += 1
                        ACT(I("activation", out=tmpb[ti][:, :], in_=s1[:, k8, 1, :], func=AF.Identity, scale=fsel),
                            reads=[f"stg{r}0", f"stg{r}1", "cct"], writes=[f"tmpb{ti}"] + (["x1bf0", "x1bf1"] if cnt <= 4 else []))
                        DVE(I("scalar_tensor_tensor", out=xT_bf[:, kc, c0:c0 + 512], in0=s0[:, k8, 0, :], scalar=gsel, in1=tmpb[ti][:, :],
                              op0=ALU.mult, op1=ALU.add), reads=[f"stg{r}0", f"stg{r}1", f"tmpb{ti}", "cct"], writes=[xk(kc, 0), xk(kc, 1)])
        if not last:
            for t in range(3):
                self.pre_w[(li + 1, t)] = self.load_w(S, li + 1, t)
        S.run()


def build_single_layer(L, debug=False, units=range(16), do_post=True):
    p = Prog([L], debug=debug)
    p.phase_clear()
    p.phase_setup()
    p.phase_attn(0, L, True, units=units, next_xres=(p.xT[:, 0:1024] if do_post else None))
    if do_post:
        p.phase_post(0, L, p.xT[:, 0:1024], True)
    return p.nc


def build_fused():
    p = Prog([0, 1])
    p.phase_clear()
    p.phase_setup()
    p.phase_attn(0, 0, True, next_xres=p.xT[:, 0:1024])
    p.phase_post(0, 0, p.xT[:, 0:1024], False)
    p.phase_attn(1, 1, False, next_xres=p.xsave)
    p.phase_post(1, 1, p.xsave, True)
    return p.nc


_CACHE = {}


def _run_layer(L, x, inp, debug=False):
    if ("nc", L, debug) not in _CACHE:
        _CACHE[("nc", L, debug)] = build_single_layer(L, debug)
    nc = _CACHE[("nc", L, debug)]
    W, wfz = _layer_weights(inp["w_in"][L], inp["w_out"][L], inp["w_mlp_in"][L], inp["w_mlp_out"][L])
    pp = _layer_params(L, inp)
    masks = _mask_tables()
    cmat = _const_mats()
    in_maps = []
    for c in range(8):
        b, half = divmod(c, 2)
        tp = _tokperm(half)
        cc, rtab = _core_consts(half)
        xT = np.ascontiguousarray(x[b][tp, :].T)
        in_maps.append({"xT": xT, "W0": W, "wfz0": wfz, "pp0": pp, "cc": cc, "rtab": rtab, "masks": masks, "cmat": cmat})
    res = run_bass_kernel_spmd(nc, in_maps, core_ids=list(range(8)))
    out = np.empty_like(x)
    dbg = None
    if debug:
        dbg = np.empty((4, 2048, 2048), np.float32)
    for c in range(8):
        b, half = divmod(c, 2)
        tp = _tokperm(half)
        out[b][tp[0:1024], :] = res.results[c]["yT"].T
        if debug:
            dbg[b][tp[0:1024], :] = res.results[c]["dbg"].T
    return (out, dbg) if debug else out


def kernel_unfused(**inputs):
    inp = {k: np.asarray(v) for k, v in inputs.items()}
    x = np.ascontiguousarray(inp["x"], dtype=np.float32)
    for L in range(DEPTH):
        x = _run_layer(L, x, inp)
    return x


def kernel(**inputs):
    inp = {k: np.asarray(v) for k, v in inputs.items()}
    x = np.ascontiguousarray(inp["x"], dtype=np.float32)
    if "fused" not in _CACHE:
        _CACHE["fused"] = build_fused()
    nc = _CACHE["fused"]
    shared = {"masks": _mask_tables(), "cmat": _const_mats()}
    for L in range(DEPTH):
        W, wfz = _layer_weights(inp["w_in"][L], inp["w_out"][L], inp["w_mlp_in"][L], inp["w_mlp_out"][L])
        shared[f"W{L}"] = W
        shared[f"wfz{L}"] = wfz
        shared[f"pp{L}"] = _layer_params(L, inp)
    in_maps = []
    for c in range(8):
        b, half = divmod(c, 2)
        tp = _tokperm(half)
        cc, rtab = _core_consts(half)
        m = dict(shared)
        m.update({"xT": np.ascontiguousarray(x[b][tp, :].T), "cc": cc, "rtab": rtab})
        in_maps.append(m)
    res = run_bass_kernel_spmd(nc, in_maps, core_ids=list(range(8)))
    out = np.empty_like(x)
    for c in range(8):
        b, half = divmod(c, 2)
        tp = _tokperm(half)
        out[b][tp[0:1024], :] = res.results[c]["yT"].T
    return out
```

```python
import numpy as np
import concourse.bass as bass
import concourse.mybir as mybir
from concourse.bass_utils import run_bass_kernel_spmd

F32 = mybir.dt.float32
BF16 = mybir.dt.bfloat16
AF = mybir.ActivationFunctionType
ALU = mybir.AluOpType
AX = mybir.AxisListType

D = 2048
SEQ = 2048
KC = 16
DEPTH = 2
NEG = -30000.0
ALPHA = (2 * DEPTH) ** 0.25
NWT = 192
NSLOT = 3
OVERLAP = True
GROUPS = ([0, 2, 1, 3], [1, 3, 2, 0])
N_MASK = 17
CC_COLS = 192 + 128 + 16 + 16 + 2
PP_COLS = 64 + 1 + 256 + 1


def I(name, *args, **kw):
    return (name, args, kw)


class Tok:
    __slots__ = ("sem", "val", "eng")

    def __init__(self, sem, val, eng):
        self.sem, self.val, self.eng = sem, val, eng


class Sched:
    COMPUTE = ("pe", "act", "dve", "pool")

    def __init__(self, nc, sems, dsems):
        self.nc = nc
        self.sem = sems
        self.cnt = sems["_cnt"]
        self.dsem = dsems
        self.dcnt = dsems["_cnt"]
        self.pending = {e: [] for e in self.COMPUTE}
        self.streams = {e: [] for e in ("pe", "act", "dve", "pool", "sp")}
        self.res = {}
        self.dma_toks = []

    def _r(self, key):
        r = self.res.get(key)
        if r is None:
            r = self.res[key] = [None, []]
        return r

    def op(self, eng, fn, reads=(), writes=(), sig=True, dma=None):
        assert sig or eng == "pe"
        deps = []
        for k in reads:
            r = self._r(k)
            if r[0] is not None:
                deps.append(r[0])
        for k in writes:
            r = self._r(k)
            if r[0] is not None:
                deps.append(r[0])
            deps.extend(r[1])
        if dma is None and eng == "pe":
            deps = [d for d in deps if d.eng != "pe"]
        if dma is not None:
            assert dma in self.dsem, dma
            self.dcnt[dma] += 16
            tok = Tok(self.dsem[dma], self.dcnt[dma], "dma")
            inc = (self.dsem[dma], 16)
            self.dma_toks.append(tok)
        else:
            tok = Tok(self.sem[eng], None, eng)
            self.pending[eng].append(tok)
            inc = None
            if sig:
                self.cnt[eng] += 1
                for t in self.pending[eng]:
                    t.val = self.cnt[eng]
                self.pending[eng] = []
                inc = (self.sem[eng], 1)
        self.streams[eng].append((deps, fn, inc))
        for k in reads:
            self._r(k)[1].append(tok)
        for k in writes:
            r = self._r(k)
            r[0] = tok
            r[1] = []
        return tok

    def run(self):
        for e in self.COMPUTE:
            assert not self.pending[e], f"unsignalled tail on {e}"
        self.streams["sp"].append((list(self.dma_toks), None, None))
        with self.nc.Block() as block:
            binder = {"pe": block.tensor, "act": block.scalar, "dve": block.vector,
                      "pool": block.gpsimd, "sp": block.sync}
            for e, stream in self.streams.items():
                if not stream:
                    continue

                def body(engine, stream=stream):
                    waited = {}
                    for deps, fn, inc in stream:
                        need = {}
                        for d in deps:
                            assert d.val is not None
                            key = id(d.sem)
                            if waited.get(key, 0) >= d.val:
                                continue
                            if key not in need or need[key][1] < d.val:
                                need[key] = (d.sem, d.val)
                        for key, (s, v) in need.items():
                            engine.wait_ge(s, v)
                            waited[key] = v
                        if fn is None:
                            continue
                        inst = getattr(engine, fn[0])(*fn[1], **fn[2])
                        if inc is not None:
                            inst.then_inc(inc[0], inc[1])

                binder[e](body)


def _slopes():
    n = 12
    return np.exp2(-8.0 * np.arange(1, n + 1, dtype=np.float64) / n)


def _tokperm(half):
    return np.concatenate([np.arange(512) + 512 * g for g in GROUPS[half]])


def _mask_tables():
    sl = np.arange(128)[:, None]
    tl = np.arange(512)[None, :]
    m = np.zeros((N_MASK, 128, 512), np.float32)
    for i in range(4):
        m[i] = np.where(128 * i + sl <= tl, 0.0, NEG)
        m[4 + i] = np.where(128 * i + sl < tl, 0.0, NEG)

    def dil(rel):
        d = 128 * rel + tl - sl
        mult = ((d >= 0) & (d <= 128)).astype(np.int64) + ((d >= 0) & (d % 4 == 0) & (d <= 512)) \
            + ((d >= 0) & (d % 16 == 0) & (d <= 2048))
        return np.where(mult > 0, np.log(np.maximum(mult, 1)), NEG).astype(np.float32)

    for i in range(4):
        m[8 + i] = dil(-i)
    for r in range(1, 5):
        m[12 + (4 - r)] = dil(r)
    m[16] = dil(8)
    return m


def _const_mats():
    c = np.zeros((4, 128, 128), np.float32)
    c[0] = np.eye(128)
    c[1] = -np.eye(128)
    kk = np.arange(128)[:, None]
    mm = np.arange(128)[None, :]
    c[2] = (kk >= mm).astype(np.float32)
    c[3] = 1.0
    return c


def _core_consts(half):
    tp = _tokperm(half)
    s_nat = tp.reshape(16, 128).T.astype(np.float64)
    pad = np.zeros(16)
    if half == 0:
        pad[12:16] = NEG
    sl = _slopes()
    cc = np.zeros((128, CC_COLS), np.float32)
    biasK = s_nat[:, :, None] * sl[None, None, :] + pad[None, :, None]
    cc[:, 0:192] = biasK.reshape(128, 192)
    cc[:, 192:320] = np.repeat(pad[None, :, None], 8, axis=2).repeat(128, axis=0).reshape(128, 128)
    cc[:, 320:336] = pad[None, :]
    nat = GROUPS[half]
    m = np.zeros((4, 4))
    for g in range(4):
        for g2 in range(4):
            m[g, g2] = 1.0 if nat[g2] < nat[g] else 0.0
    cc[:, 336:352] = m.reshape(1, 16)
    cc[:, 352] = 1.0 if half == 0 else 0.0
    cc[:, 353] = 0.0 if half == 0 else 1.0
    rtab = np.zeros((14, 1024), np.float32)
    rtab[0:12] = (-sl[:, None] * tp[None, 0:1024].astype(np.float64)).astype(np.float32)
    rtab[12:14] = np.repeat(pad, 128).reshape(2, 1024)
    return cc, rtab


IN_OFF = {"fq": 0, "fk": 512, "fv": 1024, "fz": 1536, "sq": 1544, "sk": 2056, "sv": 2568,
          "dq": 3080, "dk": 3592, "dv": 4104, "gq": 4616, "gk": 5128, "gv": 5640}
MIX = (("fq", "fk", "fv"), ("sq", "sk", "sv"), ("dq", "dk", "dv"), ("gq", "gk", "gv"))


def _wtile(w_cols):
    return np.ascontiguousarray(w_cols.reshape(16, 128, 128).transpose(1, 0, 2).reshape(128, 2048))


def _layer_weights(w_in, w_out, w1, w2):
    W = np.empty((NWT, 128, 2048), np.float32)
    t = 0
    for u in range(16):
        m, i = divmod(u, 4)
        for nm in MIX[m]:
            c0 = IN_OFF[nm] + i * 128
            W[t] = _wtile(w_in[:, c0:c0 + 128])
            t += 1
    for oc in range(16):
        W[t] = _wtile(w_out[:, oc * 128:(oc + 1) * 128])
        t += 1
    for qf in range(4):
        for fcl in range(16):
            fc = qf * 16 + fcl
            W[t] = _wtile(w1[:, fc * 128:(fc + 1) * 128])
            t += 1
        for c in range(16):
            W[t] = _wtile(w2[qf * 2048:(qf + 1) * 2048, c * 128:(c + 1) * 128])
            t += 1
    assert t == NWT
    wfz = np.ascontiguousarray(
        w_in[:, 1536:1544].reshape(16, 128, 8).transpose(1, 0, 2).reshape(128, 128))
    return W, wfz


def _layer_params(l, inp):
    pp = np.zeros((128, PP_COLS), np.float32)
    pp[:, 0:16] = inp["ln1_gain"][l].reshape(16, 128).T
    pp[:, 16:32] = inp["ln1_bias"][l].reshape(16, 128).T
    pp[:, 32:48] = inp["ln2_gain"][l].reshape(16, 128).T
    pp[:, 48:64] = inp["ln2_bias"][l].reshape(16, 128).T
    pp[:, 64] = inp["diff_subln_gain"][l]
    pp[:, 65:129] = inp["diff_lambda_q1"][l][None, :]
    pp[:, 129:193] = inp["diff_lambda_k1"][l][None, :]
    pp[:, 193:257] = inp["diff_lambda_q2"][l][None, :]
    pp[:, 257:321] = inp["diff_lambda_k2"][l][None, :]
    pp[0:8, 321] = inp["fox_forget_bias"][l]
    return pp


class Prog:
    def __init__(self, layers, debug=False):
        self.layers = layers
        self.debug = debug
        nc = self.nc = bass.Bass("TRN2", target_bir_lowering=False)
        nL = len(layers)
        self.xT = nc.dram_tensor("xT", [D, SEQ], F32, kind="ExternalInput").ap()
        self.W = [nc.dram_tensor(f"W{i}", [NWT, 128, 2048], F32, kind="ExternalInput").ap() for i in range(nL)]
        self.wfz = [nc.dram_tensor(f"wfz{i}", [128, 128], F32, kind="ExternalInput").ap() for i in range(nL)]
        self.pp = [nc.dram_tensor(f"pp{i}", [128, PP_COLS], F32, kind="ExternalInput").ap() for i in range(nL)]
        self.cc = nc.dram_tensor("cc", [128, CC_COLS], F32, kind="ExternalInput").ap()
        self.rtab = nc.dram_tensor("rtab", [14, 1024], F32, kind="ExternalInput").ap()
        self.masks = nc.dram_tensor("masks", [N_MASK, 128, 512], F32, kind="ExternalInput").ap()
        self.cmat = nc.dram_tensor("cmat", [4, 128, 128], F32, kind="ExternalInput").ap()
        self.yT = nc.dram_tensor("yT", [D, 1024], F32, kind="ExternalOutput").ap()
        if debug:
            self.dbg = nc.dram_tensor("dbg", [D, 1024], F32, kind="ExternalOutput").ap()
        if nL > 1:
            self.xsave = nc.dram_tensor("xsave", [D, 1024], F32).ap()
            self.xs = [nc.dram_tensor(f"xs{i}", [D, 512], BF16) for i in range(2)]
            self.xg = [nc.dram_tensor(f"xg{i}", [2 * D, 512], BF16) for i in range(2)]
        self.sems = {e: nc.alloc_semaphore(f"sem_{e}") for e in Sched.COMPUTE}
        self.sems["_cnt"] = {e: 0 for e in Sched.COMPUTE}
        self.dsems = {"_cnt": {}}
        for nm in ["c0", "c1", "xt", "xres", "yout", "dbg0", "dbg1", "qrow00", "qrow01", "qrow10", "qrow11", "qrowp00", "qrowp01", "qrowp10", "qrowp11", "xch0", "xch1", "xch2", "xch3"] + \
                [f"wr{i}" for i in range(NSLOT)]:
            self.dsems[nm] = nc.alloc_semaphore(f"dsem_{nm}")
            self.dsems["_cnt"][nm] = 0
        self.pst = nc.alloc_psum_tensor("pst", [128, 8, 512], F32)
        self.ps = [self.pst[:, i, :] for i in range(8)]
        R = self.R = {}
        R["R1"] = nc.alloc_sbuf_tensor("R1", [128, 16384], F32)
        R["R2"] = nc.alloc_sbuf_tensor("R2", [128, 8192], F32)
        R["R4"] = nc.alloc_sbuf_tensor("R4", [128, 8192], F32)
        R["R5"] = nc.alloc_sbuf_tensor("R5", [128, 8192], F32)
        self.wr = [nc.alloc_sbuf_tensor(f"wr{i}", [128, 16, 128], BF16) for i in range(NSLOT)]
        self.maskt = nc.alloc_sbuf_tensor("maskt", [128, N_MASK, 512], BF16)
        self.cmb = nc.alloc_sbuf_tensor("cmb", [128, 4, 128], BF16)
        self.cm32 = nc.alloc_sbuf_tensor("cm32", [128, 4, 128], F32)
        self.cct = nc.alloc_sbuf_tensor("cct", [128, CC_COLS], F32)
        self.ppt = [nc.alloc_sbuf_tensor(f"ppt{i}", [128, PP_COLS], F32) for i in range(nL)]
        self.gT = nc.alloc_sbuf_tensor("gT", [128, 16, 8], F32)
        self.small = nc.alloc_sbuf_tensor("small", [128, 160], F32)

        def view(reg, boff, shape, dt):
            n = int(np.prod(shape[1:]))
            esz = 4 if dt == F32 else 2
            assert boff % 4 == 0 and (n * esz) % 4 == 0
            ap = R[reg][:, boff // 4: boff // 4 + (n * esz) // 4]
            if dt != F32:
                ap = ap.bitcast(dt)
            if len(shape) == 3:
                ap = ap.rearrange("p (a b) -> p a b", a=shape[1])
            return ap

        self.view = view
        self.xT_bf = view("R1", 0, [128, 16, 2048], BF16)
        self.xres = view("R1", 0, [128, 16, 1024], F32)
        self.mixed = view("R2", 0, [128, 16, 1024], BF16)
        self.aT = view("R2", 0, [128, 16, 1024], BF16)
        o = 0
        KT0 = [view("R4", o + i * 4096, [128, 2048], BF16) for i in range(2)]
        o += 8192
        QT0 = [view("R4", o + i * 2048, [128, 1024], BF16) for i in range(2)]
        o += 4096
        nQT0 = [view("R4", o + i * 2048, [128, 1024], BF16) for i in range(2)]
        o += 4096
        self.VT = view("R4", o, [128, 2048], BF16)
        o += 4096
        Vtok0 = view("R4", o, [128, 16, 256], BF16)
        o += 8192
        self.E = [view("R4", o + i * 1024, [128, 512], BF16) for i in range(4)]
        self.Et = view("R4", o, [128, 4, 512], BF16)
        o += 4096
        assert o == 32768
        self.tmp32 = [view("R4", i * 2048, [128, 512], F32) for i in range(8)]
        self.fz = view("R5", 0, [128, 2048], F32)
        self.grow = view("R5", 8192, [128, 2048], F32)
        self.ones8 = view("R2", 0, [128, 512], F32)
        self.spt = view("R5", 0, [128, 4, 512], BF16).rearrange("p (a h) c -> p a h c", a=2)
        nQT1 = [view("R5", 4096 + i * 2048, [128, 1024], BF16) for i in range(2)]
        KT1 = [view("R5", 8192 + i * 4096, [128, 2048], BF16) for i in range(2)]
        self.rneg = view("R5", 16384, [128, 1024], BF16)
        self.SS32 = [view("R5", 18432 + i * 2048, [128, 512], F32) for i in range(2)]
        self.SSt = view("R5", 22528, [128, 4, 512], BF16).rearrange("p (a h) c -> p a h c", a=2)
        qt1b = nc.alloc_sbuf_tensor("qt1b", [128, 1024], BF16)
        QT1 = [view("R5", 26624, [128, 1024], BF16), qt1b[:, :]]
        self.rz = [view("R5", 28672 + i * 2048, [128, 512], F32) for i in range(2)]
        vtok1 = nc.alloc_sbuf_tensor("vtok1", [128, 16, 256], BF16)
        self.KT = [KT0, KT1]
        self.QT = [QT0, QT1]
        self.nQT = [nQT0, nQT1]
        self.Vtok = [Vtok0, vtok1[:, :, :]]
        self.x1bf = view("R5", 0, [128, 16, 1024], BF16)
        self.wt = 0
        self.pre_w = {}
        self.pre_xres = False

    def new_sched(self):
        return Sched(self.nc, self.sems, self.dsems)

    def get_w(self, S, L, t):
        if (L, t) in self.pre_w:
            return self.pre_w.pop((L, t))
        return self.load_w(S, L, t)

    def load_w(self, S, L, t):
        slot = self.wt % NSLOT
        self.wt += 1
        dst = self.wr[slot]
        src = self.W[L][t].rearrange("p (a b) -> p a b", a=16)
        S.op("pool", I("dma_start", out=dst[:], in_=src), writes=[f"wr{slot}"], dma=f"wr{slot}")
        return dst, f"wr{slot}"

    def phase_clear(self):
        sems = [self.sems[e] for e in Sched.COMPUTE] + [v for k, v in self.dsems.items() if k != "_cnt"]
        with self.nc.Block() as block:
            def body(engine):
                for s_ in sems:
                    engine.sem_clear(s_)
            block.gpsimd(body)

    def phase_setup(self):
        S = self.new_sched()
        S.op("pool", I("dma_start", out=self.maskt[:], in_=self.masks.rearrange("m p t -> p m t")),
             writes=["maskt"], dma="c0")
        S.op("pool", I("dma_start", out=self.cmb[:], in_=self.cmat.rearrange("m p t -> p m t")),
             writes=["cmb"], dma="c0")
        S.op("sp", I("dma_start", out=self.cm32[:], in_=self.cmat.rearrange("m p t -> p m t")),
             writes=["cm32"], dma="c1")
        S.op("sp", I("dma_start", out=self.cct[:], in_=self.cc), writes=["cct"], dma="c1")
        for i in range(len(self.layers)):
            S.op("sp", I("dma_start", out=self.ppt[i][:], in_=self.pp[i]), writes=["ppt"], dma="c1")
        S.run()

    def phase_attn(self, li, L, first, units=range(16), next_xres=None):
        nc = self.nc
        S = self.new_sched()
        ps = self.ps
        ident = self.cmb[:, 0, :]
        negident = self.cmb[:, 1, :]
        utri = self.cmb[:, 2, :]
        ones_b = self.cmb[:, 3, :]
        ident32 = self.cm32[:, 0, :]
        ones32 = self.cm32[:, 3, :]
        cct, ppt = self.cct, self.ppt[li]
        KT, QT, nQT, VT, Vtok, E = self.KT, self.QT, self.nQT, self.VT, self.Vtok, self.E
        xT_bf = self.xT_bf
        sm = self.small
        lam_init = 0.8 - 0.6 * float(np.exp(-0.3 * L))

        def PE(fn, reads=(), writes=(), sig=True):
            return S.op("pe", fn, reads, writes, sig)

        def ACT(fn, reads=(), writes=()):
            return S.op("act", fn, reads, writes)

        def DVE(fn, reads=(), writes=()):
            return S.op("dve", fn, reads, writes)

        if first:
            for kc in range(KC):
                S.op("pool", I("dma_start", out=xT_bf[:, kc, :], in_=self.xT[kc * 128:(kc + 1) * 128, :]),
                     writes=["xT_bf"], dma="xt")
        for bb in range(2):
            for i in range(2):
                DVE(I("memset", KT[bb][i][64:65, :], 1.0), writes=[f"KT{bb}{i}"])
            DVE(I("memset", Vtok[bb][:, :, :], 1.0), writes=[f"Vtok{bb}"])

        DVE(I("tensor_tensor", out=sm[:, 64:128], in0=ppt[:, 65:129], in1=ppt[:, 129:193], op=ALU.mult),
            reads=["ppt"], writes=["sm_a"])
        DVE(I("tensor_reduce", out=sm[:, 1:2], in_=sm[:, 64:128], axis=AX.X, op=ALU.add), reads=["sm_a"], writes=["sm_b"])
        DVE(I("tensor_tensor", out=sm[:, 64:128], in0=ppt[:, 193:257], in1=ppt[:, 257:321], op=ALU.mult),
            reads=["ppt", "sm_b"], writes=["sm_a"])
        DVE(I("tensor_reduce", out=sm[:, 2:3], in_=sm[:, 64:128], axis=AX.X, op=ALU.add), reads=["sm_a"], writes=["sm_c"])
        ACT(I("activation", out=sm[:, 4:5], in_=sm[:, 1:2], func=AF.Exp), reads=["sm_b"], writes=["sm_d"])
        ACT(I("activation", out=sm[:, 5:6], in_=sm[:, 2:3], func=AF.Exp), reads=["sm_c"], writes=["sm_e"])
        DVE(I("scalar_tensor_tensor", out=sm[:, 6:7], in0=sm[:, 5:6], scalar=-lam_init, in1=sm[:, 4:5],
                                             op0=ALU.add, op1=ALU.subtract), reads=["sm_d", "sm_e"], writes=["neglam"])
        DVE(I("tensor_scalar", out=sm[:, 7:8], in0=ppt[:, 64:65], scalar1=(1.0 - lam_init), scalar2=None, op0=ALU.mult),
            reads=["ppt"], writes=["subs"])
        DVE(I("tensor_scalar", out=sm[:, 8:9], in0=ppt[:, 321:322], scalar1=-1.0, scalar2=None, op0=ALU.mult),
            reads=["ppt"], writes=["negbf"])
        neglam = sm[:, 6:7]
        subs = sm[:, 7:8]
        negbf = sm[0:8, 8:9]

        def evac(out, in_, reads, writes, scale=None, shifted=False, allow_act=True):
            use_act = allow_act and (not shifted) and (evac_toggle[0] % 2 == 1)
            evac_toggle[0] += 1
            if use_act:
                if scale is None:
                    ACT(I("copy", out=out, in_=in_), reads, writes)
                else:
                    ACT(I("activation", out=out, in_=in_, func=AF.Identity, scale=scale), reads, writes)
            else:
                if scale is None:
                    DVE(I("tensor_copy", out=out, in_=in_), reads, writes)
                else:
                    DVE(I("tensor_scalar", out=out, in0=in_, scalar1=scale, scalar2=None, op0=ALU.mult), reads, writes)

        evac_toggle = [0]

        wfz_t = self.view("R4", 0, [128, 16, 8], BF16)
        S.op("pool", I("dma_start", out=wfz_t, in_=self.wfz[li].rearrange("p (a b) -> p a b", a=16)),
             writes=["KT00"], dma="c0")
        fz, grow, rneg = self.fz, self.grow, self.rneg
        for tc in range(4):
            b = 6 + (tc % 2)
            for kc in range(KC):
                PE(I("matmul", ps[b][0:8, :], lhsT=wfz_t[:, kc, :],
                                                               rhs=xT_bf[:, kc, tc * 512:(tc + 1) * 512],
                                                               start=(kc == 0), stop=(kc == KC - 1)),
                   reads=["KT00", "xT_bf"], writes=[f"ps{b}"], sig=(kc == KC - 1))
            ACT(I("activation", out=fz[0:8, tc * 512:(tc + 1) * 512], in_=ps[b][0:8, :], func=AF.Exp,
                                                   bias=negbf, scale=-1.0), reads=[f"ps{b}", "negbf"], writes=["fz"])
        ACT(I("activation", out=fz[0:8, :], in_=fz[0:8, :], func=AF.Ln, bias=1.0, scale=1.0), reads=["fz"], writes=["fz"])
        DVE(I("memset", KT[0][0][64:65, :], 1.0), reads=[], writes=["KT00"])
        DVE(I("memset", self.ones8[0:8, :], 1.0), writes=["ones8"])
        for g in range(4):
            DVE(I("tensor_tensor_scan", out=grow[0:8, g * 512:(g + 1) * 512], data0=self.ones8[0:8, :],
                                                    data1=fz[0:8, g * 512:(g + 1) * 512], initial=0.0,
                                                    op0=ALU.mult, op1=ALU.add), reads=["fz", "ones8"], writes=["grow"])
        off = sm[0:8, 16:20]
        DVE(I("memset", sm[0:8, 16:20], 0.0), writes=["off"])
        for g in range(4):
            for g2 in range(4):
                DVE(I("scalar_tensor_tensor",
                    out=sm[0:8, 16 + g:17 + g], in0=grow[0:8, g2 * 512 + 511:g2 * 512 + 512],
                    scalar=cct[0:8, 336 + g * 4 + g2:337 + g * 4 + g2], in1=sm[0:8, 16 + g:17 + g],
                    op0=ALU.mult, op1=ALU.add), reads=["grow", "cct", "off"], writes=["off"])
        for g in range(4):
            DVE(I("tensor_scalar", out=grow[0:8, g * 512:(g + 1) * 512], in0=grow[0:8, g * 512:(g + 1) * 512],
                                               scalar1=sm[0:8, 16 + g:17 + g], scalar2=None, op0=ALU.add),
                reads=["off", "grow"], writes=["grow"])
        DVE(I("tensor_scalar", out=rneg[0:8, :], in0=grow[0:8, 0:1024], scalar1=-1.0, scalar2=None, op0=ALU.mult),
            reads=["grow"], writes=["rneg"])
        for blk in range(16):
            PE(I("transpose", out=ps[6][:, blk * 8:(blk + 1) * 8], in_=grow[0:8, blk * 128:(blk + 1) * 128],
                                              identity=ident32[0:8, 0:8]), reads=["grow", "cm32"], writes=["ps6"])
        DVE(I("tensor_tensor", out=self.gT[:, :, :].rearrange("p a b -> p (a b)"), in0=ps[6][:, 0:128],
                                      in1=cct[:, 192:320], op=ALU.add), reads=["ps6", "cct"], writes=["gT"])

        deferred = []

        def flush_deferred(bank):
            while deferred:
                deferred.pop(0)(bank)

        def finish64(O, Oname, rzi, dst, dname):
            rz = self.rz[rzi]
            zc = self.SS32[rzi]
            ACT(I("copy", out=rz[0:64, :], in_=O[0:64, :]), reads=[Oname], writes=[f"rz{rzi}"])
            DVE(I("tensor_copy", out=zc[0:64, :], in_=O[64:128, :]), reads=[Oname], writes=[f"SS32{rzi}"])
            DVE(I("reciprocal", out=zc[0:64, :], in_=zc[0:64, :]), reads=[f"SS32{rzi}"], writes=[f"SS32{rzi}"])
            DVE(I("tensor_tensor", out=dst, in0=rz[0:64, :], in1=zc[0:64, :], op=ALU.mult),
                reads=[f"rz{rzi}", f"SS32{rzi}"], writes=[dname])

        def softmax_chains(chains, klist, lookahead=True):
            n = len(klist)

            def emitS(c, j):
                pos, mask = klist[j]
                sb = c["S"][j % 2]
                kk = c["K"]
                PE(I("matmul", ps[sb][:, :], lhsT=c["KT"][0:kk, pos * 128:(pos + 1) * 128], rhs=c["QT"][0:kk, c["qc"]],
                     start=True, stop=(mask is None)),
                   reads=[c["KTn"], c["QTn"]], writes=[f"ps{sb}"], sig=(mask is None))
                if mask is not None:
                    PE(I("matmul", ps[sb][:, :], lhsT=ident, rhs=self.maskt[:, mask, :], start=False, stop=True),
                       reads=["cmb", "maskt"], writes=[f"ps{sb}"])

            def emitExp(c, j):
                pos = klist[j][0]
                sb = c["S"][j % 2]
                et = c["E"][j % 2]
                ACT(I("activation", out=E[et][:, :], in_=ps[sb][:, :], func=AF.Exp, bias=c["bias"](pos), scale=1.0),
                    reads=[f"ps{sb}", "gT", "cct"], writes=[f"E{et}"])

            def emitPV(c, j):
                pos = klist[j][0]
                et = c["E"][j % 2]
                PE(I("matmul", ps[c["O"]][:, :], lhsT=c["vl"](pos), rhs=E[et][:, :], start=(j == 0), stop=(j == n - 1)),
                   reads=[f"E{et}", c["Vn"]], writes=[f"ps{c['O']}"], sig=("Z" not in c))
                if "Z" in c:
                    PE(I("matmul", ps[c["Z"]][:, :], lhsT=ones_b, rhs=E[et][:, :], start=(j == 0), stop=(j == n - 1)),
                       reads=[f"E{et}", "cmb"], writes=[f"ps{c['Z']}"])

            if lookahead:
                for c in chains:
                    emitS(c, 0)
                for j in range(n):
                    for c in chains:
                        if j + 1 < n:
                            emitS(c, j + 1)
                    for c in chains:
                        emitExp(c, j)
                    for c in chains:
                        emitPV(c, j)
                    if j == min(2, n - 1):
                        flush_deferred(chains[0]["S"][j % 2])
                    yield
            else:
                for j in range(n):
                    for c in chains:
                        emitS(c, j)
                    for c in chains:
                        emitExp(c, j)
                    yield
                    for c in chains:
                        emitPV(c, j)
                    if j == min(2, n - 1):
                        flush_deferred(chains[0]["S"][0])

        KL_A = [(0, 0), (1, 1), (2, 2), (3, 3), (12, None), (13, None), (14, None), (15, None)]
        KL_B = [(4, 0), (5, 1), (6, 2), (7, 3)] + [(p, None) for p in (8, 9, 10, 11, 0, 1, 2, 3, 12, 13, 14, 15)]
        KL_A_D = [(0, 8), (1, 9), (2, 10), (3, 11), (12, 12), (13, 13), (14, 14), (15, 15)]
        KL_B_D = [(4, 8), (5, 9), (6, 10), (7, 11), (8, 12), (9, 13), (10, 14), (11, 15)] + \
                 [(p, 16) for p in (0, 1, 2, 3, 12, 13, 14, 15)]
        KL_A_S = [(3, 7), (2, 6), (1, 5), (0, 4), (15, None), (14, None), (13, None), (12, None)]
        KL_B_S = [(7, 7), (6, 6), (5, 5), (4, 4)] + [(p, None) for p in (11, 10, 9, 8, 3, 2, 1, 0, 15, 14, 13, 12)]
        KIND = ("fox", "sb", "diff", "dil")
        vdirty = [False, False]
        kdirty = [False, False]

        def proj_gen(u, buf, banks, overlapped):
            kind = KIND[u // 4]
            i = u % 4
            KTb, QTb, nQTb, Vt = KT[buf], QT[buf], nQT[buf], Vtok[buf]
            kq = [f"KT{buf}0", f"KT{buf}1"]
            qq = [f"QT{buf}0", f"QT{buf}1"]
            nq = [f"nQT{buf}0", f"nQT{buf}1"]
            vn = f"Vtok{buf}"
            wq, wqn = self.get_w(S, li, 3 * u + 0)
            wk, wkn = self.get_w(S, li, 3 * u + 1)
            wv, wvn = self.get_w(S, li, 3 * u + 2)
            aa = not overlapped
            bsel = [0]

            def nextbank():
                b = banks[bsel[0] % 2]
                bsel[0] += 1
                return b

            if kind == "sb":
                kdirty[buf] = True
            elif kdirty[buf]:
                for hh in range(2):
                    DVE(I("memset", KTb[hh][64:65, :], 1.0), writes=[kq[hh]])
                kdirty[buf] = False
            if kind != "diff" and vdirty[buf]:
                DVE(I("memset", Vt[:, :, :], 1.0), writes=[vn])
                vdirty[buf] = False
            if kind == "diff":
                vdirty[buf] = True

            def group(w, wn, tc, b):
                for kc in range(KC):
                    PE(I("matmul", ps[b][:, :], lhsT=w[:, kc, :], rhs=xT_bf[:, kc, tc * 512:(tc + 1) * 512],
                         start=(kc == 0), stop=(kc == KC - 1)),
                       reads=[wn, "xT_bf"], writes=[f"ps{b}"], sig=(kc == KC - 1))
                    if kc % 4 == 3:
                        yield

            for tc in range(2):
                b = nextbank()
                yield from group(wq, wqn, tc, b)
                p, pn = ps[b], f"ps{b}"
                cols = slice(tc * 512, (tc + 1) * 512)
                evac(QTb[0][0:64, cols], p[0:64, :], [pn], [qq[0]], scale=0.125, allow_act=aa)
                evac(QTb[1][0:64, cols], p[64:128, :], [pn], [qq[1]], scale=0.125, shifted=True)
                if kind == "sb":
                    evac(nQTb[0][0:64, cols], p[0:64, :], [pn], [nq[0]], scale=-0.125, allow_act=aa)
                    evac(nQTb[1][0:64, cols], p[64:128, :], [pn], [nq[1]], scale=-0.125, shifted=True)
            for hh in range(2):
                if kind == "fox":
                    h = 2 * i + hh
                    S.op("sp", I("dma_start", out=QTb[hh][64:65, :], in_=rneg[h:h + 1, :]), reads=["rneg"], writes=[qq[hh]],
                         dma=f"qrow{buf}{hh}")
                elif kind == "sb":
                    DVE(I("memset", QTb[hh][64:65, :], 1.0), writes=[qq[hh]])
                    DVE(I("memset", nQTb[hh][64:65, :], -1.0), writes=[nq[hh]])
                elif kind in ("diff", "dil"):
                    h = i if kind == "diff" else 4 + 2 * i + hh
                    S.op("pool", I("dma_start", out=QTb[hh][64:65, :], in_=self.rtab[h:h + 1, :]), writes=[qq[hh]],
                         dma=f"qrowp{buf}{hh}")
            if not overlapped:
                flush_deferred(banks[0])
            for tc in range(4):
                b = nextbank()
                yield from group(wk, wkn, tc, b)
                p, pn = ps[b], f"ps{b}"
                cols = slice(tc * 512, (tc + 1) * 512)
                evac(KTb[0][0:64, cols], p[0:64, :], [pn], [kq[0]], allow_act=aa)
                evac(KTb[1][0:64, cols], p[64:128, :], [pn], [kq[1]], shifted=True)
            if kind == "sb":
                for hh in range(2):
                    S.op("pool", I("dma_start", out=KTb[hh][64:65, :],
                                   in_=self.rtab[12:14, :].rearrange("(o a) b -> o (a b)", o=1)), writes=[kq[hh]],
                         dma=f"qrowp{buf}{hh}")
            for tc in range(4):
                b = nextbank()
                yield from group(wv, wvn, tc, b)
                p, pn = ps[b], f"ps{b}"
                cols = slice(tc * 512, (tc + 1) * 512)
                evac(VT[:, cols], p[:, :], [pn], ["VT"], allow_act=aa)
            for tg in range(4):
                b = nextbank()
                pbf = ps[b][:, :].bitcast(BF16)
                for t4 in range(4):
                    tb = tg * 4 + t4
                    PE(I("transpose", out=pbf[:, t4 * 128:(t4 + 1) * 128], in_=VT[:, tb * 128:(tb + 1) * 128], identity=ident),
                       reads=["VT", "cmb"], writes=[f"ps{b}"], sig=(t4 == 3))
                if kind == "diff":
                    DVE(I("tensor_copy", out=Vt[:, tg * 4:(tg + 1) * 4, 0:128],
                          in_=pbf[:, 0:512].rearrange("p (a b) -> p a b", a=4)), reads=[f"ps{b}"], writes=[vn])
                else:
                    DVE(I("tensor_copy", out=Vt[:, tg * 4:(tg + 1) * 4, :].rearrange("p a (h c) -> p a h c", h=2)[:, :, :, 0:64],
                          in_=pbf[:, 0:512].rearrange("p (a h c) -> p a h c", a=4, h=2)), reads=[f"ps{b}"], writes=[vn])
                yield

        def chain_gen(u, buf):
            kind = KIND[u // 4]
            i = u % 4
            KTb, QTb, nQTb, Vt = KT[buf], QT[buf], nQT[buf], Vtok[buf]
            kq = [f"KT{buf}0", f"KT{buf}1"]
            qq = [f"QT{buf}0", f"QT{buf}1"]
            nq = [f"nQT{buf}0", f"nQT{buf}1"]
            vn = f"Vtok{buf}"
            mixu = self.mixed[:, u, :]
            if kind in ("fox", "dil"):
                for slot in range(2):
                    chains = []
                    for hh in range(2):
                        if kind == "fox":
                            h = 2 * i + hh
                            bias = (lambda pos, h=h: self.gT[:, pos, h:h + 1])
                        else:
                            h = 4 + 2 * i + hh
                            bias = (lambda pos, h=h: cct[:, pos * 12 + h:pos * 12 + h + 1])
                        chains.append(dict(KT=KTb[hh], KTn=kq[hh], QT=QTb[hh], QTn=qq[hh], K=65, Vn=vn,
                                           qc=slice(slot * 512, (slot + 1) * 512), S=(2 * hh, 2 * hh + 1),
                                           E=(2 * hh, 2 * hh + 1), O=4 + hh, bias=bias,
                                           vl=(lambda pos, hh=hh: Vt[:, pos, hh * 128:(hh + 1) * 128])))
                    if kind == "fox":
                        kl = KL_A if slot == 0 else KL_B
                    else:
                        kl = KL_A_D if slot == 0 else KL_B_D
                    yield from softmax_chains(chains, kl)
                    for hh in range(2):
                        dst = mixu[hh * 64:(hh + 1) * 64, slot * 512:(slot + 1) * 512]
                        finish64(ps[4 + hh], f"ps{4 + hh}", hh, dst, "mixed")
            elif kind == "diff":
                h = i
                for slot in range(2):
                    chains = []
                    for hh in range(2):
                        chains.append(dict(KT=KTb[hh], KTn=kq[hh], QT=QTb[hh], QTn=qq[hh], K=65, Vn=vn,
                                           qc=slice(slot * 512, (slot + 1) * 512), S=(hh, hh),
                                           E=(2 * hh, 2 * hh + 1), O=4 + hh, Z=6 + hh,
                                           bias=(lambda pos, h=h: cct[:, pos * 12 + h:pos * 12 + h + 1]),
                                           vl=(lambda pos: Vt[:, pos, 0:128])))
                    yield from softmax_chains(chains, KL_A if slot == 0 else KL_B, lookahead=False)
                    rz0, rz1 = self.rz
                    zc0, zc1 = self.SS32
                    DVE(I("tensor_copy", out=rz0[:, :], in_=ps[4][:, :]), reads=["ps4"], writes=["rz0"])
                    ACT(I("copy", out=rz1[:, :], in_=ps[5][:, :]), reads=["ps5"], writes=["rz1"])
                    DVE(I("tensor_copy", out=zc0[:, :], in_=ps[6][:, :]), reads=["ps6"], writes=["SS320"])
                    ACT(I("copy", out=zc1[:, :], in_=ps[7][:, :]), reads=["ps7"], writes=["SS321"])
                    DVE(I("reciprocal", out=zc0[:, :], in_=zc0[:, :]), reads=["SS320"], writes=["SS320"])
                    DVE(I("reciprocal", out=zc1[:, :], in_=zc1[:, :]), reads=["SS321"], writes=["SS321"])
                    DVE(I("tensor_tensor", out=rz0[:, :], in0=rz0[:, :], in1=zc0[:, :], op=ALU.mult), reads=["rz0", "SS320"], writes=["rz0"])
                    DVE(I("tensor_tensor", out=rz1[:, :], in0=rz1[:, :], in1=zc1[:, :], op=ALU.mult), reads=["rz1", "SS321"], writes=["rz1"])
                    DVE(I("scalar_tensor_tensor", out=rz0[:, :], in0=rz1[:, :], scalar=neglam, in1=rz0[:, :],
                          op0=ALU.mult, op1=ALU.add), reads=["rz0", "rz1", "neglam"], writes=["rz0"])
                    DVE(I("tensor_tensor", out=zc0[:, :], in0=rz0[:, :], in1=rz0[:, :], op=ALU.mult), reads=["rz0"], writes=["SS320"])

                    def tail(bank, slot=slot, mixu=mixu, rz0=rz0, zc0=zc0, zc1=zc1):
                        PE(I("matmul", ps[bank][:, :], lhsT=ones32, rhs=zc0[:, :], start=True, stop=True),
                           reads=["SS320", "cm32"], writes=[f"ps{bank}"])
                        ACT(I("activation", out=zc1[:, :], in_=ps[bank][:, :], func=AF.Ln, bias=1e-5, scale=1.0 / 128.0),
                            reads=[f"ps{bank}"], writes=["SS321"])
                        ACT(I("activation", out=zc1[:, :], in_=zc1[:, :], func=AF.Exp, scale=-0.5), reads=["SS321"], writes=["SS321"])
                        DVE(I("tensor_tensor", out=rz0[:, :], in0=rz0[:, :], in1=zc1[:, :], op=ALU.mult),
                            reads=["rz0", "SS321"], writes=["rz0"])
                        DVE(I("tensor_scalar", out=mixu[:, slot * 512:(slot + 1) * 512], in0=rz0[:, :], scalar1=subs,
                              scalar2=None, op0=ALU.mult), reads=["rz0", "subs"], writes=["mixed"])

                    deferred.append(tail)
            else:
                spt, SSt = self.spt, self.SSt
                for slot in range(2):
                    kl = KL_A_S if slot == 0 else KL_B_S
                    n = len(kl)
                    qc = slice(slot * 512, (slot + 1) * 512)

                    def emit_z(hh, j):
                        pos, mask = kl[j]
                        zb = hh
                        PE(I("matmul", ps[zb][:, :], lhsT=KTb[hh][0:65, pos * 128:(pos + 1) * 128], rhs=QTb[hh][0:65, qc],
                             start=True, stop=(mask is None)),
                           reads=[kq[hh], qq[hh]], writes=[f"ps{zb}"], sig=(mask is None))
                        if mask is not None:
                            PE(I("matmul", ps[zb][:, :], lhsT=ident, rhs=self.maskt[:, mask, :], start=False, stop=True),
                               reads=["cmb", "maskt"], writes=[f"ps{zb}"])

                    def emit_softplus(j, hh):
                        pos, mask = kl[j]
                        p = j % 2
                        padb = cct[:, 320 + pos:321 + pos]
                        zb = hh
                        et = 2 * hh
                        ACT(I("activation", out=E[et][:, :], in_=ps[zb][:, :], func=AF.Exp),
                            reads=[f"ps{zb}"], writes=[f"E{et}"])
                        ACT(I("activation", out=spt[:, p, hh, :], in_=E[et][:, :], func=AF.Ln, bias=1.0, scale=1.0),
                            reads=[f"E{et}"], writes=[f"sp{p}{hh}"])

                    def emit_T(j, hh):
                        pos, mask = kl[j]
                        p = j % 2
                        tb = 6 + hh
                        PE(I("matmul", ps[tb][:, :], lhsT=utri, rhs=spt[:, p, hh, :], start=True, stop=False),
                           reads=[f"sp{p}{hh}", "cmb"], writes=[f"ps{tb}"], sig=False)
                        if j > 0:
                            PE(I("matmul", ps[tb][:, :], lhsT=ones_b, rhs=SSt[:, p, hh, :], start=False, stop=False),
                               reads=[f"SSt{p}{hh}", "cmb"], writes=[f"ps{tb}"], sig=False)
                        PE(I("matmul", ps[tb][:, :], lhsT=KTb[hh][0:65, pos * 128:(pos + 1) * 128],
                             rhs=nQTb[hh][0:65, qc], start=False, stop=(mask is None)),
                           reads=[kq[hh], nq[hh]], writes=[f"ps{tb}"], sig=(mask is None))
                        if mask is not None:
                            PE(I("matmul", ps[tb][:, :], lhsT=negident, rhs=self.maskt[:, mask, :], start=False, stop=True),
                               reads=["cmb", "maskt"], writes=[f"ps{tb}"])

                    def emit_A(j, hh):
                        pos, mask = kl[j]
                        padb = cct[:, 320 + pos:321 + pos]
                        tb = 6 + hh
                        at = 2 * hh + 1
                        ACT(I("activation", out=E[at][:, :], in_=ps[tb][:, :], func=AF.Exp, scale=-1.0),
                            reads=[f"ps{tb}"], writes=[f"E{at}"])

                    def emit_PV(j, hh):
                        pos, mask = kl[j]
                        at = 2 * hh + 1
                        PE(I("matmul", ps[4 + hh][:, :], lhsT=Vt[:, pos, hh * 128:(hh + 1) * 128],
                             rhs=E[at][:, :], start=(j == 0), stop=(j == n - 1)),
                           reads=[f"E{at}", vn], writes=[f"ps{4 + hh}"])

                    for hh in range(2):
                        emit_z(hh, 0)
                    for hh in range(2):
                        emit_softplus(0, hh)
                    for j in range(n):
                        p = j % 2
                        for hh in range(2):
                            if j + 1 < n:
                                emit_z(hh, j + 1)
                            emit_T(j, hh)
                            emit_A(j, hh)
                        yield
                        if j + 1 < n:
                            for hh in range(2):
                                emit_softplus(j + 1, hh)
                        for hh in range(2):
                            emit_PV(j, hh)
                        if j + 1 < n:
                            q = (j + 1) % 2
                            for hh in range(2):
                                if j == 0:
                                    DVE(I("tensor_copy", out=SSt[:, q, hh, :], in_=spt[:, p, hh, :]), reads=[f"sp{p}{hh}"], writes=[f"SSt{q}{hh}"])
                                else:
                                    DVE(I("tensor_tensor", out=SSt[:, q, hh, :], in0=SSt[:, p, hh, :], in1=spt[:, p, hh, :], op=ALU.add),
                                        reads=[f"sp{p}{hh}", f"SSt{p}{hh}"], writes=[f"SSt{q}{hh}"])
                        yield
                    for hh in range(2):
                        dst = mixu[hh * 64:(hh + 1) * 64, qc]
                        DVE(I("tensor_copy", out=dst, in_=ps[4 + hh][0:64, :]), reads=[f"ps{4 + hh}"], writes=["mixed"])

        order = list(units)
        for _ in proj_gen(order[0], 0, (6, 7), False):
            pass
        for idx, u in enumerate(order):
            kind = KIND[u // 4]
            nxt = order[idx + 1] if idx + 1 < len(order) else None
            cg = chain_gen(u, idx % 2)
            if nxt is not None and OVERLAP:
                pg = proj_gen(nxt, (idx + 1) % 2, {"sb": (2, 3), "diff": (2, 3)}.get(kind, (6, 7)), True)
                r = 1 if kind == "sb" else 2
                for _ in cg:
                    for _k in range(r):
                        next(pg, None)
                for _ in pg:
                    pass
            else:
                for _ in cg:
                    pass
                if nxt is not None:
                    for _ in proj_gen(nxt, (idx + 1) % 2, (6, 7), False):
                        pass
        flush_deferred(0)
        if next_xres is not None:
            for t in range(48, 48 + NSLOT):
                self.pre_w[(li, t)] = self.load_w(S, li, t)
            for kc in range(KC):
                S.op("sp", I("dma_start", out=self.xres[:, kc, :], in_=next_xres[kc * 128:(kc + 1) * 128, :]),
                     writes=["xT_bf"], dma="xres")
            self.pre_xres = True
        if self.debug and li == len(self.layers) - 1 and self.debug == "mixed":
            for kc in range(KC):
                DVE(I("tensor_copy", out=self.tmp32[0][:, :], in_=self.mixed[:, kc, 0:512]), reads=["mixed"], writes=["dbgt0"])
                S.op("sp", I("dma_start", out=self.dbg[kc * 128:(kc + 1) * 128, 0:512], in_=self.tmp32[0][:, :]),
                     reads=["dbgt0"], dma="dbg0")
                DVE(I("tensor_copy", out=self.tmp32[1][:, :], in_=self.mixed[:, kc, 512:1024]), reads=["mixed"], writes=["dbgt1"])
                S.op("sp", I("dma_start", out=self.dbg[kc * 128:(kc + 1) * 128, 512:1024], in_=self.tmp32[1][:, :]),
                     reads=["dbgt1"], dma="dbg1")
        S.run()

    def phase_post(self, li, L, xres_src, last):
        S = self.new_sched()
        ps = self.ps
        ones_b = self.cmb[:, 3, :]
        ppt = self.ppt[li]
        cct = self.cct
        xres, mixed, x1bf, aT, xT_bf = self.xres, self.mixed, self.x1bf, self.aT, self.xT_bf
        tmp = self.tmp32
        tz = [self.view("R4", 16384 + i * 1024, [128, 512], BF16) for i in range(4)]
        tq = [self.view("R4", 16384 + 4096 + i * 1024, [128, 512], BF16) for i in range(4)]

        def PE(fn, reads=(), writes=(), sig=True):
            return S.op("pe", fn, reads, writes, sig)

        def ACT(fn, reads=(), writes=()):
            return S.op("act", fn, reads, writes)

        def DVE(fn, reads=(), writes=()):
            return S.op("dve", fn, reads, writes)

        def POOL(fn, reads=(), writes=()):
            return S.op("pool", fn, reads, writes)

        def xk(kc, half):
            return f"xres{kc}h{half}"

        if self.pre_xres:
            self.pre_xres = False
        else:
            for kc in range(KC):
                S.op("sp", I("dma_start", out=xres[:, kc, :], in_=xres_src[kc * 128:(kc + 1) * 128, :]),
                     writes=[xk(kc, 0), xk(kc, 1), "xres_ld"], dma="xres")
        bank = [0]

        def nb():
            b = bank[0] % 4
            bank[0] += 1
            return b

        wbase = 48
        for oc in range(16):
            w, wn = self.get_w(S, li, wbase + oc)
            for half in range(2):
                cols = slice(half * 512, (half + 1) * 512)
                b = nb()
                for kc in range(KC):
                    PE(I("matmul", ps[b][:, :], lhsT=w[:, kc, :], rhs=mixed[:, kc, cols], start=(kc == 0), stop=(kc == KC - 1)),
                       reads=[wn, "mixed"], writes=[f"ps{b}"], sig=(kc == KC - 1))
                DVE(I("scalar_tensor_tensor", out=xres[:, oc, cols], in0=xres[:, oc, cols], scalar=ALPHA, in1=ps[b][:, :],
                      op0=ALU.mult, op1=ALU.add), reads=[f"ps{b}", xk(oc, half), "xres_ld"], writes=[xk(oc, half)])

        wq_pref = []

        def next_w(t):
            if wq_pref and wq_pref[0][0] == t:
                return wq_pref.pop(0)[1]
            return self.load_w(S, li, t)

        def layer_norm(goff, boff, write_bf, prefetch=(), after_half=None):
            stat = {}
            for half in range(2):
                cols = slice(half * 512, (half + 1) * 512)
                b1, b2, bm, br = (4, 5, 6, 7) if half == 0 else (0, 1, 2, 3)
                for kc in range(KC):
                    z_ = tz[(half * 2 + kc) % 4]
                    q_ = tq[(half * 2 + kc) % 4]
                    zi, qi = (half * 2 + kc) % 4, (half * 2 + kc) % 4
                    DVE(I("tensor_copy", out=z_[:, :], in_=xres[:, kc, cols]), reads=[xk(kc, half)], writes=[f"tz{zi}"])
                    ACT(I("activation", out=q_[:, :], in_=xres[:, kc, cols], func=AF.Square), reads=[xk(kc, half)], writes=[f"tq{qi}"])
                    PE(I("matmul", ps[b1][:, :], lhsT=ones_b, rhs=z_[:, :], start=(kc == 0), stop=(kc == KC - 1)),
                       reads=[f"tz{zi}", "cmb"], writes=[f"ps{b1}"])
                    PE(I("matmul", ps[b2][:, :], lhsT=ones_b, rhs=q_[:, :], start=(kc == 0), stop=(kc == KC - 1)),
                       reads=[f"tq{qi}", "cmb"], writes=[f"ps{b2}"])
                if half == 0:
                    for t in prefetch:
                        wq_pref.append((t, self.load_w(S, li, t)))
                msq, lnv = tmp[half * 2], tmp[half * 2 + 1]
                mk, lk = f"tmp{half * 2}", f"tmp{half * 2 + 1}"
                ACT(I("activation", out=ps[bm][:, :], in_=ps[b1][:, :], func=AF.Identity, scale=1.0 / D), reads=[f"ps{b1}"], writes=[f"ps{bm}"])
                ACT(I("activation", out=msq[:, :], in_=ps[b1][:, :], func=AF.Square, scale=1.0 / D), reads=[f"ps{b1}"], writes=[mk])
                DVE(I("scalar_tensor_tensor", out=msq[:, :], in0=ps[b2][:, :], scalar=1.0 / D, in1=msq[:, :],
                      op0=ALU.mult, op1=ALU.subtract), reads=[f"ps{b2}", mk], writes=[mk])
                ACT(I("activation", out=lnv[:, :], in_=msq[:, :], func=AF.Ln, bias=1e-5, scale=1.0), reads=[mk], writes=[lk])
                ACT(I("activation", out=ps[br][:, :], in_=lnv[:, :], func=AF.Exp, scale=-0.5), reads=[lk], writes=[f"ps{br}"])
                stat[half] = (bm, br)
            for half in range(2):
                cols = slice(half * 512, (half + 1) * 512)
                bm, br = stat[half]
                for kc in range(KC):
                    ti = 4 + (kc % 4)
                    t = tmp[ti]
                    tn = f"tmp{ti}"
                    DVE(I("tensor_tensor", out=t[:, :], in0=xres[:, kc, cols], in1=ps[bm][:, :], op=ALU.subtract),
                        reads=[xk(kc, half), f"ps{bm}"], writes=[tn])
                    DVE(I("tensor_tensor", out=t[:, :], in0=t[:, :], in1=ps[br][:, :], op=ALU.mult), reads=[tn, f"ps{br}"], writes=[tn])
                    ACT(I("activation", out=xres[:, kc, cols], in_=t[:, :], func=AF.Identity,
                          bias=ppt[:, boff + kc:boff + kc + 1], scale=ppt[:, goff + kc:goff + kc + 1]),
                        reads=[tn, "ppt"], writes=[xk(kc, half)])
                    if write_bf:
                        ACT(I("activation", out=x1bf[:, kc, cols], in_=t[:, :], func=AF.Identity,
                              bias=ppt[:, boff + kc:boff + kc + 1], scale=ppt[:, goff + kc:goff + kc + 1]),
                            reads=[tn, "ppt"], writes=[f"x1bf{half}"])
                if after_half is not None:
                    after_half(half)

        layer_norm(0, 16, True, prefetch=[64 + i for i in range(NSLOT)])

        wbase = 64
        for qf in range(4):
            for fcl in range(16):
                w, wn = next_w(wbase + qf * 32 + fcl)
                for half in range(2):
                    cols = slice(half * 512, (half + 1) * 512)
                    b = nb()
                    for kc in range(KC):
                        PE(I("matmul", ps[b][:, :], lhsT=w[:, kc, :], rhs=x1bf[:, kc, cols], start=(kc == 0), stop=(kc == KC - 1)),
                           reads=[wn, f"x1bf{half}"], writes=[f"ps{b}"], sig=(kc == KC - 1))
                    t = tmp[4 + b]
                    ACT(I("activation", out=t[:, :], in_=ps[b][:, :], func=AF.Relu), reads=[f"ps{b}"], writes=[f"tmp{4 + b}"])
                    DVE(I("tensor_tensor", out=aT[:, fcl, cols], in0=ps[b][:, :], in1=t[:, :], op=ALU.mult),
                        reads=[f"ps{b}", f"tmp{4 + b}"], writes=[f"aT{fcl}h{half}"])
            for c in range(16):
                w, wn = next_w(wbase + qf * 32 + 16 + c)
                for half in range(2):
                    cols = slice(half * 512, (half + 1) * 512)
                    b = nb()
                    for fcl in range(16):
                        PE(I("matmul", ps[b][:, :], lhsT=w[:, fcl, :], rhs=aT[:, fcl, cols], start=(fcl == 0), stop=(fcl == 15)),
                           reads=[wn, f"aT{fcl}h{half}"], writes=[f"ps{b}"], sig=(fcl == 15))
                    DVE(I("scalar_tensor_tensor", out=xres[:, c, cols], in0=xres[:, c, cols], scalar=(ALPHA if qf == 0 else 1.0),
                          in1=ps[b][:, :], op0=ALU.mult, op1=ALU.add), reads=[f"ps{b}", xk(c, half)], writes=[xk(c, half)])
        if last:
            def out_half(half):
                cols = slice(half * 512, (half + 1) * 512)
                for kc in range(KC):
                    S.op("sp", I("dma_start", out=self.yT[kc * 128:(kc + 1) * 128, cols], in_=xres[:, kc, cols]),
                         reads=[xk(kc, half)], dma="yout")
            layer_norm(32, 48, False, after_half=out_half)
        else:
            def xchg_half(half):
                cols = slice(half * 512, (half + 1) * 512)
                for kc in range(KC):
                    S.op("sp", I("dma_start", out=self.xsave[kc * 128:(kc + 1) * 128, cols], in_=xres[:, kc, cols]),
                         reads=[xk(kc, half)], writes=["xsave"], dma="yout")
                S.op("sp", I("dma_start", out=self.xs[half].ap().rearrange("(a p) c -> p a c", p=128),
                             in_=x1bf[:, :, cols]), reads=[f"x1bf{half}"], writes=[f"xs{half}"], dma=f"xch{half}")
                S.op("pool", I("collective_compute", "AllGather", ALU.bypass,
                               replica_groups=[[0, 1], [2, 3], [4, 5], [6, 7]],
                               ins=[self.xs[half].ap().opt()], outs=[self.xg[half].ap().opt()]),
                     reads=[f"xs{half}"], writes=[f"xg{half}"])
            layer_norm(32, 48, True, after_half=xchg_half)
            for kc in range(KC):
                if kc % 2 == 0:
                    DVE(I("tensor_copy", out=xT_bf[:, kc, 0:1024], in_=x1bf[:, kc, :]), reads=["x1bf0", "x1bf1"],
                        writes=[xk(kc, 0), xk(kc, 1)])
                else:
                    ACT(I("copy", out=xT_bf[:, kc, 0:1024], in_=x1bf[:, kc, :]), reads=["x1bf0", "x1bf1"],
                        writes=[xk(kc, 0), xk(kc, 1)])
            fsel = cct[:, 352:353]
            gsel = cct[:, 353:354]
            stg = [[self.view(reg, i * 16384, [128, 16, 512], BF16).rearrange("p (a r) c -> p a r c", r=2) for i in range(2)]
                   for reg in ("R2", "R4")]
            tmpb = [self.view("R5", i * 1024, [128, 512], BF16) for i in range(4)]
            for r in range(2):
                for i in range(2):
                    for rk in range(2):
                        src = self.xg[i].ap().rearrange("(r a p) c -> r p a c", r=2, p=128)[rk][:, r * 8:(r + 1) * 8]
                        S.op("sp", I("dma_start", out=stg[r][i][:, :, rk, :], in_=src), reads=[f"xg{i}"],
                             writes=[f"stg{r}{i}"] + ([f"tz{j}" for j in range(4)] + [f"tq{j}" for j in range(4)] + [f"tmp{j}" for j in range(8)]
                                                      if r == 1 else [f"aT{j}h{h}" for j in range(16) for h in range(2)]),
                             dma=f"xch{2 + r}")
            cnt = 0
            for r in range(2):
                sA, sB = stg[r]
                for k8 in range(8):
                    kc = r * 8 + k8
                    for (s1, s0, c0) in ((sA, sB, 1024), (sB, sA, 1536)):
                        ti = cnt % 4
                        cnt += 1
                        ACT(I("activation", out=tmpb[ti][:, :], in_=s1[:, k8, 1, :], func=AF.Identity, scale=fsel),
                            reads=[f"stg{r}0", f"stg{r}1", "cct"], writes=[f"tmpb{ti}"] + (["x1bf0", "x1bf1"] if cnt <= 4 else []))
                        DVE(I("scalar_tensor_tensor", out=xT_bf[:, kc, c0:c0 + 512], in0=s0[:, k8, 0, :], scalar=gsel, in1=tmpb[ti][:, :],
                              op0=ALU.mult, op1=ALU.add), reads=[f"stg{r}0", f"stg{r}1", f"tmpb{ti}", "cct"], writes=[xk(kc, 0), xk(kc, 1)])
        if not last:
            for t in range(3):
                self.pre_w[(li + 1, t)] = self.load_w(S, li + 1, t)
        S.run()


def build_single_layer(L, debug=False, units=range(16), do_post=True):
    p = Prog([L], debug=debug)
    p.phase_clear()
    p.phase_setup()
    p.phase_attn(0, L, True, units=units, next_xres=(p.xT[:, 0:1024] if do_post else None))
    if do_post:
        p.phase_post(0, L, p.xT[:, 0:1024], True)
    return p.nc


def build_fused():
    p = Prog([0, 1])
    p.phase_clear()
    p.phase_setup()
    p.phase_attn(0, 0, True, next_xres=p.xT[:, 0:1024])
    p.phase_post(0, 0, p.xT[:, 0:1024], False)
    p.phase_attn(1, 1, False, next_xres=p.xsave)
    p.phase_post(1, 1, p.xsave, True)
    return p.nc


_CACHE = {}


def _run_layer(L, x, inp, debug=False):
    if ("nc", L, debug) not in _CACHE:
        _CACHE[("nc", L, debug)] = build_single_layer(L, debug)
    nc = _CACHE[("nc", L, debug)]
    W, wfz = _layer_weights(inp["w_in"][L], inp["w_out"][L], inp["w_mlp_in"][L], inp["w_mlp_out"][L])
    pp = _layer_params(L, inp)
    masks = _mask_tables()
    cmat = _const_mats()
    in_maps = []
    for c in range(8):
        b, half = divmod(c, 2)
        tp = _tokperm(half)
        cc, rtab = _core_consts(half)
        xT = np.ascontiguousarray(x[b][tp, :].T)
        in_maps.append({"xT": xT, "W0": W, "wfz0": wfz, "pp0": pp, "cc": cc, "rtab": rtab, "masks": masks, "cmat": cmat})
    res = run_bass_kernel_spmd(nc, in_maps, core_ids=list(range(8)))
    out = np.empty_like(x)
    dbg = None
    if debug:
        dbg = np.empty((4, 2048, 2048), np.float32)
    for c in range(8):
        b, half = divmod(c, 2)
        tp = _tokperm(half)
        out[b][tp[0:1024], :] = res.results[c]["yT"].T
        if debug:
            dbg[b][tp[0:1024], :] = res.results[c]["dbg"].T
    return (out, dbg) if debug else out


def kernel_unfused(**inputs):
    inp = {k: np.asarray(v) for k, v in inputs.items()}
    x = np.ascontiguousarray(inp["x"], dtype=np.float32)
    for L in range(DEPTH):
        x = _run_layer(L, x, inp)
    return x


def kernel(**inputs):
    inp = {k: np.asarray(v) for k, v in inputs.items()}
    x = np.ascontiguousarray(inp["x"], dtype=np.float32)
    if "fused" not in _CACHE:
        _CACHE["fused"] = build_fused()
    nc = _CACHE["fused"]
    shared = {"masks": _mask_tables(), "cmat": _const_mats()}
    for L in range(DEPTH):
        W, wfz = _layer_weights(inp["w_in"][L], inp["w_out"][L], inp["w_mlp_in"][L], inp["w_mlp_out"][L])
        shared[f"W{L}"] = W
        shared[f"wfz{L}"] = wfz
        shared[f"pp{L}"] = _layer_params(L, inp)
    in_maps = []
    for c in range(8):
        b, half = divmod(c, 2)
        tp = _tokperm(half)
        cc, rtab = _core_consts(half)
        m = dict(shared)
        m.update({"xT": np.ascontiguousarray(x[b][tp, :].T), "cc": cc, "rtab": rtab})
        in_maps.append(m)
    res = run_bass_kernel_spmd(nc, in_maps, core_ids=list(range(8)))
    out = np.empty_like(x)
    for c in range(8):
        b, half = divmod(c, 2)
        tp = _tokperm(half)
        out[b][tp[0:1024], :] = res.results[c]["yT"].T
    return out
```

```python
import numpy as np
import concourse.bass as bass
import concourse.mybir as mybir
from concourse.bass_utils import run_bass_kernel_spmd

F32 = mybir.dt.float32
BF16 = mybir.dt.bfloat16
AF = mybir.ActivationFunctionType
ALU = mybir.AluOpType
AX = mybir.AxisListType

D = 2048
SEQ = 2048
KC = 16
DEPTH = 2
NEG = -30000.0
ALPHA = (2 * DEPTH) ** 0.25
NWT = 192
NSLOT = 3
OVERLAP = True
GROUPS = ([0, 2, 1, 3], [1, 3, 2, 0])
N_MASK = 17
CC_COLS = 192 + 128 + 16 + 16 + 2
PP_COLS = 64 + 1 + 256 + 1


def I(name, *args, **kw):
    return (name, args, kw)


class Tok:
    __slots__ = ("sem", "val", "eng")

    def __init__(self, sem, val, eng):
        self.sem, self.val, self.eng = sem, val, eng


class Sched:
    COMPUTE = ("pe", "act", "dve", "pool")

    def __init__(self, nc, sems, dsems):
        self.nc = nc
        self.sem = sems
        self.cnt = sems["_cnt"]
        self.dsem = dsems
        self.dcnt = dsems["_cnt"]
        self.pending = {e: [] for e in self.COMPUTE}
        self.streams = {e: [] for e in ("pe", "act", "dve", "pool", "sp")}
        self.res = {}
        self.dma_toks = []

    def _r(self, key):
        r = self.res.get(key)
        if r is None:
            r = self.res[key] = [None, []]
        return r

    def op(self, eng, fn, reads=(), writes=(), sig=True, dma=None):
        assert sig or eng == "pe"
        deps = []
        for k in reads:
            r = self._r(k)
            if r[0] is not None:
                deps.append(r[0])
        for k in writes:
            r = self._r(k)
            if r[0] is not None:
                deps.append(r[0])
            deps.extend(r[1])
        if dma is None and eng == "pe":
            deps = [d for d in deps if d.eng != "pe"]
        if dma is not None:
            assert dma in self.dsem, dma
            self.dcnt[dma] += 16
            tok = Tok(self.dsem[dma], self.dcnt[dma], "dma")
            inc = (self.dsem[dma], 16)
            self.dma_toks.append(tok)
        else:
            tok = Tok(self.sem[eng], None, eng)
            self.pending[eng].append(tok)
            inc = None
            if sig:
                self.cnt[eng] += 1
                for t in self.pending[eng]:
                    t.val = self.cnt[eng]
                self.pending[eng] = []
                inc = (self.sem[eng], 1)
        self.streams[eng].append((deps, fn, inc))
        for k in reads:
            self._r(k)[1].append(tok)
        for k in writes:
            r = self._r(k)
            r[0] = tok
            r[1] = []
        return tok

    def run(self):
        for e in self.COMPUTE:
            assert not self.pending[e], f"unsignalled tail on {e}"
        self.streams["sp"].append((list(self.dma_toks), None, None))
        with self.nc.Block() as block:
            binder = {"pe": block.tensor, "act": block.scalar, "dve": block.vector,
                      "pool": block.gpsimd, "sp": block.sync}
            for e, stream in self.streams.items():
                if not stream:
                    continue

                def body(engine, stream=stream):
                    waited = {}
                    for deps, fn, inc in stream:
                        need = {}
                        for d in deps:
                            assert d.val is not None
                            key = id(d.sem)
                            if waited.get(key, 0) >= d.val:
                                continue
                            if key not in need or need[key][1] < d.val:
                                need[key] = (d.sem, d.val)
                        for key, (s, v) in need.items():
                            engine.wait_ge(s, v)
                            waited[key] = v
                        if fn is None:
                            continue
                        inst = getattr(engine, fn[0])(*fn[1], **fn[2])
                        if inc is not None:
                            inst.then_inc(inc[0], inc[1])

                binder[e](body)


def _slopes():
    n = 12
    return np.exp2(-8.0 * np.arange(1, n + 1, dtype=np.float64) / n)


def _tokperm(half):
    return np.concatenate([np.arange(512) + 512 * g for g in GROUPS[half]])


def _mask_tables():
    sl = np.arange(128)[:, None]
    tl = np.arange(512)[None, :]
    m = np.zeros((N_MASK, 128, 512), np.float32)
    for i in range(4):
        m[i] = np.where(128 * i + sl <= tl, 0.0, NEG)
        m[4 + i] = np.where(128 * i + sl < tl, 0.0, NEG)

    def dil(rel):
        d = 128 * rel + tl - sl
        mult = ((d >= 0) & (d <= 128)).astype(np.int64) + ((d >= 0) & (d % 4 == 0) & (d <= 512)) \
            + ((d >= 0) & (d % 16 == 0) & (d <= 2048))
        return np.where(mult > 0, np.log(np.maximum(mult, 1)), NEG).astype(np.float32)

    for i in range(4):
        m[8 + i] = dil(-i)
    for r in range(1, 5):
        m[12 + (4 - r)] = dil(r)
    m[16] = dil(8)
    return m


def _const_mats():
    c = np.zeros((4, 128, 128), np.float32)
    c[0] = np.eye(128)
    c[1] = -np.eye(128)
    kk = np.arange(128)[:, None]
    mm = np.arange(128)[None, :]
    c[2] = (kk >= mm).astype(np.float32)
    c[3] = 1.0
    return c


def _core_consts(half):
    tp = _tokperm(half)
    s_nat = tp.reshape(16, 128).T.astype(np.float64)
    pad = np.zeros(16)
    if half == 0:
        pad[12:16] = NEG
    sl = _slopes()
    cc = np.zeros((128, CC_COLS), np.float32)
    biasK = s_nat[:, :, None] * sl[None, None, :] + pad[None, :, None]
    cc[:, 0:192] = biasK.reshape(128, 192)
    cc[:, 192:320] = np.repeat(pad[None, :, None], 8, axis=2).repeat(128, axis=0).reshape(128, 128)
    cc[:, 320:336] = pad[None, :]
    nat = GROUPS[half]
    m = np.zeros((4, 4))
    for g in range(4):
        for g2 in range(4):
            m[g, g2] = 1.0 if nat[g2] < nat[g] else 0.0
    cc[:, 336:352] = m.reshape(1, 16)
    cc[:, 352] = 1.0 if half == 0 else 0.0
    cc[:, 353] = 0.0 if half == 0 else 1.0
    rtab = np.zeros((14, 1024), np.float32)
    rtab[0:12] = (-sl[:, None] * tp[None, 0:1024].astype(np.float64)).astype(np.float32)
    rtab[12:14] = np.repeat(pad, 128).reshape(2, 1024)
    return cc, rtab


IN_OFF = {"fq": 0, "fk": 512, "fv": 1024, "fz": 1536, "sq": 1544, "sk": 2056, "sv": 2568,
          "dq": 3080, "dk": 3592, "dv": 4104, "gq": 4616, "gk": 5128, "gv": 5640}
MIX = (("fq", "fk", "fv"), ("sq", "sk", "sv"), ("dq", "dk", "dv"), ("gq", "gk", "gv"))


def _wtile(w_cols):
    return np.ascontiguousarray(w_cols.reshape(16, 128, 128).transpose(1, 0, 2).reshape(128, 2048))


def _layer_weights(w_in, w_out, w1, w2):
    W = np.empty((NWT, 128, 2048), np.float32)
    t = 0
    for u in range(16):
        m, i = divmod(u, 4)
        for nm in MIX[m]:
            c0 = IN_OFF[nm] + i * 128
            W[t] = _wtile(w_in[:, c0:c0 + 128])
            t += 1
    for oc in range(16):
        W[t] = _wtile(w_out[:, oc * 128:(oc + 1) * 128])
        t += 1
    for qf in range(4):
        for fcl in range(16):
            fc = qf * 16 + fcl
            W[t] = _wtile(w1[:, fc * 128:(fc + 1) * 128])
            t += 1
        for c in range(16):
            W[t] = _wtile(w2[qf * 2048:(qf + 1) * 2048, c * 128:(c + 1) * 128])
            t += 1
    assert t == NWT
    wfz = np.ascontiguousarray(
        w_in[:, 1536:1544].reshape(16, 128, 8).transpose(1, 0, 2).reshape(128, 128))
    return W, wfz


def _layer_params(l, inp):
    pp = np.zeros((128, PP_COLS), np.float32)
    pp[:, 0:16] = inp["ln1_gain"][l].reshape(16, 128).T
    pp[:, 16:32] = inp["ln1_bias"][l].reshape(16, 128).T
    pp[:, 32:48] = inp["ln2_gain"][l].reshape(16, 128).T
    pp[:, 48:64] = inp["ln2_bias"][l].reshape(16, 128).T
    pp[:, 64] = inp["diff_subln_gain"][l]
    pp[:, 65:129] = inp["diff_lambda_q1"][l][None, :]
    pp[:, 129:193] = inp["diff_lambda_k1"][l][None, :]
    pp[:, 193:257] = inp["diff_lambda_q2"][l][None, :]
    pp[:, 257:321] = inp["diff_lambda_k2"][l][None, :]
    pp[0:8, 321] = inp["fox_forget_bias"][l]
    return pp


class Prog:
    def __init__(self, layers, debug=False):
        self.layers = layers
        self.debug = debug
        nc = self.nc = bass.Bass("TRN2", target_bir_lowering=False)
        nL = len(layers)
        self.xT = nc.dram_tensor("xT", [D, SEQ], F32, kind="ExternalInput").ap()
        self.W = [nc.dram_tensor(f"W{i}", [NWT, 128, 2048], F32, kind="ExternalInput").ap() for i in range(nL)]
        self.wfz = [nc.dram_tensor(f"wfz{i}", [128, 128], F32, kind="ExternalInput").ap() for i in range(nL)]
        self.pp = [nc.dram_tensor(f"pp{i}", [128, PP_COLS], F32, kind="ExternalInput").ap() for i in range(nL)]
        self.cc = nc.dram_tensor("cc", [128, CC_COLS], F32, kind="ExternalInput").ap()
        self.rtab = nc.dram_tensor("rtab", [14, 1024], F32, kind="ExternalInput").ap()
        self.masks = nc.dram_tensor("masks", [N_MASK, 128, 512], F32, kind="ExternalInput").ap()
        self.cmat = nc.dram_tensor("cmat", [4, 128, 128], F32, kind="ExternalInput").ap()
        self.yT = nc.dram_tensor("yT", [D, 1024], F32, kind="ExternalOutput").ap()
        if debug:
            self.dbg = nc.dram_tensor("dbg", [D, 1024], F32, kind="ExternalOutput").ap()
        if nL > 1:
            self.xsave = nc.dram_tensor("xsave", [D, 1024], F32).ap()
            self.xs = [nc.dram_tensor(f"xs{i}", [D, 512], BF16) for i in range(2)]
            self.xg = [nc.dram_tensor(f"xg{i}", [2 * D, 512], BF16) for i in range(2)]
        self.sems = {e: nc.alloc_semaphore(f"sem_{e}") for e in Sched.COMPUTE}
        self.sems["_cnt"] = {e: 0 for e in Sched.COMPUTE}
        self.dsems = {"_cnt": {}}
        for nm in ["c0", "c1", "xt", "xres", "yout", "dbg0", "dbg1", "qrow00", "qrow01", "qrow10", "qrow11", "qrowp00", "qrowp01", "qrowp10", "qrowp11", "xch0", "xch1", "xch2", "xch3"] + \
                [f"wr{i}" for i in range(NSLOT)]:
            self.dsems[nm] = nc.alloc_semaphore(f"dsem_{nm}")
            self.dsems["_cnt"][nm] = 0
        self.pst = nc.alloc_psum_tensor("pst", [128, 8, 512], F32)
        self.ps = [self.pst[:, i, :] for i in range(8)]
        R = self.R = {}
        R["R1"] = nc.alloc_sbuf_tensor("R1", [128, 16384], F32)
        R["R2"] = nc.alloc_sbuf_tensor("R2", [128, 8192], F32)
        R["R4"] = nc.alloc_sbuf_tensor("R4", [128, 8192], F32)
        R["R5"] = nc.alloc_sbuf_tensor("R5", [128, 8192], F32)
        self.wr = [nc.alloc_sbuf_tensor(f"wr{i}", [128, 16, 128], BF16) for i in range(NSLOT)]
        self.maskt = nc.alloc_sbuf_tensor("maskt", [128, N_MASK, 512], BF16)
        self.cmb = nc.alloc_sbuf_tensor("cmb", [128, 4, 128], BF16)
        self.cm32 = nc.alloc_sbuf_tensor("cm32", [128, 4, 128], F32)
        self.cct = nc.alloc_sbuf_tensor("cct", [128, CC_COLS], F32)
        self.ppt = [nc.alloc_sbuf_tensor(f"ppt{i}", [128, PP_COLS], F32) for i in range(nL)]
        self.gT = nc.alloc_sbuf_tensor("gT", [128, 16, 8], F32)
        self.small = nc.alloc_sbuf_tensor("small", [128, 160], F32)

        def view(reg, boff, shape, dt):
            n = int(np.prod(shape[1:]))
            esz = 4 if dt == F32 else 2
            assert boff % 4 == 0 and (n * esz) % 4 == 0
            ap = R[reg][:, boff // 4: boff // 4 + (n * esz) // 4]
            if dt != F32:
                ap = ap.bitcast(dt)
            if len(shape) == 3:
                ap = ap.rearrange("p (a b) -> p a b", a=shape[1])
            return ap

        self.view = view
        self.xT_bf = view("R1", 0, [128, 16, 2048], BF16)
        self.xres = view("R1", 0, [128, 16, 1024], F32)
        self.mixed = view("R2", 0, [128, 16, 1024], BF16)
        self.aT = view("R2", 0, [128, 16, 1024], BF16)
        o = 0
        KT0 = [view("R4", o + i * 4096, [128, 2048], BF16) for i in range(2)]
        o += 8192
        QT0 = [view("R4", o + i * 2048, [128, 1024], BF16) for i in range(2)]
        o += 4096
        nQT0 = [view("R4", o + i * 2048, [128, 1024], BF16) for i in range(2)]
        o += 4096
        self.VT = view("R4", o, [128, 2048], BF16)
        o += 4096
        Vtok0 = view("R4", o, [128, 16, 256], BF16)
        o += 8192
        self.E = [view("R4", o + i * 1024, [128, 512], BF16) for i in range(4)]
        self.Et = view("R4", o, [128, 4, 512], BF16)
        o += 4096
        assert o == 32768
        self.tmp32 = [view("R4", i * 2048, [128, 512], F32) for i in range(8)]
        self.fz = view("R5", 0, [128, 2048], F32)
        self.grow = view("R5", 8192, [128, 2048], F32)
        self.ones8 = view("R2", 0, [128, 512], F32)
        self.spt = view("R5", 0, [128, 4, 512], BF16).rearrange("p (a h) c -> p a h c", a=2)
        nQT1 = [view("R5", 4096 + i * 2048, [128, 1024], BF16) for i in range(2)]
        KT1 = [view("R5", 8192 + i * 4096, [128, 2048], BF16) for i in range(2)]
        self.rneg = view("R5", 16384, [128, 1024], BF16)
        self.SS32 = [view("R5", 18432 + i * 2048, [128, 512], F32) for i in range(2)]
        self.SSt = view("R5", 22528, [128, 4, 512], BF16).rearrange("p (a h) c -> p a h c", a=2)
        qt1b = nc.alloc_sbuf_tensor("qt1b", [128, 1024], BF16)
        QT1 = [view("R5", 26624, [128, 1024], BF16), qt1b[:, :]]
        self.rz = [view("R5", 28672 + i * 2048, [128, 512], F32) for i in range(2)]
        vtok1 = nc.alloc_sbuf_tensor("vtok1", [128, 16, 256], BF16)
        self.KT = [KT0, KT1]
        self.QT = [QT0, QT1]
        self.nQT = [nQT0, nQT1]
        self.Vtok = [Vtok0, vtok1[:, :, :]]
        self.x1bf = view("R5", 0, [128, 16, 1024], BF16)
        self.wt = 0
        self.pre_w = {}
        self.pre_xres = False

    def new_sched(self):
        return Sched(self.nc, self.sems, self.dsems)

    def get_w(self, S, L, t):
        if (L, t) in self.pre_w:
            return self.pre_w.pop((L, t))
        return self.load_w(S, L, t)

    def load_w(self, S, L, t):
        slot = self.wt % NSLOT
        self.wt += 1
        dst = self.wr[slot]
        src = self.W[L][t].rearrange("p (a b) -> p a b", a=16)
        S.op("pool", I("dma_start", out=dst[:], in_=src), writes=[f"wr{slot}"], dma=f"wr{slot}")
        return dst, f"wr{slot}"

    def phase_clear(self):
        sems = [self.sems[e] for e in Sched.COMPUTE] + [v for k, v in self.dsems.items() if k != "_cnt"]
        with self.nc.Block() as block:
            def body(engine):
                for s_ in sems:
                    engine.sem_clear(s_)
            block.gpsimd(body)

    def phase_setup(self):
        S = self.new_sched()
        S.op("pool", I("dma_start", out=self.maskt[:], in_=self.masks.rearrange("m p t -> p m t")),
             writes=["maskt"], dma="c0")
        S.op("pool", I("dma_start", out=self.cmb[:], in_=self.cmat.rearrange("m p t -> p m t")),
             writes=["cmb"], dma="c0")
        S.op("sp", I("dma_start", out=self.cm32[:], in_=self.cmat.rearrange("m p t -> p m t")),
             writes=["cm32"], dma="c1")
        S.op("sp", I("dma_start", out=self.cct[:], in_=self.cc), writes=["cct"], dma="c1")
        for i in range(len(self.layers)):
            S.op("sp", I("dma_start", out=self.ppt[i][:], in_=self.pp[i]), writes=["ppt"], dma="c1")
        S.run()

    def phase_attn(self, li, L, first, units=range(16), next_xres=None):
        nc = self.nc
        S = self.new_sched()
        ps = self.ps
        ident = self.cmb[:, 0, :]
        negident = self.cmb[:, 1, :]
        utri = self.cmb[:, 2, :]
        ones_b = self.cmb[:, 3, :]
        ident32 = self.cm32[:, 0, :]
        ones32 = self.cm32[:, 3, :]
        cct, ppt = self.cct, self.ppt[li]
        KT, QT, nQT, VT, Vtok, E = self.KT, self.QT, self.nQT, self.VT, self.Vtok, self.E
        xT_bf = self.xT_bf
        sm = self.small
        lam_init = 0.8 - 0.6 * float(np.exp(-0.3 * L))

        def PE(fn, reads=(), writes=(), sig=True):
            return S.op("pe", fn, reads, writes, sig)

        def ACT(fn, reads=(), writes=()):
            return S.op("act", fn, reads, writes)

        def DVE(fn, reads=(), writes=()):
            return S.op("dve", fn, reads, writes)

        if first:
            for kc in range(KC):
                S.op("pool", I("dma_start", out=xT_bf[:, kc, :], in_=self.xT[kc * 128:(kc + 1) * 128, :]),
                     writes=["xT_bf"], dma="xt")
        for bb in range(2):
            for i in range(2):
                DVE(I("memset", KT[bb][i][64:65, :], 1.0), writes=[f"KT{bb}{i}"])
            DVE(I("memset", Vtok[bb][:, :, :], 1.0), writes=[f"Vtok{bb}"])

        DVE(I("tensor_tensor", out=sm[:, 64:128], in0=ppt[:, 65:129], in1=ppt[:, 129:193], op=ALU.mult),
            reads=["ppt"], writes=["sm_a"])
        DVE(I("tensor_reduce", out=sm[:, 1:2], in_=sm[:, 64:128], axis=AX.X, op=ALU.add), reads=["sm_a"], writes=["sm_b"])
        DVE(I("tensor_tensor", out=sm[:, 64:128], in0=ppt[:, 193:257], in1=ppt[:, 257:321], op=ALU.mult),
            reads=["ppt", "sm_b"], writes=["sm_a"])
        DVE(I("tensor_reduce", out=sm[:, 2:3], in_=sm[:, 64:128], axis=AX.X, op=ALU.add), reads=["sm_a"], writes=["sm_c"])
        ACT(I("activation", out=sm[:, 4:5], in_=sm[:, 1:2], func=AF.Exp), reads=["sm_b"], writes=["sm_d"])
        ACT(I("activation", out=sm[:, 5:6], in_=sm[:, 2:3], func=AF.Exp), reads=["sm_c"], writes=["sm_e"])
        DVE(I("scalar_tensor_tensor", out=sm[:, 6:7], in0=sm[:, 5:6], scalar=-lam_init, in1=sm[:, 4:5],
                                             op0=ALU.add, op1=ALU.subtract), reads=["sm_d", "sm_e"], writes=["neglam"])
        DVE(I("tensor_scalar", out=sm[:, 7:8], in0=ppt[:, 64:65], scalar1=(1.0 - lam_init), scalar2=None, op0=ALU.mult),
            reads=["ppt"], writes=["subs"])
        DVE(I("tensor_scalar", out=sm[:, 8:9], in0=ppt[:, 321:322], scalar1=-1.0, scalar2=None, op0=ALU.mult),
            reads=["ppt"], writes=["negbf"])
        neglam = sm[:, 6:7]
        subs = sm[:, 7:8]
        negbf = sm[0:8, 8:9]

        def evac(out, in_, reads, writes, scale=None, shifted=False, allow_act=True):
            use_act = allow_act and (not shifted) and (evac_toggle[0] % 2 == 1)
            evac_toggle[0] += 1
            if use_act:
                if scale is None:
                    ACT(I("copy", out=out, in_=in_), reads, writes)
                else:
                    ACT(I("activation", out=out, in_=in_, func=AF.Identity, scale=scale), reads, writes)
            else:
                if scale is None:
                    DVE(I("tensor_copy", out=out, in_=in_), reads, writes)
                else:
                    DVE(I("tensor_scalar", out=out, in0=in_, scalar1=scale, scalar2=None, op0=ALU.mult), reads, writes)

        evac_toggle = [0]

        wfz_t = self.view("R4", 0, [128, 16, 8], BF16)
        S.op("pool", I("dma_start", out=wfz_t, in_=self.wfz[li].rearrange("p (a b) -> p a b", a=16)),
             writes=["KT00"], dma="c0")
        fz, grow, rneg = self.fz, self.grow, self.rneg
        for tc in range(4):
            b = 6 + (tc % 2)
            for kc in range(KC):
                PE(I("matmul", ps[b][0:8, :], lhsT=wfz_t[:, kc, :],
                                                               rhs=xT_bf[:, kc, tc * 512:(tc + 1) * 512],
                                                               start=(kc == 0), stop=(kc == KC - 1)),
                   reads=["KT00", "xT_bf"], writes=[f"ps{b}"], sig=(kc == KC - 1))
            ACT(I("activation", out=fz[0:8, tc * 512:(tc + 1) * 512], in_=ps[b][0:8, :], func=AF.Exp,
                                                   bias=negbf, scale=-1.0), reads=[f"ps{b}", "negbf"], writes=["fz"])
        ACT(I("activation", out=fz[0:8, :], in_=fz[0:8, :], func=AF.Ln, bias=1.0, scale=1.0), reads=["fz"], writes=["fz"])
        DVE(I("memset", KT[0][0][64:65, :], 1.0), reads=[], writes=["KT00"])
        DVE(I("memset", self.ones8[0:8, :], 1.0), writes=["ones8"])
        for g in range(4):
            DVE(I("tensor_tensor_scan", out=grow[0:8, g * 512:(g + 1) * 512], data0=self.ones8[0:8, :],
                                                    data1=fz[0:8, g * 512:(g + 1) * 512], initial=0.0,
                                                    op0=ALU.mult, op1=ALU.add), reads=["fz", "ones8"], writes=["grow"])
        off = sm[0:8, 16:20]
        DVE(I("memset", sm[0:8, 16:20], 0.0), writes=["off"])
        for g in range(4):
            for g2 in range(4):
                DVE(I("scalar_tensor_tensor",
                    out=sm[0:8, 16 + g:17 + g], in0=grow[0:8, g2 * 512 + 511:g2 * 512 + 512],
                    scalar=cct[0:8, 336 + g * 4 + g2:337 + g * 4 + g2], in1=sm[0:8, 16 + g:17 + g],
                    op0=ALU.mult, op1=ALU.add), reads=["grow", "cct", "off"], writes=["off"])
        for g in range(4):
            DVE(I("tensor_scalar", out=grow[0:8, g * 512:(g + 1) * 512], in0=grow[0:8, g * 512:(g + 1) * 512],
                                               scalar1=sm[0:8, 16 + g:17 + g], scalar2=None, op0=ALU.add),
                reads=["off", "grow"], writes=["grow"])
        DVE(I("tensor_scalar", out=rneg[0:8, :], in0=grow[0:8, 0:1024], scalar1=-1.0, scalar2=None, op0=ALU.mult),
            reads=["grow"], writes=["rneg"])
        for blk in range(16):
            PE(I("transpose", out=ps[6][:, blk * 8:(blk + 1) * 8], in_=grow[0:8, blk * 128:(blk + 1) * 128],
                                              identity=ident32[0:8, 0:8]), reads=["grow", "cm32"], writes=["ps6"])
        DVE(I("tensor_tensor", out=self.gT[:, :, :].rearrange("p a b -> p (a b)"), in0=ps[6][:, 0:128],
                                      in1=cct[:, 192:320], op=ALU.add), reads=["ps6", "cct"], writes=["gT"])

        deferred = []

        def flush_deferred(bank):
            while deferred:
                deferred.pop(0)(bank)

        def finish64(O, Oname, rzi, dst, dname):
            rz = self.rz[rzi]
            zc = self.SS32[rzi]
            ACT(I("copy", out=rz[0:64, :], in_=O[0:64, :]), reads=[Oname], writes=[f"rz{rzi}"])
            DVE(I("tensor_copy", out=zc[0:64, :], in_=O[64:128, :]), reads=[Oname], writes=[f"SS32{rzi}"])
            DVE(I("reciprocal", out=zc[0:64, :], in_=zc[0:64, :]), reads=[f"SS32{rzi}"], writes=[f"SS32{rzi}"])
            DVE(I("tensor_tensor", out=dst, in0=rz[0:64, :], in1=zc[0:64, :], op=ALU.mult),
                reads=[f"rz{rzi}", f"SS32{rzi}"], writes=[dname])

        def softmax_chains(chains, klist, lookahead=True):
            n = len(klist)

            def emitS(c, j):
                pos, mask = klist[j]
                sb = c["S"][j % 2]
                kk = c["K"]
                PE(I("matmul", ps[sb][:, :], lhsT=c["KT"][0:kk, pos * 128:(pos + 1) * 128], rhs=c["QT"][0:kk, c["qc"]],
                     start=True, stop=(mask is None)),
                   reads=[c["KTn"], c["QTn"]], writes=[f"ps{sb}"], sig=(mask is None))
                if mask is not None:
                    PE(I("matmul", ps[sb][:, :], lhsT=ident, rhs=self.maskt[:, mask, :], start=False, stop=True),
                       reads=["cmb", "maskt"], writes=[f"ps{sb}"])

            def emitExp(c, j):
                pos = klist[j][0]
                sb = c["S"][j % 2]
                et = c["E"][j % 2]
                ACT(I("activation", out=E[et][:, :], in_=ps[sb][:, :], func=AF.Exp, bias=c["bias"](pos), scale=1.0),
                    reads=[f"ps{sb}", "gT", "cct"], writes=[f"E{et}"])

            def emitPV(c, j):
                pos = klist[j][0]
                et = c["E"][j % 2]
                PE(I("matmul", ps[c["O"]][:, :], lhsT=c["vl"](pos), rhs=E[et][:, :], start=(j == 0), stop=(j == n - 1)),
                   reads=[f"E{et}", c["Vn"]], writes=[f"ps{c['O']}"], sig=("Z" not in c))
                if "Z" in c:
                    PE(I("matmul", ps[c["Z"]][:, :], lhsT=ones_b, rhs=E[et][:, :], start=(j == 0), stop=(j == n - 1)),
                       reads=[f"E{et}", "cmb"], writes=[f"ps{c['Z']}"])

            if lookahead:
                for c in chains:
                    emitS(c, 0)
                for j in range(n):
                    for c in chains:
                        if j + 1 < n:
                            emitS(c, j + 1)
                    for c in chains:
                        emitExp(c, j)
                    for c in chains:
                        emitPV(c, j)
                    if j == min(2, n - 1):
                        flush_deferred(chains[0]["S"][j % 2])
                    yield
            else:
                for j in range(n):
                    for c in chains:
                        emitS(c, j)
                    for c in chains:
                        emitExp(c, j)
                    yield
                    for c in chains:
                        emitPV(c, j)
                    if j == min(2, n - 1):
                        flush_deferred(chains[0]["S"][0])

        KL_A = [(0, 0), (1, 1), (2, 2), (3, 3), (12, None), (13, None), (14, None), (15, None)]
        KL_B = [(4, 0), (5, 1), (6, 2), (7, 3)] + [(p, None) for p in (8, 9, 10, 11, 0, 1, 2, 3, 12, 13, 14, 15)]
        KL_A_D = [(0, 8), (1, 9), (2, 10), (3, 11), (12, 12), (13, 13), (14, 14), (15, 15)]
        KL_B_D = [(4, 8), (5, 9), (6, 10), (7, 11), (8, 12), (9, 13), (10, 14), (11, 15)] + \
                 [(p, 16) for p in (0, 1, 2, 3, 12, 13, 14, 15)]
        KL_A_S = [(3, 7), (2, 6), (1, 5), (0, 4), (15, None), (14, None), (13, None), (12, None)]
        KL_B_S = [(7, 7), (6, 6), (5, 5), (4, 4)] + [(p, None) for p in (11, 10, 9, 8, 3, 2, 1, 0, 15, 14, 13, 12)]
        KIND = ("fox", "sb", "diff", "dil")
        vdirty = [False, False]
        kdirty = [False, False]

        def proj_gen(u, buf, banks, overlapped):
            kind = KIND[u // 4]
            i = u % 4
            KTb, QTb, nQTb, Vt = KT[buf], QT[buf], nQT[buf], Vtok[buf]
            kq = [f"KT{buf}0", f"KT{buf}1"]
            qq = [f"QT{buf}0", f"QT{buf}1"]
            nq = [f"nQT{buf}0", f"nQT{buf}1"]
            vn = f"Vtok{buf}"
            wq, wqn = self.get_w(S, li, 3 * u + 0)
            wk, wkn = self.get_w(S, li, 3 * u + 1)
            wv, wvn = self.get_w(S, li, 3 * u + 2)
            aa = not overlapped
            bsel = [0]

            def nextbank():
                b = banks[bsel[0] % 2]
                bsel[0] += 1
                return b

            if kind == "sb":
                kdirty[buf] = True
            elif kdirty[buf]:
                for hh in range(2):
                    DVE(I("memset", KTb[hh][64:65, :], 1.0), writes=[kq[hh]])
                kdirty[buf] = False
            if kind != "diff" and vdirty[buf]:
                DVE(I("memset", Vt[:, :, :], 1.0), writes=[vn])
                vdirty[buf] = False
            if kind == "diff":
                vdirty[buf] = True

            def group(w, wn, tc, b):
                for kc in range(KC):
                    PE(I("matmul", ps[b][:, :], lhsT=w[:, kc, :], rhs=xT_bf[:, kc, tc * 512:(tc + 1) * 512],
                         start=(kc == 0), stop=(kc == KC - 1)),
                       reads=[wn, "xT_bf"], writes=[f"ps{b}"], sig=(kc == KC - 1))
                    if kc % 4 == 3:
                        yield

            for tc in range(2):
                b = nextbank()
                yield from group(wq, wqn, tc, b)
                p, pn = ps[b], f"ps{b}"
                cols = slice(tc * 512, (tc + 1) * 512)
                evac(QTb[0][0:64, cols], p[0:64, :], [pn], [qq[0]], scale=0.125, allow_act=aa)
                evac(QTb[1][0:64, cols], p[64:128, :], [pn], [qq[1]], scale=0.125, shifted=True)
                if kind == "sb":
                    evac(nQTb[0][0:64, cols], p[0:64, :], [pn], [nq[0]], scale=-0.125, allow_act=aa)
                    evac(nQTb[1][0:64, cols], p[64:128, :], [pn], [nq[1]], scale=-0.125, shifted=True)
            for hh in range(2):
                if kind == "fox":
                    h = 2 * i + hh
                    S.op("sp", I("dma_start", out=QTb[hh][64:65, :], in_=rneg[h:h + 1, :]), reads=["rneg"], writes=[qq[hh]],
                         dma=f"qrow{buf}{hh}")
                elif kind == "sb":
                    DVE(I("memset", QTb[hh][64:65, :], 1.0), writes=[qq[hh]])
                    DVE(I("memset", nQTb[hh][64:65, :], -1.0), writes=[nq[hh]])
                elif kind in ("diff", "dil"):
                    h = i if kind == "diff" else 4 + 2 * i + hh
                    S.op("pool", I("dma_start", out=QTb[hh][64:65, :], in_=self.rtab[h:h + 1, :]), writes=[qq[hh]],
                         dma=f"qrowp{buf}{hh}")
            if not overlapped:
                flush_deferred(banks[0])
            for tc in range(4):
                b = nextbank()
                yield from group(wk, wkn, tc, b)
                p, pn = ps[b], f"ps{b}"
                cols = slice(tc * 512, (tc + 1) * 512)
                evac(KTb[0][0:64, cols], p[0:64, :], [pn], [kq[0]], allow_act=aa)
                evac(KTb[1][0:64, cols], p[64:128, :], [pn], [kq[1]], shifted=True)
            if kind == "sb":
                for hh in range(2):
                    S.op("pool", I("dma_start", out=KTb[hh][64:65, :],
                                   in_=self.rtab[12:14, :].rearrange("(o a) b -> o (a b)", o=1)), writes=[kq[hh]],
                         dma=f"qrowp{buf}{hh}")
            for tc in range(4):
                b = nextbank()
                yield from group(wv, wvn, tc, b)
                p, pn = ps[b], f"ps{b}"
                cols = slice(tc * 512, (tc + 1) * 512)
                evac(VT[:, cols], p[:, :], [pn], ["VT"], allow_act=aa)
            for tg in range(4):
                b = nextbank()
                pbf = ps[b][:, :].bitcast(BF16)
                for t4 in range(4):
                    tb = tg * 4 + t4
                    PE(I("transpose", out=pbf[:, t4 * 128:(t4 + 1) * 128], in_=VT[:, tb * 128:(tb + 1) * 128], identity=ident),
                       reads=["VT", "cmb"], writes=[f"ps{b}"], sig=(t4 == 3))
                if kind == "diff":
                    DVE(I("tensor_copy", out=Vt[:, tg * 4:(tg + 1) * 4, 0:128],
                          in_=pbf[:, 0:512].rearrange("p (a b) -> p a b", a=4)), reads=[f"ps{b}"], writes=[vn])
                else:
                    DVE(I("tensor_copy", out=Vt[:, tg * 4:(tg + 1) * 4, :].rearrange("p a (h c) -> p a h c", h=2)[:, :, :, 0:64],
                          in_=pbf[:, 0:512].rearrange("p (a h c) -> p a h c", a=4, h=2)), reads=[f"ps{b}"], writes=[vn])
                yield

        def chain_gen(u, buf):
            kind = KIND[u // 4]
            i = u % 4
            KTb, QTb, nQTb, Vt = KT[buf], QT[buf], nQT[buf], Vtok[buf]
            kq = [f"KT{buf}0", f"KT{buf}1"]
            qq = [f"QT{buf}0", f"QT{buf}1"]
            nq = [f"nQT{buf}0", f"nQT{buf}1"]
            vn = f"Vtok{buf}"
            mixu = self.mixed[:, u, :]
            if kind in ("fox", "dil"):
                for slot in range(2):
                    chains = []
                    for hh in range(2):
                        if kind == "fox":
                            h = 2 * i + hh
                            bias = (lambda pos, h=h: self.gT[:, pos, h:h + 1])
                        else:
                            h = 4 + 2 * i + hh
                            bias = (lambda pos, h=h: cct[:, pos * 12 + h:pos * 12 + h + 1])
                        chains.append(dict(KT=KTb[hh], KTn=kq[hh], QT=QTb[hh], QTn=qq[hh], K=65, Vn=vn,
                                           qc=slice(slot * 512, (slot + 1) * 512), S=(2 * hh, 2 * hh + 1),
                                           E=(2 * hh, 2 * hh + 1), O=4 + hh, bias=bias,
                                           vl=(lambda pos, hh=hh: Vt[:, pos, hh * 128:(hh + 1) * 128])))
                    if kind == "fox":
                        kl = KL_A if slot == 0 else KL_B
                    else:
                        kl = KL_A_D if slot == 0 else KL_B_D
                    yield from softmax_chains(chains, kl)
                    for hh in range(2):
                        dst = mixu[hh * 64:(hh + 1) * 64, slot * 512:(slot + 1) * 512]
                        finish64(ps[4 + hh], f"ps{4 + hh}", hh, dst, "mixed")
            elif kind == "diff":
                h = i
                for slot in range(2):
                    chains = []
                    for hh in range(2):
                        chains.append(dict(KT=KTb[hh], KTn=kq[hh], QT=QTb[hh], QTn=qq[hh], K=65, Vn=vn,
                                           qc=slice(slot * 512, (slot + 1) * 512), S=(hh, hh),
                                           E=(2 * hh, 2 * hh + 1), O=4 + hh, Z=6 + hh,
                                           bias=(lambda pos, h=h: cct[:, pos * 12 + h:pos * 12 + h + 1]),
                                           vl=(lambda pos: Vt[:, pos, 0:128])))
                    yield from softmax_chains(chains, KL_A if slot == 0 else KL_B, lookahead=False)
                    rz0, rz1 = self.rz
                    zc0, zc1 = self.SS32
                    DVE(I("tensor_copy", out=rz0[:, :], in_=ps[4][:, :]), reads=["ps4"], writes=["rz0"])
                    ACT(I("copy", out=rz1[:, :], in_=ps[5][:, :]), reads=["ps5"], writes=["rz1"])
                    DVE(I("tensor_copy", out=zc0[:, :], in_=ps[6][:, :]), reads=["ps6"], writes=["SS320"])
                    ACT(I("copy", out=zc1[:, :], in_=ps[7][:, :]), reads=["ps7"], writes=["SS321"])
                    DVE(I("reciprocal", out=zc0[:, :], in_=zc0[:, :]), reads=["SS320"], writes=["SS320"])
                    DVE(I("reciprocal", out=zc1[:, :], in_=zc1[:, :]), reads=["SS321"], writes=["SS321"])
                    DVE(I("tensor_tensor", out=rz0[:, :], in0=rz0[:, :], in1=zc0[:, :], op=ALU.mult), reads=["rz0", "SS320"], writes=["rz0"])
                    DVE(I("tensor_tensor", out=rz1[:, :], in0=rz1[:, :], in1=zc1[:, :], op=ALU.mult), reads=["rz1", "SS321"], writes=["rz1"])
                    DVE(I("scalar_tensor_tensor", out=rz0[:, :], in0=rz1[:, :], scalar=neglam, in1=rz0[:, :],
                          op0=ALU.mult, op1=ALU.add), reads=["rz0", "rz1", "neglam"], writes=["rz0"])
                    DVE(I("tensor_tensor", out=zc0[:, :], in0=rz0[:, :], in1=rz0[:, :], op=ALU.mult), reads=["rz0"], writes=["SS320"])

                    def tail(bank, slot=slot, mixu=mixu, rz0=rz0, zc0=zc0, zc1=zc1):
                        PE(I("matmul", ps[bank][:, :], lhsT=ones32, rhs=zc0[:, :], start=True, stop=True),
                           reads=["SS320", "cm32"], writes=[f"ps{bank}"])
                        ACT(I("activation", out=zc1[:, :], in_=ps[bank][:, :], func=AF.Ln, bias=1e-5, scale=1.0 / 128.0),
                            reads=[f"ps{bank}"], writes=["SS321"])
                        ACT(I("activation", out=zc1[:, :], in_=zc1[:, :], func=AF.Exp, scale=-0.5), reads=["SS321"], writes=["SS321"])
                        DVE(I("tensor_tensor", out=rz0[:, :], in0=rz0[:, :], in1=zc1[:, :], op=ALU.mult),
                            reads=["rz0", "SS321"], writes=["rz0"])
                        DVE(I("tensor_scalar", out=mixu[:, slot * 512:(slot + 1) * 512], in0=rz0[:, :], scalar1=subs,
                              scalar2=None, op0=ALU.mult), reads=["rz0", "subs"], writes=["mixed"])

                    deferred.append(tail)
            else:
                spt, SSt = self.spt, self.SSt
                for slot in range(2):
                    kl = KL_A_S if slot == 0 else KL_B_S
                    n = len(kl)
                    qc = slice(slot * 512, (slot + 1) * 512)

                    def emit_z(hh, j):
                        pos, mask = kl[j]
                        zb = hh
                        PE(I("matmul", ps[zb][:, :], lhsT=KTb[hh][0:65, pos * 128:(pos + 1) * 128], rhs=QTb[hh][0:65, qc],
                             start=True, stop=(mask is None)),
                           reads=[kq[hh], qq[hh]], writes=[f"ps{zb}"], sig=(mask is None))
                        if mask is not None:
                            PE(I("matmul", ps[zb][:, :], lhsT=ident, rhs=self.maskt[:, mask, :], start=False, stop=True),
                               reads=["cmb", "maskt"], writes=[f"ps{zb}"])

                    def emit_softplus_e(j, hh):
                        zb = hh
                        et = 2 * hh
                        ACT(I("activation", out=E[et][:, :], in_=ps[zb][:, :], func=AF.Exp),
                            reads=[f"ps{zb}"], writes=[f"E{et}"])

                    def emit_softplus_l(j, hh):
                        p = j % 2
                        et = 2 * hh
                        ACT(I("activation", out=spt[:, p, hh, :], in_=E[et][:, :], func=AF.Ln, bias=1.0, scale=1.0),
                            reads=[f"E{et}"], writes=[f"sp{p}{hh}"])

                    def emit_softplus_both(j):
                        for hh in range(2):
                            emit_softplus_e(j, hh)
                        for hh in range(2):
                            emit_softplus_l(j, hh)

                    def emit_T(j, hh):
                        pos, mask = kl[j]
                        p = j % 2
                        tb = 6 + hh
                        PE(I("matmul", ps[tb][:, :], lhsT=utri, rhs=spt[:, p, hh, :], start=True, stop=False),
                           reads=[f"sp{p}{hh}", "cmb"], writes=[f"ps{tb}"], sig=False)
                        if j > 0:
                            PE(I("matmul", ps[tb][:, :], lhsT=ones_b, rhs=SSt[:, p, hh, :], start=False, stop=False),
                               reads=[f"SSt{p}{hh}", "cmb"], writes=[f"ps{tb}"], sig=False)
                        PE(I("matmul", ps[tb][:, :], lhsT=KTb[hh][0:65, pos * 128:(pos + 1) * 128],
                             rhs=nQTb[hh][0:65, qc], start=False, stop=(mask is None)),
                           reads=[kq[hh], nq[hh]], writes=[f"ps{tb}"], sig=(mask is None))
                        if mask is not None:
                            PE(I("matmul", ps[tb][:, :], lhsT=negident, rhs=self.maskt[:, mask, :], start=False, stop=True),
                               reads=["cmb", "maskt"], writes=[f"ps{tb}"])

                    def emit_A(j, hh):
                        pos, mask = kl[j]
                        padb = cct[:, 320 + pos:321 + pos]
                        tb = 6 + hh
                        at = 2 * hh + 1
                        ACT(I("activation", out=E[at][:, :], in_=ps[tb][:, :], func=AF.Exp, scale=-1.0),
                            reads=[f"ps{tb}"], writes=[f"E{at}"])

                    def emit_PV(j, hh):
                        pos, mask = kl[j]
                        at = 2 * hh + 1
                        PE(I("matmul", ps[4 + hh][:, :], lhsT=Vt[:, pos, hh * 128:(hh + 1) * 128],
                             rhs=E[at][:, :], start=(j == 0), stop=(j == n - 1)),
                           reads=[f"E{at}", vn], writes=[f"ps{4 + hh}"])

                    for hh in range(2):
                        emit_z(hh, 0)
                    emit_softplus_both(0)
                    for j in range(n):
                        p = j % 2
                        for hh in range(2):
                            if j + 1 < n:
                                emit_z(hh, j + 1)
                            emit_T(j, hh)
                            emit_A(j, hh)
                        yield
                        if j + 1 < n:
                            emit_softplus_both(j + 1)
                        for hh in range(2):
                            emit_PV(j, hh)
                        if j + 1 < n:
                            q = (j + 1) % 2
                            for hh in range(2):
                                if j == 0:
                                    DVE(I("tensor_copy", out=SSt[:, q, hh, :], in_=spt[:, p, hh, :]), reads=[f"sp{p}{hh}"], writes=[f"SSt{q}{hh}"])
                                else:
                                    DVE(I("tensor_tensor", out=SSt[:, q, hh, :], in0=SSt[:, p, hh, :], in1=spt[:, p, hh, :], op=ALU.add),
                                        reads=[f"sp{p}{hh}", f"SSt{p}{hh}"], writes=[f"SSt{q}{hh}"])
                        yield
                    for hh in range(2):
                        dst = mixu[hh * 64:(hh + 1) * 64, qc]
                        DVE(I("tensor_copy", out=dst, in_=ps[4 + hh][0:64, :]), reads=[f"ps{4 + hh}"], writes=["mixed"])

        order = list(units)
        for _ in proj_gen(order[0], 0, (6, 7), False):
            pass
        for idx, u in enumerate(order):
            kind = KIND[u // 4]
            nxt = order[idx + 1] if idx + 1 < len(order) else None
            cg = chain_gen(u, idx % 2)
            if nxt is not None and OVERLAP:
                pg = proj_gen(nxt, (idx + 1) % 2, {"sb": (2, 3), "diff": (2, 3)}.get(kind, (6, 7)), True)
                r = 1 if kind == "sb" else 2
                for _ in cg:
                    for _k in range(r):
                        next(pg, None)
                for _ in pg:
                    pass
            else:
                for _ in cg:
                    pass
                if nxt is not None:
                    for _ in proj_gen(nxt, (idx + 1) % 2, (6, 7), False):
                        pass
        flush_deferred(0)
        if next_xres is not None:
            for t in range(48, 48 + NSLOT):
                self.pre_w[(li, t)] = self.load_w(S, li, t)
            for kc in range(KC):
                S.op("sp", I("dma_start", out=self.xres[:, kc, :], in_=next_xres[kc * 128:(kc + 1) * 128, :]),
                     writes=["xT_bf"], dma="xres")
            self.pre_xres = True
        if self.debug and li == len(self.layers) - 1 and self.debug == "mixed":
            for kc in range(KC):
                DVE(I("tensor_copy", out=self.tmp32[0][:, :], in_=self.mixed[:, kc, 0:512]), reads=["mixed"], writes=["dbgt0"])
                S.op("sp", I("dma_start", out=self.dbg[kc * 128:(kc + 1) * 128, 0:512], in_=self.tmp32[0][:, :]),
                     reads=["dbgt0"], dma="dbg0")
                DVE(I("tensor_copy", out=self.tmp32[1][:, :], in_=self.mixed[:, kc, 512:1024]), reads=["mixed"], writes=["dbgt1"])
                S.op("sp", I("dma_start", out=self.dbg[kc * 128:(kc + 1) * 128, 512:1024], in_=self.tmp32[1][:, :]),
                     reads=["dbgt1"], dma="dbg1")
        S.run()

    def phase_post(self, li, L, xres_src, last):
        S = self.new_sched()
        ps = self.ps
        ones_b = self.cmb[:, 3, :]
        ppt = self.ppt[li]
        cct = self.cct
        xres, mixed, x1bf, aT, xT_bf = self.xres, self.mixed, self.x1bf, self.aT, self.xT_bf
        tmp = self.tmp32
        tz = [self.view("R4", 16384 + i * 1024, [128, 512], BF16) for i in range(4)]
        tq = [self.view("R4", 16384 + 4096 + i * 1024, [128, 512], BF16) for i in range(4)]

        def PE(fn, reads=(), writes=(), sig=True):
            return S.op("pe", fn, reads, writes, sig)

        def ACT(fn, reads=(), writes=()):
            return S.op("act", fn, reads, writes)

        def DVE(fn, reads=(), writes=()):
            return S.op("dve", fn, reads, writes)

        def POOL(fn, reads=(), writes=()):
            return S.op("pool", fn, reads, writes)

        def xk(kc, half):
            return f"xres{kc}h{half}"

        if self.pre_xres:
            self.pre_xres = False
        else:
            for kc in range(KC):
                S.op("sp", I("dma_start", out=xres[:, kc, :], in_=xres_src[kc * 128:(kc + 1) * 128, :]),
                     writes=[xk(kc, 0), xk(kc, 1), "xres_ld"], dma="xres")
        bank = [0]

        def nb():
            b = bank[0] % 4
            bank[0] += 1
            return b

        wbase = 48
        for oc in range(16):
            w, wn = self.get_w(S, li, wbase + oc)
            for half in range(2):
                cols = slice(half * 512, (half + 1) * 512)
                b = nb()
                for kc in range(KC):
                    PE(I("matmul", ps[b][:, :], lhsT=w[:, kc, :], rhs=mixed[:, kc, cols], start=(kc == 0), stop=(kc == KC - 1)),
                       reads=[wn, "mixed"], writes=[f"ps{b}"], sig=(kc == KC - 1))
                DVE(I("scalar_tensor_tensor", out=xres[:, oc, cols], in0=xres[:, oc, cols], scalar=ALPHA, in1=ps[b][:, :],
                      op0=ALU.mult, op1=ALU.add), reads=[f"ps{b}", xk(oc, half), "xres_ld"], writes=[xk(oc, half)])

        wq_pref = []

        def next_w(t):
            if wq_pref and wq_pref[0][0] == t:
                return wq_pref.pop(0)[1]
            return self.load_w(S, li, t)

        def layer_norm(goff, boff, write_bf, prefetch=(), after_half=None):
            stat = {}
            for half in range(2):
                cols = slice(half * 512, (half + 1) * 512)
                b1, b2, bm, br = (4, 5, 6, 7) if half == 0 else (0, 1, 2, 3)
                for kc in range(KC):
                    z_ = tz[(half * 2 + kc) % 4]
                    q_ = tq[(half * 2 + kc) % 4]
                    zi, qi = (half * 2 + kc) % 4, (half * 2 + kc) % 4
                    DVE(I("tensor_copy", out=z_[:, :], in_=xres[:, kc, cols]), reads=[xk(kc, half)], writes=[f"tz{zi}"])
                    ACT(I("activation", out=q_[:, :], in_=xres[:, kc, cols], func=AF.Square), reads=[xk(kc, half)], writes=[f"tq{qi}"])
                    PE(I("matmul", ps[b1][:, :], lhsT=ones_b, rhs=z_[:, :], start=(kc == 0), stop=(kc == KC - 1)),
                       reads=[f"tz{zi}", "cmb"], writes=[f"ps{b1}"])
                    PE(I("matmul", ps[b2][:, :], lhsT=ones_b, rhs=q_[:, :], start=(kc == 0), stop=(kc == KC - 1)),
                       reads=[f"tq{qi}", "cmb"], writes=[f"ps{b2}"])
                if half == 0:
                    for t in prefetch:
                        wq_pref.append((t, self.load_w(S, li, t)))
                msq, lnv = tmp[half * 2], tmp[half * 2 + 1]
                mk, lk = f"tmp{half * 2}", f"tmp{half * 2 + 1}"
                ACT(I("activation", out=ps[bm][:, :], in_=ps[b1][:, :], func=AF.Identity, scale=1.0 / D), reads=[f"ps{b1}"], writes=[f"ps{bm}"])
                ACT(I("activation", out=msq[:, :], in_=ps[b1][:, :], func=AF.Square, scale=1.0 / D), reads=[f"ps{b1}"], writes=[mk])
                DVE(I("scalar_tensor_tensor", out=msq[:, :], in0=ps[b2][:, :], scalar=1.0 / D, in1=msq[:, :],
                      op0=ALU.mult, op1=ALU.subtract), reads=[f"ps{b2}", mk], writes=[mk])
                ACT(I("activation", out=lnv[:, :], in_=msq[:, :], func=AF.Ln, bias=1e-5, scale=1.0), reads=[mk], writes=[lk])
                ACT(I("activation", out=ps[br][:, :], in_=lnv[:, :], func=AF.Exp, scale=-0.5), reads=[lk], writes=[f"ps{br}"])
                stat[half] = (bm, br)
            for half in range(2):
                cols = slice(half * 512, (half + 1) * 512)
                bm, br = stat[half]
                for kc in range(KC):
                    ti = 4 + (kc % 4)
                    t = tmp[ti]
                    tn = f"tmp{ti}"
                    DVE(I("tensor_tensor", out=t[:, :], in0=xres[:, kc, cols], in1=ps[bm][:, :], op=ALU.subtract),
                        reads=[xk(kc, half), f"ps{bm}"], writes=[tn])
                    DVE(I("tensor_tensor", out=t[:, :], in0=t[:, :], in1=ps[br][:, :], op=ALU.mult), reads=[tn, f"ps{br}"], writes=[tn])
                    ACT(I("activation", out=xres[:, kc, cols], in_=t[:, :], func=AF.Identity,
                          bias=ppt[:, boff + kc:boff + kc + 1], scale=ppt[:, goff + kc:goff + kc + 1]),
                        reads=[tn, "ppt"], writes=[xk(kc, half)])
                    if write_bf:
                        ACT(I("activation", out=x1bf[:, kc, cols], in_=t[:, :], func=AF.Identity,
                              bias=ppt[:, boff + kc:boff + kc + 1], scale=ppt[:, goff + kc:goff + kc + 1]),
                            reads=[tn, "ppt"], writes=[f"x1bf{half}"])
                if after_half is not None:
                    after_half(half)

        layer_norm(0, 16, True, prefetch=[64 + i for i in range(NSLOT)])

        wbase = 64
        for qf in range(4):
            for fcl in range(16):
                w, wn = next_w(wbase + qf * 32 + fcl)
                for half in range(2):
                    cols = slice(half * 512, (half + 1) * 512)
                    b = nb()
                    for kc in range(KC):
                        PE(I("matmul", ps[b][:, :], lhsT=w[:, kc, :], rhs=x1bf[:, kc, cols], start=(kc == 0), stop=(kc == KC - 1)),
                           reads=[wn, f"x1bf{half}"], writes=[f"ps{b}"], sig=(kc == KC - 1))
                    t = tmp[4 + b]
                    ACT(I("activation", out=t[:, :], in_=ps[b][:, :], func=AF.Relu), reads=[f"ps{b}"], writes=[f"tmp{4 + b}"])
                    DVE(I("tensor_tensor", out=aT[:, fcl, cols], in0=ps[b][:, :], in1=t[:, :], op=ALU.mult),
                        reads=[f"ps{b}", f"tmp{4 + b}"], writes=[f"aT{fcl}h{half}"])
            for c in range(16):
                w, wn = next_w(wbase + qf * 32 + 16 + c)
                for half in range(2):
                    cols = slice(half * 512, (half + 1) * 512)
                    b = nb()
                    for fcl in range(16):
                        PE(I("matmul", ps[b][:, :], lhsT=w[:, fcl, :], rhs=aT[:, fcl, cols], start=(fcl == 0), stop=(fcl == 15)),
                           reads=[wn, f"aT{fcl}h{half}"], writes=[f"ps{b}"], sig=(fcl == 15))
                    DVE(I("scalar_tensor_tensor", out=xres[:, c, cols], in0=xres[:, c, cols], scalar=(ALPHA if qf == 0 else 1.0),
                          in1=ps[b][:, :], op0=ALU.mult, op1=ALU.add), reads=[f"ps{b}", xk(c, half)], writes=[xk(c, half)])
        if last:
            def out_half(half):
                cols = slice(half * 512, (half + 1) * 512)
                for kc in range(KC):
                    S.op("sp", I("dma_start", out=self.yT[kc * 128:(kc + 1) * 128, cols], in_=xres[:, kc, cols]),
                         reads=[xk(kc, half)], dma="yout")
            layer_norm(32, 48, False, after_half=out_half)
        else:
            def xchg_half(half):
                cols = slice(half * 512, (half + 1) * 512)
                for kc in range(KC):
                    S.op("sp", I("dma_start", out=self.xsave[kc * 128:(kc + 1) * 128, cols], in_=xres[:, kc, cols]),
                         reads=[xk(kc, half)], writes=["xsave"], dma="yout")
                S.op("sp", I("dma_start", out=self.xs[half].ap().rearrange("(a p) c -> p a c", p=128),
                             in_=x1bf[:, :, cols]), reads=[f"x1bf{half}"], writes=[f"xs{half}"], dma=f"xch{half}")
                S.op("pool", I("collective_compute", "AllGather", ALU.bypass,
                               replica_groups=[[0, 1], [2, 3], [4, 5], [6, 7]],
                               ins=[self.xs[half].ap().opt()], outs=[self.xg[half].ap().opt()]),
                     reads=[f"xs{half}"], writes=[f"xg{half}"])
            layer_norm(32, 48, True, after_half=xchg_half)
            for kc in range(KC):
                if kc % 2 == 0:
                    DVE(I("tensor_copy", out=xT_bf[:, kc, 0:1024], in_=x1bf[:, kc, :]), reads=["x1bf0", "x1bf1"],
                        writes=[xk(kc, 0), xk(kc, 1)])
                else:
                    ACT(I("copy", out=xT_bf[:, kc, 0:1024], in_=x1bf[:, kc, :]), reads=["x1bf0", "x1bf1"],
                        writes=[xk(kc, 0), xk(kc, 1)])
            fsel = cct[:, 352:353]
            gsel = cct[:, 353:354]
            stg = [[self.view(reg, i * 16384, [128, 16, 512], BF16).rearrange("p (a r) c -> p a r c", r=2) for i in range(2)]
                   for reg in ("R2", "R4")]
            tmpb = [self.view("R5", i * 1024, [128, 512], BF16) for i in range(4)]
            for r in range(2):
                for i in range(2):
                    for rk in range(2):
                        src = self.xg[i].ap().rearrange("(r a p) c -> r p a c", r=2, p=128)[rk][:, r * 8:(r + 1) * 8]
                        S.op("sp", I("dma_start", out=stg[r][i][:, :, rk, :], in_=src), reads=[f"xg{i}"],
                             writes=[f"stg{r}{i}"] + ([f"tz{j}" for j in range(4)] + [f"tq{j}" for j in range(4)] + [f"tmp{j}" for j in range(8)]
                                                      if r == 1 else [f"aT{j}h{h}" for j in range(16) for h in range(2)]),
                             dma=f"xch{2 + r}")
            cnt = 0
            for r in range(2):
                sA, sB = stg[r]
                for k8 in range(8):
                    kc = r * 8 + k8
                    for (s1, s0, c0) in ((sA, sB, 1024), (sB, sA, 1536)):
                        ti = cnt % 4
                        cnt += 1
                        ACT(I("activation", out=tmpb[ti][:, :], in_=s1[:, k8, 1, :], func=AF.Identity, scale=fsel),
                            reads=[f"stg{r}0", f"stg{r}1", "cct"], writes=[f"tmpb{ti}"] + (["x1bf0", "x1bf1"] if cnt <= 4 else []))
                        DVE(I("scalar_tensor_tensor", out=xT_bf[:, kc, c0:c0 + 512], in0=s0[:, k8, 0, :], scalar=gsel, in1=tmpb[ti][:, :],
                              op0=ALU.mult, op1=ALU.add), reads=[f"stg{r}0", f"stg{r}1", f"tmpb{ti}", "cct"], writes=[xk(kc, 0), xk(kc, 1)])
        if not last:
            for t in range(3):
                self.pre_w[(li + 1, t)] = self.load_w(S, li + 1, t)
        S.run()


def build_single_layer(L, debug=False, units=range(16), do_post=True):
    p = Prog([L], debug=debug)
    p.phase_clear()
    p.phase_setup()
    p.phase_attn(0, L, True, units=units, next_xres=(p.xT[:, 0:1024] if do_post else None))
    if do_post:
        p.phase_post(0, L, p.xT[:, 0:1024], True)
    return p.nc


def build_fused():
    p = Prog([0, 1])
    p.phase_clear()
    p.phase_setup()
    p.phase_attn(0, 0, True, next_xres=p.xT[:, 0:1024])
    p.phase_post(0, 0, p.xT[:, 0:1024], False)
    p.phase_attn(1, 1, False, next_xres=p.xsave)
    p.phase_post(1, 1, p.xsave, True)
    return p.nc


_CACHE = {}


def _run_layer(L, x, inp, debug=False):
    if ("nc", L, debug) not in _CACHE:
        _CACHE[("nc", L, debug)] = build_single_layer(L, debug)
    nc = _CACHE[("nc", L, debug)]
    W, wfz = _layer_weights(inp["w_in"][L], inp["w_out"][L], inp["w_mlp_in"][L], inp["w_mlp_out"][L])
    pp = _layer_params(L, inp)
    masks = _mask_tables()
    cmat = _const_mats()
    in_maps = []
    for c in range(8):
        b, half = divmod(c, 2)
        tp = _tokperm(half)
        cc, rtab = _core_consts(half)
        xT = np.ascontiguousarray(x[b][tp, :].T)
        in_maps.append({"xT": xT, "W0": W, "wfz0": wfz, "pp0": pp, "cc": cc, "rtab": rtab, "masks": masks, "cmat": cmat})
    res = run_bass_kernel_spmd(nc, in_maps, core_ids=list(range(8)))
    out = np.empty_like(x)
    dbg = None
    if debug:
        dbg = np.empty((4, 2048, 2048), np.float32)
    for c in range(8):
        b, half = divmod(c, 2)
        tp = _tokperm(half)
        out[b][tp[0:1024], :] = res.results[c]["yT"].T
        if debug:
            dbg[b][tp[0:1024], :] = res.results[c]["dbg"].T
    return (out, dbg) if debug else out


def kernel_unfused(**inputs):
    inp = {k: np.asarray(v) for k, v in inputs.items()}
    x = np.ascontiguousarray(inp["x"], dtype=np.float32)
    for L in range(DEPTH):
        x = _run_layer(L, x, inp)
    return x


def kernel(**inputs):
    inp = {k: np.asarray(v) for k, v in inputs.items()}
    x = np.ascontiguousarray(inp["x"], dtype=np.float32)
    if "fused" not in _CACHE:
        _CACHE["fused"] = build_fused()
    nc = _CACHE["fused"]
    shared = {"masks": _mask_tables(), "cmat": _const_mats()}
    for L in range(DEPTH):
        W, wfz = _layer_weights(inp["w_in"][L], inp["w_out"][L], inp["w_mlp_in"][L], inp["w_mlp_out"][L])
        shared[f"W{L}"] = W
        shared[f"wfz{L}"] = wfz
        shared[f"pp{L}"] = _layer_params(L, inp)
    in_maps = []
    for c in range(8):
        b, half = divmod(c, 2)
        tp = _tokperm(half)
        cc, rtab = _core_consts(half)
        m = dict(shared)
        m.update({"xT": np.ascontiguousarray(x[b][tp, :].T), "cc": cc, "rtab": rtab})
        in_maps.append(m)
    res = run_bass_kernel_spmd(nc, in_maps, core_ids=list(range(8)))
    out = np.empty_like(x)
    for c in range(8):
        b, half = divmod(c, 2)
        tp = _tokperm(half)
        out[b][tp[0:1024], :] = res.results[c]["yT"].T
    return out
```

```python
import numpy as np
import concourse.bass as bass
import concourse.mybir as mybir
from concourse.bass_utils import run_bass_kernel_spmd

F32 = mybir.dt.float32
BF16 = mybir.dt.bfloat16
AF = mybir.ActivationFunctionType
ALU = mybir.AluOpType
AX = mybir.AxisListType

D = 2048
SEQ = 2048
KC = 16
DEPTH = 2
NEG = -30000.0
ALPHA = (2 * DEPTH) ** 0.25
NWT = 192
NSLOT = 3
OVERLAP = True
GROUPS = ([0, 2, 1, 3], [1, 3, 2, 0])
N_MASK = 17
CC_COLS = 192 + 128 + 16 + 16 + 2
PP_COLS = 64 + 1 + 256 + 1


def I(name, *args, **kw):
    return (name, args, kw)


class Tok:
    __slots__ = ("sem", "val", "eng")

    def __init__(self, sem, val, eng):
        self.sem, self.val, self.eng = sem, val, eng


class Sched:
    COMPUTE = ("pe", "act", "dve", "pool")

    def __init__(self, nc, sems, dsems):
        self.nc = nc
        self.sem = sems
        self.cnt = sems["_cnt"]
        self.dsem = dsems
        self.dcnt = dsems["_cnt"]
        self.pending = {e: [] for e in self.COMPUTE}
        self.streams = {e: [] for e in ("pe", "act", "dve", "pool", "sp")}
        self.res = {}
        self.dma_toks = []

    def _r(self, key):
        r = self.res.get(key)
        if r is None:
            r = self.res[key] = [None, []]
        return r

    def op(self, eng, fn, reads=(), writes=(), sig=True, dma=None):
        assert sig or eng == "pe"
        deps = []
        for k in reads:
            r = self._r(k)
            if r[0] is not None:
                deps.append(r[0])
        for k in writes:
            r = self._r(k)
            if r[0] is not None:
                deps.append(r[0])
            deps.extend(r[1])
        if dma is None and eng == "pe":
            deps = [d for d in deps if d.eng != "pe"]
        if dma is not None:
            assert dma in self.dsem, dma
            self.dcnt[dma] += 16
            tok = Tok(self.dsem[dma], self.dcnt[dma], "dma")
            inc = (self.dsem[dma], 16)
            self.dma_toks.append(tok)
        else:
            tok = Tok(self.sem[eng], None, eng)
            self.pending[eng].append(tok)
            inc = None
            if sig:
                self.cnt[eng] += 1
                for t in self.pending[eng]:
                    t.val = self.cnt[eng]
                self.pending[eng] = []
                inc = (self.sem[eng], 1)
        self.streams[eng].append((deps, fn, inc))
        for k in reads:
            self._r(k)[1].append(tok)
        for k in writes:
            r = self._r(k)
            r[0] = tok
            r[1] = []
        return tok

    def run(self):
        for e in self.COMPUTE:
            assert not self.pending[e], f"unsignalled tail on {e}"
        self.streams["sp"].append((list(self.dma_toks), None, None))
        with self.nc.Block() as block:
            binder = {"pe": block.tensor, "act": block.scalar, "dve": block.vector,
                      "pool": block.gpsimd, "sp": block.sync}
            for e, stream in self.streams.items():
                if not stream:
                    continue

                def body(engine, stream=stream):
                    waited = {}
                    for deps, fn, inc in stream:
                        need = {}
                        for d in deps:
                            assert d.val is not None
                            key = id(d.sem)
                            if waited.get(key, 0) >= d.val:
                                continue
                            if key not in need or need[key][1] < d.val:
                                need[key] = (d.sem, d.val)
                        for key, (s, v) in need.items():
                            engine.wait_ge(s, v)
                            waited[key] = v
                        if fn is None:
                            continue
                        inst = getattr(engine, fn[0])(*fn[1], **fn[2])
                        if inc is not None:
                            inst.then_inc(inc[0], inc[1])

                binder[e](body)


def _slopes():
    n = 12
    return np.exp2(-8.0 * np.arange(1, n + 1, dtype=np.float64) / n)


def _tokperm(half):
    return np.concatenate([np.arange(512) + 512 * g for g in GROUPS[half]])


def _mask_tables():
    sl = np.arange(128)[:, None]
    tl = np.arange(512)[None, :]
    m = np.zeros((N_MASK, 128, 512), np.float32)
    for i in range(4):
        m[i] = np.where(128 * i + sl <= tl, 0.0, NEG)
        m[4 + i] = np.where(128 * i + sl < tl, 0.0, NEG)

    def dil(rel):
        d = 128 * rel + tl - sl
        mult = ((d >= 0) & (d <= 128)).astype(np.int64) + ((d >= 0) & (d % 4 == 0) & (d <= 512)) \
            + ((d >= 0) & (d % 16 == 0) & (d <= 2048))
        return np.where(mult > 0, np.log(np.maximum(mult, 1)), NEG).astype(np.float32)

    for i in range(4):
        m[8 + i] = dil(-i)
    for r in range(1, 5):
        m[12 + (4 - r)] = dil(r)
    m[16] = dil(8)
    return m


def _const_mats():
    c = np.zeros((4, 128, 128), np.float32)
    c[0] = np.eye(128)
    c[1] = -np.eye(128)
    kk = np.arange(128)[:, None]
    mm = np.arange(128)[None, :]
    c[2] = (kk >= mm).astype(np.float32)
    c[3] = 1.0
    return c


def _core_consts(half):
    tp = _tokperm(half)
    s_nat = tp.reshape(16, 128).T.astype(np.float64)
    pad = np.zeros(16)
    if half == 0:
        pad[12:16] = NEG
    sl = _slopes()
    cc = np.zeros((128, CC_COLS), np.float32)
    biasK = s_nat[:, :, None] * sl[None, None, :] + pad[None, :, None]
    cc[:, 0:192] = biasK.reshape(128, 192)
    cc[:, 192:320] = np.repeat(pad[None, :, None], 8, axis=2).repeat(128, axis=0).reshape(128, 128)
    cc[:, 320:336] = pad[None, :]
    nat = GROUPS[half]
    m = np.zeros((4, 4))
    for g in range(4):
        for g2 in range(4):
            m[g, g2] = 1.0 if nat[g2] < nat[g] else 0.0
    cc[:, 336:352] = m.reshape(1, 16)
    cc[:, 352] = 1.0 if half == 0 else 0.0
    cc[:, 353] = 0.0 if half == 0 else 1.0
    rtab = np.zeros((14, 1024), np.float32)
    rtab[0:12] = (-sl[:, None] * tp[None, 0:1024].astype(np.float64)).astype(np.float32)
    rtab[12:14] = np.repeat(pad, 128).reshape(2, 1024)
    return cc, rtab


IN_OFF = {"fq": 0, "fk": 512, "fv": 1024, "fz": 1536, "sq": 1544, "sk": 2056, "sv": 2568,
          "dq": 3080, "dk": 3592, "dv": 4104, "gq": 4616, "gk": 5128, "gv": 5640}
MIX = (("fq", "fk", "fv"), ("sq", "sk", "sv"), ("dq", "dk", "dv"), ("gq", "gk", "gv"))


def _wtile(w_cols):
    return np.ascontiguousarray(w_cols.reshape(16, 128, 128).transpose(1, 0, 2).reshape(128, 2048))


def _layer_weights(w_in, w_out, w1, w2):
    W = np.empty((NWT, 128, 2048), np.float32)
    t = 0
    for u in range(16):
        m, i = divmod(u, 4)
        for nm in MIX[m]:
            c0 = IN_OFF[nm] + i * 128
            W[t] = _wtile(w_in[:, c0:c0 + 128])
            t += 1
    for oc in range(16):
        W[t] = _wtile(w_out[:, oc * 128:(oc + 1) * 128])
        t += 1
    for qf in range(4):
        for fcl in range(16):
            fc = qf * 16 + fcl
            W[t] = _wtile(w1[:, fc * 128:(fc + 1) * 128])
            t += 1
        for c in range(16):
            W[t] = _wtile(w2[qf * 2048:(qf + 1) * 2048, c * 128:(c + 1) * 128])
            t += 1
    assert t == NWT
    wfz = np.ascontiguousarray(
        w_in[:, 1536:1544].reshape(16, 128, 8).transpose(1, 0, 2).reshape(128, 128))
    return W, wfz


def _layer_params(l, inp):
    pp = np.zeros((128, PP_COLS), np.float32)
    pp[:, 0:16] = inp["ln1_gain"][l].reshape(16, 128).T
    pp[:, 16:32] = inp["ln1_bias"][l].reshape(16, 128).T
    pp[:, 32:48] = inp["ln2_gain"][l].reshape(16, 128).T
    pp[:, 48:64] = inp["ln2_bias"][l].reshape(16, 128).T
    pp[:, 64] = inp["diff_subln_gain"][l]
    pp[:, 65:129] = inp["diff_lambda_q1"][l][None, :]
    pp[:, 129:193] = inp["diff_lambda_k1"][l][None, :]
    pp[:, 193:257] = inp["diff_lambda_q2"][l][None, :]
    pp[:, 257:321] = inp["diff_lambda_k2"][l][None, :]
    pp[0:8, 321] = inp["fox_forget_bias"][l]
    return pp


class Prog:
    def __init__(self, layers, debug=False):
        self.layers = layers
        self.debug = debug
        nc = self.nc = bass.Bass("TRN2", target_bir_lowering=False)
        nL = len(layers)
        self.xT = nc.dram_tensor("xT", [D, SEQ], F32, kind="ExternalInput").ap()
        self.W = [nc.dram_tensor(f"W{i}", [NWT, 128, 2048], F32, kind="ExternalInput").ap() for i in range(nL)]
        self.wfz = [nc.dram_tensor(f"wfz{i}", [128, 128], F32, kind="ExternalInput").ap() for i in range(nL)]
        self.pp = [nc.dram_tensor(f"pp{i}", [128, PP_COLS], F32, kind="ExternalInput").ap() for i in range(nL)]
        self.cc = nc.dram_tensor("cc", [128, CC_COLS], F32, kind="ExternalInput").ap()
        self.rtab = nc.dram_tensor("rtab", [14, 1024], F32, kind="ExternalInput").ap()
        self.masks = nc.dram_tensor("masks", [N_MASK, 128, 512], F32, kind="ExternalInput").ap()
        self.cmat = nc.dram_tensor("cmat", [4, 128, 128], F32, kind="ExternalInput").ap()
        self.yT = nc.dram_tensor("yT", [D, 1024], F32, kind="ExternalOutput").ap()
        if debug:
            self.dbg = nc.dram_tensor("dbg", [D, 1024], F32, kind="ExternalOutput").ap()
        if nL > 1:
            self.xsave = nc.dram_tensor("xsave", [D, 1024], F32).ap()
            self.xs = [nc.dram_tensor(f"xs{i}", [D, 512], BF16) for i in range(2)]
            self.xg = [nc.dram_tensor(f"xg{i}", [2 * D, 512], BF16) for i in range(2)]
        self.sems = {e: nc.alloc_semaphore(f"sem_{e}") for e in Sched.COMPUTE}
        self.sems["_cnt"] = {e: 0 for e in Sched.COMPUTE}
        self.dsems = {"_cnt": {}}
        for nm in ["c0", "c1", "xt", "xt0", "xt1", "xt2", "xt3", "ldm", "xres", "yout", "dbg0", "dbg1", "qrow00", "qrow01", "qrow10", "qrow11", "qrowp00", "qrowp01", "qrowp10", "qrowp11", "xch0", "xch1", "xch2", "xch3"] + \
                [f"wr{i}" for i in range(NSLOT)]:
            self.dsems[nm] = nc.alloc_semaphore(f"dsem_{nm}")
            self.dsems["_cnt"][nm] = 0
        self.pst = nc.alloc_psum_tensor("pst", [128, 8, 512], F32)
        self.ps = [self.pst[:, i, :] for i in range(8)]
        R = self.R = {}
        R["R1"] = nc.alloc_sbuf_tensor("R1", [128, 16384], F32)
        R["R2"] = nc.alloc_sbuf_tensor("R2", [128, 8192], F32)
        R["R4"] = nc.alloc_sbuf_tensor("R4", [128, 8192], F32)
        R["R5"] = nc.alloc_sbuf_tensor("R5", [128, 8192], F32)
        self.wr = [nc.alloc_sbuf_tensor(f"wr{i}", [128, 16, 128], BF16) for i in range(NSLOT)]
        self.maskt = nc.alloc_sbuf_tensor("maskt", [128, N_MASK, 512], BF16)
        self.cmb = nc.alloc_sbuf_tensor("cmb", [128, 4, 128], BF16)
        self.cm32 = nc.alloc_sbuf_tensor("cm32", [128, 4, 128], F32)
        self.cct = nc.alloc_sbuf_tensor("cct", [128, CC_COLS], F32)
        self.ppt = [nc.alloc_sbuf_tensor(f"ppt{i}", [128, PP_COLS], F32) for i in range(nL)]
        self.gT = nc.alloc_sbuf_tensor("gT", [128, 16, 8], F32)
        self.small = nc.alloc_sbuf_tensor("small", [128, 160], F32)

        def view(reg, boff, shape, dt):
            n = int(np.prod(shape[1:]))
            esz = 4 if dt == F32 else 2
            assert boff % 4 == 0 and (n * esz) % 4 == 0
            ap = R[reg][:, boff // 4: boff // 4 + (n * esz) // 4]
            if dt != F32:
                ap = ap.bitcast(dt)
            if len(shape) == 3:
                ap = ap.rearrange("p (a b) -> p a b", a=shape[1])
            return ap

        self.view = view
        self.xT_bf = view("R1", 0, [128, 16, 2048], BF16)
        self.xres = view("R1", 0, [128, 16, 1024], F32)
        self.mixed = view("R2", 0, [128, 16, 1024], BF16)
        self.aT = view("R2", 0, [128, 16, 1024], BF16)
        o = 0
        KT0 = [view("R4", o + i * 4096, [128, 2048], BF16) for i in range(2)]
        o += 8192
        QT0 = [view("R4", o + i * 2048, [128, 1024], BF16) for i in range(2)]
        o += 4096
        nQT0 = [view("R4", o + i * 2048, [128, 1024], BF16) for i in range(2)]
        o += 4096
        self.VT = view("R4", o, [128, 2048], BF16)
        o += 4096
        Vtok0 = view("R4", o, [128, 16, 256], BF16)
        o += 8192
        self.E = [view("R4", o + i * 1024, [128, 512], BF16) for i in range(4)]
        self.Et = view("R4", o, [128, 4, 512], BF16)
        o += 4096
        assert o == 32768
        self.tmp32 = [view("R4", i * 2048, [128, 512], F32) for i in range(8)]
        self.fz = view("R5", 0, [128, 2048], F32)
        self.grow = view("R5", 8192, [128, 2048], F32)
        self.ones8 = view("R2", 0, [128, 512], F32)
        self.spt = view("R5", 0, [128, 4, 512], BF16).rearrange("p (a h) c -> p a h c", a=2)
        nQT1 = [view("R5", 4096 + i * 2048, [128, 1024], BF16) for i in range(2)]
        KT1 = [view("R5", 8192 + i * 4096, [128, 2048], BF16) for i in range(2)]
        self.rneg = view("R5", 16384, [128, 1024], BF16)
        self.SS32 = [view("R5", 18432 + i * 2048, [128, 512], F32) for i in range(2)]
        self.SSt = view("R5", 22528, [128, 4, 512], BF16).rearrange("p (a h) c -> p a h c", a=2)
        qt1b = nc.alloc_sbuf_tensor("qt1b", [128, 1024], BF16)
        QT1 = [view("R5", 26624, [128, 1024], BF16), qt1b[:, :]]
        self.rz = [view("R5", 28672 + i * 2048, [128, 512], F32) for i in range(2)]
        vtok1 = nc.alloc_sbuf_tensor("vtok1", [128, 16, 256], BF16)
        self.KT = [KT0, KT1]
        self.QT = [QT0, QT1]
        self.nQT = [nQT0, nQT1]
        self.Vtok = [Vtok0, vtok1[:, :, :]]
        self.x1bf = view("R5", 0, [128, 16, 1024], BF16)
        self.wt = 0
        self.pre_w = {}
        self.pre_xres = False

    def new_sched(self):
        return Sched(self.nc, self.sems, self.dsems)

    def get_w(self, S, L, t):
        if (L, t) in self.pre_w:
            return self.pre_w.pop((L, t))
        return self.load_w(S, L, t)

    def load_w(self, S, L, t):
        slot = self.wt % NSLOT
        self.wt += 1
        dst = self.wr[slot]
        src = self.W[L][t].rearrange("p (a b) -> p a b", a=16)
        S.op("pool", I("dma_start", out=dst[:], in_=src), writes=[f"wr{slot}"], dma=f"wr{slot}")
        return dst, f"wr{slot}"

    def phase_clear(self):
        sems = [self.sems[e] for e in Sched.COMPUTE] + [v for k, v in self.dsems.items() if k != "_cnt"]
        with self.nc.Block() as block:
            def body(engine):
                for s_ in sems:
                    engine.sem_clear(s_)
            block.gpsimd(body)

    def phase_setup(self):
        S = self.new_sched()
        S.op("pool", I("dma_start", out=self.cmb[:], in_=self.cmat.rearrange("m p t -> p m t")),
             writes=["cmb"], dma="c0")
        S.op("sp", I("dma_start", out=self.cm32[:], in_=self.cmat.rearrange("m p t -> p m t")),
             writes=["cm32"], dma="c1")
        S.op("sp", I("dma_start", out=self.cct[:], in_=self.cc), writes=["cct"], dma="c1")
        for i in range(len(self.layers)):
            S.op("sp", I("dma_start", out=self.ppt[i][:], in_=self.pp[i]), writes=["ppt"], dma="c1")
        S.run()

    def phase_attn(self, li, L, first, units=range(16), next_xres=None):
        nc = self.nc
        S = self.new_sched()
        ps = self.ps
        ident = self.cmb[:, 0, :]
        negident = self.cmb[:, 1, :]
        utri = self.cmb[:, 2, :]
        ones_b = self.cmb[:, 3, :]
        ident32 = self.cm32[:, 0, :]
        ones32 = self.cm32[:, 3, :]
        cct, ppt = self.cct, self.ppt[li]
        KT, QT, nQT, VT, Vtok, E = self.KT, self.QT, self.nQT, self.VT, self.Vtok, self.E
        xT_bf = self.xT_bf
        sm = self.small
        lam_init = 0.8 - 0.6 * float(np.exp(-0.3 * L))

        def PE(fn, reads=(), writes=(), sig=True):
            return S.op("pe", fn, reads, writes, sig)

        def ACT(fn, reads=(), writes=()):
            return S.op("act", fn, reads, writes)

        def DVE(fn, reads=(), writes=()):
            return S.op("dve", fn, reads, writes)

        if first:
            for tc in range(4):
                S.op("pool", I("dma_start", out=xT_bf[:, :, tc * 512:(tc + 1) * 512],
                               in_=self.xT.rearrange("(a p) c -> p a c", p=128)[:, :, tc * 512:(tc + 1) * 512]),
                     writes=[f"xT_bf{tc}"], dma=f"xt{tc}")
            S.op("pool", I("dma_start", out=self.maskt[:], in_=self.masks.rearrange("m p t -> p m t")),
                 writes=["maskt"], dma="ldm")
        for bb in range(2):
            for i in range(2):
                DVE(I("memset", KT[bb][i][64:65, :], 1.0), writes=[f"KT{bb}{i}"])
            DVE(I("memset", Vtok[bb][:, :, :], 1.0), writes=[f"Vtok{bb}"])

        DVE(I("tensor_tensor", out=sm[:, 64:128], in0=ppt[:, 65:129], in1=ppt[:, 129:193], op=ALU.mult),
            reads=["ppt"], writes=["sm_a"])
        DVE(I("tensor_reduce", out=sm[:, 1:2], in_=sm[:, 64:128], axis=AX.X, op=ALU.add), reads=["sm_a"], writes=["sm_b"])
        DVE(I("tensor_tensor", out=sm[:, 64:128], in0=ppt[:, 193:257], in1=ppt[:, 257:321], op=ALU.mult),
            reads=["ppt", "sm_b"], writes=["sm_a"])
        DVE(I("tensor_reduce", out=sm[:, 2:3], in_=sm[:, 64:128], axis=AX.X, op=ALU.add), reads=["sm_a"], writes=["sm_c"])
        ACT(I("activation", out=sm[:, 4:5], in_=sm[:, 1:2], func=AF.Exp), reads=["sm_b"], writes=["sm_d"])
        ACT(I("activation", out=sm[:, 5:6], in_=sm[:, 2:3], func=AF.Exp), reads=["sm_c"], writes=["sm_e"])
        DVE(I("scalar_tensor_tensor", out=sm[:, 6:7], in0=sm[:, 5:6], scalar=-lam_init, in1=sm[:, 4:5],
                                             op0=ALU.add, op1=ALU.subtract), reads=["sm_d", "sm_e"], writes=["neglam"])
        DVE(I("tensor_scalar", out=sm[:, 7:8], in0=ppt[:, 64:65], scalar1=(1.0 - lam_init), scalar2=None, op0=ALU.mult),
            reads=["ppt"], writes=["subs"])
        DVE(I("tensor_scalar", out=sm[:, 8:9], in0=ppt[:, 321:322], scalar1=-1.0, scalar2=None, op0=ALU.mult),
            reads=["ppt"], writes=["negbf"])
        neglam = sm[:, 6:7]
        subs = sm[:, 7:8]
        negbf = sm[0:8, 8:9]

        def evac(out, in_, reads, writes, scale=None, shifted=False, allow_act=True):
            use_act = allow_act and (not shifted) and (evac_toggle[0] % 2 == 1)
            evac_toggle[0] += 1
            if use_act:
                if scale is None:
                    ACT(I("copy", out=out, in_=in_), reads, writes)
                else:
                    ACT(I("activation", out=out, in_=in_, func=AF.Identity, scale=scale), reads, writes)
            else:
                if scale is None:
                    DVE(I("tensor_copy", out=out, in_=in_), reads, writes)
                else:
                    DVE(I("tensor_scalar", out=out, in0=in_, scalar1=scale, scalar2=None, op0=ALU.mult), reads, writes)

        evac_toggle = [0]

        wfz_t = self.view("R4", 0, [128, 16, 8], BF16)
        S.op("pool", I("dma_start", out=wfz_t, in_=self.wfz[li].rearrange("p (a b) -> p a b", a=16)),
             writes=["KT00"], dma="c0")
        fz, grow, rneg = self.fz, self.grow, self.rneg
        for tc in range(4):
            b = 6 + (tc % 2)
            for kc in range(KC):
                PE(I("matmul", ps[b][0:8, :], lhsT=wfz_t[:, kc, :],
                                                               rhs=xT_bf[:, kc, tc * 512:(tc + 1) * 512],
                                                               start=(kc == 0), stop=(kc == KC - 1)),
                   reads=["KT00", f"xT_bf{tc}"], writes=[f"ps{b}"], sig=(kc == KC - 1))
            ACT(I("activation", out=fz[0:8, tc * 512:(tc + 1) * 512], in_=ps[b][0:8, :], func=AF.Exp,
                                                   bias=negbf, scale=-1.0), reads=[f"ps{b}", "negbf"], writes=["fz"])
        ACT(I("activation", out=fz[0:8, :], in_=fz[0:8, :], func=AF.Ln, bias=1.0, scale=1.0), reads=["fz"], writes=["fz"])
        DVE(I("memset", KT[0][0][64:65, :], 1.0), reads=[], writes=["KT00"])
        DVE(I("memset", self.ones8[0:8, :], 1.0), writes=["ones8"])
        for g in range(4):
            DVE(I("tensor_tensor_scan", out=grow[0:8, g * 512:(g + 1) * 512], data0=self.ones8[0:8, :],
                                                    data1=fz[0:8, g * 512:(g + 1) * 512], initial=0.0,
                                                    op0=ALU.mult, op1=ALU.add), reads=["fz", "ones8"], writes=["grow"])
        off = sm[0:8, 16:20]
        DVE(I("memset", sm[0:8, 16:20], 0.0), writes=["off"])
        for g in range(4):
            for g2 in range(4):
                DVE(I("scalar_tensor_tensor",
                    out=sm[0:8, 16 + g:17 + g], in0=grow[0:8, g2 * 512 + 511:g2 * 512 + 512],
                    scalar=cct[0:8, 336 + g * 4 + g2:337 + g * 4 + g2], in1=sm[0:8, 16 + g:17 + g],
                    op0=ALU.mult, op1=ALU.add), reads=["grow", "cct", "off"], writes=["off"])
        for g in range(4):
            DVE(I("tensor_scalar", out=grow[0:8, g * 512:(g + 1) * 512], in0=grow[0:8, g * 512:(g + 1) * 512],
                                               scalar1=sm[0:8, 16 + g:17 + g], scalar2=None, op0=ALU.add),
                reads=["off", "grow"], writes=["grow"])
        DVE(I("tensor_scalar", out=rneg[0:8, :], in0=grow[0:8, 0:1024], scalar1=-1.0, scalar2=None, op0=ALU.mult),
            reads=["grow"], writes=["rneg"])
        for blk in range(16):
            PE(I("transpose", out=ps[6][:, blk * 8:(blk + 1) * 8], in_=grow[0:8, blk * 128:(blk + 1) * 128],
                                              identity=ident32[0:8, 0:8]), reads=["grow", "cm32"], writes=["ps6"])
        DVE(I("tensor_tensor", out=self.gT[:, :, :].rearrange("p a b -> p (a b)"), in0=ps[6][:, 0:128],
                                      in1=cct[:, 192:320], op=ALU.add), reads=["ps6", "cct"], writes=["gT"])

        deferred = []

        def flush_deferred(bank):
            while deferred:
                deferred.pop(0)(bank)

        def finish64(O, Oname, rzi, dst, dname):
            rz = self.rz[rzi]
            zc = self.SS32[rzi]
            ACT(I("copy", out=rz[0:64, :], in_=O[0:64, :]), reads=[Oname], writes=[f"rz{rzi}"])
            DVE(I("tensor_copy", out=zc[0:64, :], in_=O[64:128, :]), reads=[Oname], writes=[f"SS32{rzi}"])
            DVE(I("reciprocal", out=zc[0:64, :], in_=zc[0:64, :]), reads=[f"SS32{rzi}"], writes=[f"SS32{rzi}"])
            DVE(I("tensor_tensor", out=dst, in0=rz[0:64, :], in1=zc[0:64, :], op=ALU.mult),
                reads=[f"rz{rzi}", f"SS32{rzi}"], writes=[dname])

        def softmax_chains(chains, klist, lookahead=True):
            n = len(klist)

            def emitS(c, j):
                pos, mask = klist[j]
                sb = c["S"][j % 2]
                kk = c["K"]
                PE(I("matmul", ps[sb][:, :], lhsT=c["KT"][0:kk, pos * 128:(pos + 1) * 128], rhs=c["QT"][0:kk, c["qc"]],
                     start=True, stop=(mask is None)),
                   reads=[c["KTn"], c["QTn"]], writes=[f"ps{sb}"], sig=(mask is None))
                if mask is not None:
                    PE(I("matmul", ps[sb][:, :], lhsT=ident, rhs=self.maskt[:, mask, :], start=False, stop=True),
                       reads=["cmb", "maskt"], writes=[f"ps{sb}"])

            def emitExp(c, j):
                pos = klist[j][0]
                sb = c["S"][j % 2]
                et = c["E"][j % 2]
                ACT(I("activation", out=E[et][:, :], in_=ps[sb][:, :], func=AF.Exp, bias=c["bias"](pos), scale=1.0),
                    reads=[f"ps{sb}", "gT", "cct"], writes=[f"E{et}"])

            def emitPV(c, j):
                pos = klist[j][0]
                et = c["E"][j % 2]
                PE(I("matmul", ps[c["O"]][:, :], lhsT=c["vl"](pos), rhs=E[et][:, :], start=(j == 0), stop=(j == n - 1)),
                   reads=[f"E{et}", c["Vn"]], writes=[f"ps{c['O']}"], sig=("Z" not in c))
                if "Z" in c:
                    PE(I("matmul", ps[c["Z"]][:, :], lhsT=ones_b, rhs=E[et][:, :], start=(j == 0), stop=(j == n - 1)),
                       reads=[f"E{et}", "cmb"], writes=[f"ps{c['Z']}"])

            if lookahead:
                for c in chains:
                    emitS(c, 0)
                for j in range(n):
                    for c in chains:
                        if j + 1 < n:
                            emitS(c, j + 1)
                    for c in chains:
                        emitExp(c, j)
                    for c in chains:
                        emitPV(c, j)
                    if j == min(2, n - 1):
                        flush_deferred(chains[0]["S"][j % 2])
                    yield
            else:
                for j in range(n):
                    for c in chains:
                        emitS(c, j)
                    for c in chains:
                        emitExp(c, j)
                    yield
                    for c in chains:
                        emitPV(c, j)
                    if j == min(2, n - 1):
                        flush_deferred(chains[0]["S"][0])

        KL_A = [(0, 0), (1, 1), (2, 2), (3, 3), (12, None), (13, None), (14, None), (15, None)]
        KL_B = [(4, 0), (5, 1), (6, 2), (7, 3)] + [(p, None) for p in (8, 9, 10, 11, 0, 1, 2, 3, 12, 13, 14, 15)]
        KL_A_D = [(0, 8), (1, 9), (2, 10), (3, 11), (12, 12), (13, 13), (14, 14), (15, 15)]
        KL_B_D = [(4, 8), (5, 9), (6, 10), (7, 11), (8, 12), (9, 13), (10, 14), (11, 15)] + \
                 [(p, 16) for p in (0, 1, 2, 3, 12, 13, 14, 15)]
        KL_A_S = [(3, 7), (2, 6), (1, 5), (0, 4), (15, None), (14, None), (13, None), (12, None)]
        KL_B_S = [(7, 7), (6, 6), (5, 5), (4, 4)] + [(p, None) for p in (11, 10, 9, 8, 3, 2, 1, 0, 15, 14, 13, 12)]
        KIND = ("fox", "sb", "diff", "dil")
        vdirty = [False, False]
        kdirty = [False, False]

        def proj_gen(u, buf, banks, overlapped):
            kind = KIND[u // 4]
            i = u % 4
            KTb, QTb, nQTb, Vt = KT[buf], QT[buf], nQT[buf], Vtok[buf]
            kq = [f"KT{buf}0", f"KT{buf}1"]
            qq = [f"QT{buf}0", f"QT{buf}1"]
            nq = [f"nQT{buf}0", f"nQT{buf}1"]
            vn = f"Vtok{buf}"
            wq, wqn = self.get_w(S, li, 3 * u + 0)
            wk, wkn = self.get_w(S, li, 3 * u + 1)
            wv, wvn = self.get_w(S, li, 3 * u + 2)
            aa = not overlapped
            bsel = [0]

            def nextbank():
                b = banks[bsel[0] % 2]
                bsel[0] += 1
                return b

            if kind == "sb":
                kdirty[buf] = True
            elif kdirty[buf]:
                for hh in range(2):
                    DVE(I("memset", KTb[hh][64:65, :], 1.0), writes=[kq[hh]])
                kdirty[buf] = False
            if kind != "diff" and vdirty[buf]:
                DVE(I("memset", Vt[:, :, :], 1.0), writes=[vn])
                vdirty[buf] = False
            if kind == "diff":
                vdirty[buf] = True

            def group(w, wn, tc, b):
                for kc in range(KC):
                    PE(I("matmul", ps[b][:, :], lhsT=w[:, kc, :], rhs=xT_bf[:, kc, tc * 512:(tc + 1) * 512],
                         start=(kc == 0), stop=(kc == KC - 1)),
                       reads=[wn, f"xT_bf{tc}"], writes=[f"ps{b}"], sig=(kc == KC - 1))
                    if kc % 4 == 3:
                        yield

            for tc in range(2):
                b = nextbank()
                yield from group(wq, wqn, tc, b)
                p, pn = ps[b], f"ps{b}"
                cols = slice(tc * 512, (tc + 1) * 512)
                evac(QTb[0][0:64, cols], p[0:64, :], [pn], [qq[0]], scale=0.125, allow_act=aa)
                evac(QTb[1][0:64, cols], p[64:128, :], [pn], [qq[1]], scale=0.125, shifted=True)
                if kind == "sb":
                    evac(nQTb[0][0:64, cols], p[0:64, :], [pn], [nq[0]], scale=-0.125, allow_act=aa)
                    evac(nQTb[1][0:64, cols], p[64:128, :], [pn], [nq[1]], scale=-0.125, shifted=True)
            for hh in range(2):
                if kind == "fox":
                    h = 2 * i + hh
                    S.op("sp", I("dma_start", out=QTb[hh][64:65, :], in_=rneg[h:h + 1, :]), reads=["rneg"], writes=[qq[hh]],
                         dma=f"qrow{buf}{hh}")
                elif kind == "sb":
                    DVE(I("memset", QTb[hh][64:65, :], 1.0), writes=[qq[hh]])
                    DVE(I("memset", nQTb[hh][64:65, :], -1.0), writes=[nq[hh]])
                elif kind in ("diff", "dil"):
                    h = i if kind == "diff" else 4 + 2 * i + hh
                    S.op("pool", I("dma_start", out=QTb[hh][64:65, :], in_=self.rtab[h:h + 1, :]), writes=[qq[hh]],
                         dma=f"qrowp{buf}{hh}")
            if not overlapped:
                flush_deferred(banks[0])
            for tc in range(4):
                b = nextbank()
                yield from group(wk, wkn, tc, b)
                p, pn = ps[b], f"ps{b}"
                cols = slice(tc * 512, (tc + 1) * 512)
                evac(KTb[0][0:64, cols], p[0:64, :], [pn], [kq[0]], allow_act=aa)
                evac(KTb[1][0:64, cols], p[64:128, :], [pn], [kq[1]], shifted=True)
            if kind == "sb":
                for hh in range(2):
                    S.op("pool", I("dma_start", out=KTb[hh][64:65, :],
                                   in_=self.rtab[12:14, :].rearrange("(o a) b -> o (a b)", o=1)), writes=[kq[hh]],
                         dma=f"qrowp{buf}{hh}")
            for tc in range(4):
                b = nextbank()
                yield from group(wv, wvn, tc, b)
                p, pn = ps[b], f"ps{b}"
                cols = slice(tc * 512, (tc + 1) * 512)
                evac(VT[:, cols], p[:, :], [pn], ["VT"], allow_act=aa)
            for tg in range(4):
                b = nextbank()
                pbf = ps[b][:, :].bitcast(BF16)
                for t4 in range(4):
                    tb = tg * 4 + t4
                    PE(I("transpose", out=pbf[:, t4 * 128:(t4 + 1) * 128], in_=VT[:, tb * 128:(tb + 1) * 128], identity=ident),
                       reads=["VT", "cmb"], writes=[f"ps{b}"], sig=(t4 == 3))
                if kind == "diff":
                    DVE(I("tensor_copy", out=Vt[:, tg * 4:(tg + 1) * 4, 0:128],
                          in_=pbf[:, 0:512].rearrange("p (a b) -> p a b", a=4)), reads=[f"ps{b}"], writes=[vn])
                else:
                    DVE(I("tensor_copy", out=Vt[:, tg * 4:(tg + 1) * 4, :].rearrange("p a (h c) -> p a h c", h=2)[:, :, :, 0:64],
                          in_=pbf[:, 0:512].rearrange("p (a h c) -> p a h c", a=4, h=2)), reads=[f"ps{b}"], writes=[vn])
                yield

        def chain_gen(u, buf):
            kind = KIND[u // 4]
            i = u % 4
            KTb, QTb, nQTb, Vt = KT[buf], QT[buf], nQT[buf], Vtok[buf]
            kq = [f"KT{buf}0", f"KT{buf}1"]
            qq = [f"QT{buf}0", f"QT{buf}1"]
            nq = [f"nQT{buf}0", f"nQT{buf}1"]
            vn = f"Vtok{buf}"
            mixu = self.mixed[:, u, :]
            if kind in ("fox", "dil"):
                for slot in range(2):
                    chains = []
                    for hh in range(2):
                        if kind == "fox":
                            h = 2 * i + hh
                            bias = (lambda pos, h=h: self.gT[:, pos, h:h + 1])
                        else:
                            h = 4 + 2 * i + hh
                            bias = (lambda pos, h=h: cct[:, pos * 12 + h:pos * 12 + h + 1])
                        chains.append(dict(KT=KTb[hh], KTn=kq[hh], QT=QTb[hh], QTn=qq[hh], K=65, Vn=vn,
                                           qc=slice(slot * 512, (slot + 1) * 512), S=(2 * hh, 2 * hh + 1),
                                           E=(2 * hh, 2 * hh + 1), O=4 + hh, bias=bias,
                                           vl=(lambda pos, hh=hh: Vt[:, pos, hh * 128:(hh + 1) * 128])))
                    if kind == "fox":
                        kl = KL_A if slot == 0 else KL_B
                    else:
                        kl = KL_A_D if slot == 0 else KL_B_D
                    yield from softmax_chains(chains, kl)
                    for hh in range(2):
                        dst = mixu[hh * 64:(hh + 1) * 64, slot * 512:(slot + 1) * 512]
                        finish64(ps[4 + hh], f"ps{4 + hh}", hh, dst, "mixed")
            elif kind == "diff":
                h = i
                for slot in range(2):
                    chains = []
                    for hh in range(2):
                        chains.append(dict(KT=KTb[hh], KTn=kq[hh], QT=QTb[hh], QTn=qq[hh], K=65, Vn=vn,
                                           qc=slice(slot * 512, (slot + 1) * 512), S=(hh, hh),
                                           E=(2 * hh, 2 * hh + 1), O=4 + hh, Z=6 + hh,
                                           bias=(lambda pos, h=h: cct[:, pos * 12 + h:pos * 12 + h + 1]),
                                           vl=(lambda pos: Vt[:, pos, 0:128])))
                    yield from softmax_chains(chains, KL_A if slot == 0 else KL_B, lookahead=False)
                    rz0, rz1 = self.rz
                    zc0, zc1 = self.SS32
                    DVE(I("tensor_copy", out=rz0[:, :], in_=ps[4][:, :]), reads=["ps4"], writes=["rz0"])
                    ACT(I("copy", out=rz1[:, :], in_=ps[5][:, :]), reads=["ps5"], writes=["rz1"])
                    DVE(I("tensor_copy", out=zc0[:, :], in_=ps[6][:, :]), reads=["ps6"], writes=["SS320"])
                    ACT(I("copy", out=zc1[:, :], in_=ps[7][:, :]), reads=["ps7"], writes=["SS321"])
                    DVE(I("reciprocal", out=zc0[:, :], in_=zc0[:, :]), reads=["SS320"], writes=["SS320"])
                    DVE(I("reciprocal", out=zc1[:, :], in_=zc1[:, :]), reads=["SS321"], writes=["SS321"])
                    DVE(I("tensor_tensor", out=rz0[:, :], in0=rz0[:, :], in1=zc0[:, :], op=ALU.mult), reads=["rz0", "SS320"], writes=["rz0"])
                    DVE(I("tensor_tensor", out=rz1[:, :], in0=rz1[:, :], in1=zc1[:, :], op=ALU.mult), reads=["rz1", "SS321"], writes=["rz1"])
                    DVE(I("scalar_tensor_tensor", out=rz0[:, :], in0=rz1[:, :], scalar=neglam, in1=rz0[:, :],
                          op0=ALU.mult, op1=ALU.add), reads=["rz0", "rz1", "neglam"], writes=["rz0"])
                    DVE(I("tensor_tensor", out=zc0[:, :], in0=rz0[:, :], in1=rz0[:, :], op=ALU.mult), reads=["rz0"], writes=["SS320"])

                    def tail(bank, slot=slot, mixu=mixu, rz0=rz0, zc0=zc0, zc1=zc1):
                        PE(I("matmul", ps[bank][:, :], lhsT=ones32, rhs=zc0[:, :], start=True, stop=True),
                           reads=["SS320", "cm32"], writes=[f"ps{bank}"])
                        ACT(I("activation", out=zc1[:, :], in_=ps[bank][:, :], func=AF.Ln, bias=1e-5, scale=1.0 / 128.0),
                            reads=[f"ps{bank}"], writes=["SS321"])
                        ACT(I("activation", out=zc1[:, :], in_=zc1[:, :], func=AF.Exp, scale=-0.5), reads=["SS321"], writes=["SS321"])
                        DVE(I("tensor_tensor", out=rz0[:, :], in0=rz0[:, :], in1=zc1[:, :], op=ALU.mult),
                            reads=["rz0", "SS321"], writes=["rz0"])
                        DVE(I("tensor_scalar", out=mixu[:, slot * 512:(slot + 1) * 512], in0=rz0[:, :], scalar1=subs,
                              scalar2=None, op0=ALU.mult), reads=["rz0", "subs"], writes=["mixed"])

                    deferred.append(tail)
            else:
                spt, SSt = self.spt, self.SSt
                for slot in range(2):
                    kl = KL_A_S if slot == 0 else KL_B_S
                    n = len(kl)
                    qc = slice(slot * 512, (slot + 1) * 512)

                    def emit_z(hh, j):
                        pos, mask = kl[j]
                        zb = hh
                        PE(I("matmul", ps[zb][:, :], lhsT=KTb[hh][0:65, pos * 128:(pos + 1) * 128], rhs=QTb[hh][0:65, qc],
                             start=True, stop=(mask is None)),
                           reads=[kq[hh], qq[hh]], writes=[f"ps{zb}"], sig=(mask is None))
                        if mask is not None:
                            PE(I("matmul", ps[zb][:, :], lhsT=ident, rhs=self.maskt[:, mask, :], start=False, stop=True),
                               reads=["cmb", "maskt"], writes=[f"ps{zb}"])

                    def emit_softplus_e(j, hh):
                        zb = hh
                        et = 2 * hh
                        ACT(I("activation", out=E[et][:, :], in_=ps[zb][:, :], func=AF.Exp),
                            reads=[f"ps{zb}"], writes=[f"E{et}"])

                    def emit_softplus_l(j, hh):
                        p = j % 2
                        et = 2 * hh
                        ACT(I("activation", out=spt[:, p, hh, :], in_=E[et][:, :], func=AF.Ln, bias=1.0, scale=1.0),
                            reads=[f"E{et}"], writes=[f"sp{p}{hh}"])

                    def emit_softplus_both(j):
                        for hh in range(2):
                            emit_softplus_e(j, hh)
                        for hh in range(2):
                            emit_softplus_l(j, hh)

                    def emit_T(j, hh):
                        pos, mask = kl[j]
                        p = j % 2
                        tb = 6 + hh
                        PE(I("matmul", ps[tb][:, :], lhsT=utri, rhs=spt[:, p, hh, :], start=True, stop=False),
                           reads=[f"sp{p}{hh}", "cmb"], writes=[f"ps{tb}"], sig=False)
                        if j > 0:
                            PE(I("matmul", ps[tb][:, :], lhsT=ones_b, rhs=SSt[:, p, hh, :], start=False, stop=False),
                               reads=[f"SSt{p}{hh}", "cmb"], writes=[f"ps{tb}"], sig=False)
                        PE(I("matmul", ps[tb][:, :], lhsT=KTb[hh][0:65, pos * 128:(pos + 1) * 128],
                             rhs=nQTb[hh][0:65, qc], start=False, stop=(mask is None)),
                           reads=[kq[hh], nq[hh]], writes=[f"ps{tb}"], sig=(mask is None))
                        if mask is not None:
                            PE(I("matmul", ps[tb][:, :], lhsT=negident, rhs=self.maskt[:, mask, :], start=False, stop=True),
                               reads=["cmb", "maskt"], writes=[f"ps{tb}"])

                    def emit_A(j, hh):
                        pos, mask = kl[j]
                        padb = cct[:, 320 + pos:321 + pos]
                        tb = 6 + hh
                        at = 2 * hh + 1
                        ACT(I("activation", out=E[at][:, :], in_=ps[tb][:, :], func=AF.Exp, scale=-1.0),
                            reads=[f"ps{tb}"], writes=[f"E{at}"])

                    def emit_PV(j, hh):
                        pos, mask = kl[j]
                        at = 2 * hh + 1
                        PE(I("matmul", ps[4 + hh][:, :], lhsT=Vt[:, pos, hh * 128:(hh + 1) * 128],
                             rhs=E[at][:, :], start=(j == 0), stop=(j == n - 1)),
                           reads=[f"E{at}", vn], writes=[f"ps{4 + hh}"])

                    for hh in range(2):
                        emit_z(hh, 0)
                    emit_softplus_both(0)
                    for j in range(n):
                        p = j % 2
                        for hh in range(2):
                            if j + 1 < n:
                                emit_z(hh, j + 1)
                            emit_T(j, hh)
                            emit_A(j, hh)
                        yield
                        if j + 1 < n:
                            emit_softplus_both(j + 1)
                        for hh in range(2):
                            emit_PV(j, hh)
                        if j + 1 < n:
                            q = (j + 1) % 2
                            for hh in range(2):
                                if j == 0:
                                    DVE(I("tensor_copy", out=SSt[:, q, hh, :], in_=spt[:, p, hh, :]), reads=[f"sp{p}{hh}"], writes=[f"SSt{q}{hh}"])
                                else:
                                    DVE(I("tensor_tensor", out=SSt[:, q, hh, :], in0=SSt[:, p, hh, :], in1=spt[:, p, hh, :], op=ALU.add),
                                        reads=[f"sp{p}{hh}", f"SSt{p}{hh}"], writes=[f"SSt{q}{hh}"])
                        yield
                    for hh in range(2):
                        dst = mixu[hh * 64:(hh + 1) * 64, qc]
                        DVE(I("tensor_copy", out=dst, in_=ps[4 + hh][0:64, :]), reads=[f"ps{4 + hh}"], writes=["mixed"])

        order = list(units)
        for _ in proj_gen(order[0], 0, (6, 7), False):
            pass
        for idx, u in enumerate(order):
            kind = KIND[u // 4]
            nxt = order[idx + 1] if idx + 1 < len(order) else None
            cg = chain_gen(u, idx % 2)
            if nxt is not None and OVERLAP:
                pg = proj_gen(nxt, (idx + 1) % 2, {"sb": (2, 3), "diff": (2, 3)}.get(kind, (6, 7)), True)
                r = 1 if kind == "sb" else 2
                for _ in cg:
                    for _k in range(r):
                        next(pg, None)
                for _ in pg:
                    pass
            else:
                for _ in cg:
                    pass
                if nxt is not None:
                    for _ in proj_gen(nxt, (idx + 1) % 2, (6, 7), False):
                        pass
        flush_deferred(0)
        if next_xres is not None:
            for t in range(48, 48 + NSLOT):
                self.pre_w[(li, t)] = self.load_w(S, li, t)
            for kc in range(KC):
                S.op("sp", I("dma_start", out=self.xres[:, kc, :], in_=next_xres[kc * 128:(kc + 1) * 128, :]),
                     writes=["xT_bf0", "xT_bf1", "xT_bf2", "xT_bf3"], dma="xres")
            self.pre_xres = True
        if self.debug and li == len(self.layers) - 1 and self.debug == "mixed":
            for kc in range(KC):
                DVE(I("tensor_copy", out=self.tmp32[0][:, :], in_=self.mixed[:, kc, 0:512]), reads=["mixed"], writes=["dbgt0"])
                S.op("sp", I("dma_start", out=self.dbg[kc * 128:(kc + 1) * 128, 0:512], in_=self.tmp32[0][:, :]),
                     reads=["dbgt0"], dma="dbg0")
                DVE(I("tensor_copy", out=self.tmp32[1][:, :], in_=self.mixed[:, kc, 512:1024]), reads=["mixed"], writes=["dbgt1"])
                S.op("sp", I("dma_start", out=self.dbg[kc * 128:(kc + 1) * 128, 512:1024], in_=self.tmp32[1][:, :]),
                     reads=["dbgt1"], dma="dbg1")
        S.run()

    def phase_post(self, li, L, xres_src, last):
        S = self.new_sched()
        ps = self.ps
        ones_b = self.cmb[:, 3, :]
        ppt = self.ppt[li]
        cct = self.cct
        xres, mixed, x1bf, aT, xT_bf = self.xres, self.mixed, self.x1bf, self.aT, self.xT_bf
        tmp = self.tmp32
        tz = [self.view("R4", 16384 + i * 1024, [128, 512], BF16) for i in range(4)]
        tq = [self.view("R4", 16384 + 4096 + i * 1024, [128, 512], BF16) for i in range(4)]

        def PE(fn, reads=(), writes=(), sig=True):
            return S.op("pe", fn, reads, writes, sig)

        def ACT(fn, reads=(), writes=()):
            return S.op("act", fn, reads, writes)

        def DVE(fn, reads=(), writes=()):
            return S.op("dve", fn, reads, writes)

        def POOL(fn, reads=(), writes=()):
            return S.op("pool", fn, reads, writes)

        def xk(kc, half):
            return f"xres{kc}h{half}"

        if self.pre_xres:
            self.pre_xres = False
        else:
            for kc in range(KC):
                S.op("sp", I("dma_start", out=xres[:, kc, :], in_=xres_src[kc * 128:(kc + 1) * 128, :]),
                     writes=[xk(kc, 0), xk(kc, 1), "xres_ld"], dma="xres")
        bank = [0]

        def nb():
            b = bank[0] % 4
            bank[0] += 1
            return b

        wbase = 48
        for oc in range(16):
            w, wn = self.get_w(S, li, wbase + oc)
            for half in range(2):
                cols = slice(half * 512, (half + 1) * 512)
                b = nb()
                for kc in range(KC):
                    PE(I("matmul", ps[b][:, :], lhsT=w[:, kc, :], rhs=mixed[:, kc, cols], start=(kc == 0), stop=(kc == KC - 1)),
                       reads=[wn, "mixed"], writes=[f"ps{b}"], sig=(kc == KC - 1))
                DVE(I("scalar_tensor_tensor", out=xres[:, oc, cols], in0=xres[:, oc, cols], scalar=ALPHA, in1=ps[b][:, :],
                      op0=ALU.mult, op1=ALU.add), reads=[f"ps{b}", xk(oc, half), "xres_ld"], writes=[xk(oc, half)])

        wq_pref = []

        def next_w(t):
            if wq_pref and wq_pref[0][0] == t:
                return wq_pref.pop(0)[1]
            return self.load_w(S, li, t)

        def layer_norm(goff, boff, write_bf, prefetch=(), after_half=None):
            stat = {}
            for half in range(2):
                cols = slice(half * 512, (half + 1) * 512)
                b1, b2, bm, br = (4, 5, 6, 7) if half == 0 else (0, 1, 2, 3)
                for kc in range(KC):
                    z_ = tz[(half * 2 + kc) % 4]
                    q_ = tq[(half * 2 + kc) % 4]
                    zi, qi = (half * 2 + kc) % 4, (half * 2 + kc) % 4
                    DVE(I("tensor_copy", out=z_[:, :], in_=xres[:, kc, cols]), reads=[xk(kc, half)], writes=[f"tz{zi}"])
                    ACT(I("activation", out=q_[:, :], in_=xres[:, kc, cols], func=AF.Square), reads=[xk(kc, half)], writes=[f"tq{qi}"])
                    PE(I("matmul", ps[b1][:, :], lhsT=ones_b, rhs=z_[:, :], start=(kc == 0), stop=(kc == KC - 1)),
                       reads=[f"tz{zi}", "cmb"], writes=[f"ps{b1}"])
                    PE(I("matmul", ps[b2][:, :], lhsT=ones_b, rhs=q_[:, :], start=(kc == 0), stop=(kc == KC - 1)),
                       reads=[f"tq{qi}", "cmb"], writes=[f"ps{b2}"])
                if half == 0:
                    for t in prefetch:
                        wq_pref.append((t, self.load_w(S, li, t)))
                msq, lnv = tmp[half * 2], tmp[half * 2 + 1]
                mk, lk = f"tmp{half * 2}", f"tmp{half * 2 + 1}"
                ACT(I("activation", out=ps[bm][:, :], in_=ps[b1][:, :], func=AF.Identity, scale=1.0 / D), reads=[f"ps{b1}"], writes=[f"ps{bm}"])
                ACT(I("activation", out=msq[:, :], in_=ps[b1][:, :], func=AF.Square, scale=1.0 / D), reads=[f"ps{b1}"], writes=[mk])
                DVE(I("scalar_tensor_tensor", out=msq[:, :], in0=ps[b2][:, :], scalar=1.0 / D, in1=msq[:, :],
                      op0=ALU.mult, op1=ALU.subtract), reads=[f"ps{b2}", mk], writes=[mk])
                ACT(I("activation", out=lnv[:, :], in_=msq[:, :], func=AF.Ln, bias=1e-5, scale=1.0), reads=[mk], writes=[lk])
                ACT(I("activation", out=ps[br][:, :], in_=lnv[:, :], func=AF.Exp, scale=-0.5), reads=[lk], writes=[f"ps{br}"])
                stat[half] = (bm, br)
            for half in range(2):
                cols = slice(half * 512, (half + 1) * 512)
                bm, br = stat[half]
                for kc in range(KC):
                    ti = 4 + (kc % 4)
                    t = tmp[ti]
                    tn = f"tmp{ti}"
                    DVE(I("tensor_tensor", out=t[:, :], in0=xres[:, kc, cols], in1=ps[bm][:, :], op=ALU.subtract),
                        reads=[xk(kc, half), f"ps{bm}"], writes=[tn])
                    DVE(I("tensor_tensor", out=t[:, :], in0=t[:, :], in1=ps[br][:, :], op=ALU.mult), reads=[tn, f"ps{br}"], writes=[tn])
                    ACT(I("activation", out=xres[:, kc, cols], in_=t[:, :], func=AF.Identity,
                          bias=ppt[:, boff + kc:boff + kc + 1], scale=ppt[:, goff + kc:goff + kc + 1]),
                        reads=[tn, "ppt"], writes=[xk(kc, half)])
                    if write_bf:
                        ACT(I("activation", out=x1bf[:, kc, cols], in_=t[:, :], func=AF.Identity,
                              bias=ppt[:, boff + kc:boff + kc + 1], scale=ppt[:, goff + kc:goff + kc + 1]),
                            reads=[tn, "ppt"], writes=[f"x1bf{half}"])
                if after_half is not None:
                    after_half(half)

        layer_norm(0, 16, True, prefetch=[64 + i for i in range(NSLOT)])

        wbase = 64
        for qf in range(4):
            for fcl in range(16):
                w, wn = next_w(wbase + qf * 32 + fcl)
                for half in range(2):
                    cols = slice(half * 512, (half + 1) * 512)
                    b = nb()
                    for kc in range(KC):
                        PE(I("matmul", ps[b][:, :], lhsT=w[:, kc, :], rhs=x1bf[:, kc, cols], start=(kc == 0), stop=(kc == KC - 1)),
                           reads=[wn, f"x1bf{half}"], writes=[f"ps{b}"], sig=(kc == KC - 1))
                    t = tmp[4 + b]
                    ACT(I("activation", out=t[:, :], in_=ps[b][:, :], func=AF.Relu), reads=[f"ps{b}"], writes=[f"tmp{4 + b}"])
                    DVE(I("tensor_tensor", out=aT[:, fcl, cols], in0=ps[b][:, :], in1=t[:, :], op=ALU.mult),
                        reads=[f"ps{b}", f"tmp{4 + b}"], writes=[f"aT{fcl}h{half}"])
            for c in range(16):
                w, wn = next_w(wbase + qf * 32 + 16 + c)
                for half in range(2):
                    cols = slice(half * 512, (half + 1) * 512)
                    b = nb()
                    for fcl in range(16):
                        PE(I("matmul", ps[b][:, :], lhsT=w[:, fcl, :], rhs=aT[:, fcl, cols], start=(fcl == 0), stop=(fcl == 15)),
                           reads=[wn, f"aT{fcl}h{half}"], writes=[f"ps{b}"], sig=(fcl == 15))
                    DVE(I("scalar_tensor_tensor", out=xres[:, c, cols], in0=xres[:, c, cols], scalar=(ALPHA if qf == 0 else 1.0),
                          in1=ps[b][:, :], op0=ALU.mult, op1=ALU.add), reads=[f"ps{b}", xk(c, half)], writes=[xk(c, half)])
        if last:
            def out_half(half):
                cols = slice(half * 512, (half + 1) * 512)
                for kc in range(KC):
                    S.op("sp", I("dma_start", out=self.yT[kc * 128:(kc + 1) * 128, cols], in_=xres[:, kc, cols]),
                         reads=[xk(kc, half)], dma="yout")
            layer_norm(32, 48, False, after_half=out_half)
        else:
            def xchg_half(half):
                cols = slice(half * 512, (half + 1) * 512)
                for kc in range(KC):
                    S.op("sp", I("dma_start", out=self.xsave[kc * 128:(kc + 1) * 128, cols], in_=xres[:, kc, cols]),
                         reads=[xk(kc, half)], writes=["xsave"], dma="yout")
                S.op("sp", I("dma_start", out=self.xs[half].ap().rearrange("(a p) c -> p a c", p=128),
                             in_=x1bf[:, :, cols]), reads=[f"x1bf{half}"], writes=[f"xs{half}"], dma=f"xch{half}")
                S.op("pool", I("collective_compute", "AllGather", ALU.bypass,
                               replica_groups=[[0, 1], [2, 3], [4, 5], [6, 7]],
                               ins=[self.xs[half].ap().opt()], outs=[self.xg[half].ap().opt()]),
                     reads=[f"xs{half}"], writes=[f"xg{half}"])
            layer_norm(32, 48, True, after_half=xchg_half)
            for kc in range(KC):
                if kc % 2 == 0:
                    DVE(I("tensor_copy", out=xT_bf[:, kc, 0:1024], in_=x1bf[:, kc, :]), reads=["x1bf0", "x1bf1"],
                        writes=[xk(kc, 0), xk(kc, 1)])
                else:
                    ACT(I("copy", out=xT_bf[:, kc, 0:1024], in_=x1bf[:, kc, :]), reads=["x1bf0", "x1bf1"],
                        writes=[xk(kc, 0), xk(kc, 1)])
            fsel = cct[:, 352:353]
            gsel = cct[:, 353:354]
            stg = [[self.view(reg, i * 16384, [128, 16, 512], BF16).rearrange("p (a r) c -> p a r c", r=2) for i in range(2)]
                   for reg in ("R2", "R4")]
            tmpb = [self.view("R5", i * 1024, [128, 512], BF16) for i in range(4)]
            for r in range(2):
                for i in range(2):
                    for rk in range(2):
                        src = self.xg[i].ap().rearrange("(r a p) c -> r p a c", r=2, p=128)[rk][:, r * 8:(r + 1) * 8]
                        S.op("sp", I("dma_start", out=stg[r][i][:, :, rk, :], in_=src), reads=[f"xg{i}"],
                             writes=[f"stg{r}{i}"] + ([f"tz{j}" for j in range(4)] + [f"tq{j}" for j in range(4)] + [f"tmp{j}" for j in range(8)]
                                                      if r == 1 else [f"aT{j}h{h}" for j in range(16) for h in range(2)]),
                             dma=f"xch{2 + r}")
            cnt = 0
            for r in range(2):
                sA, sB = stg[r]
                for k8 in range(8):
                    kc = r * 8 + k8
                    for (s1, s0, c0) in ((sA, sB, 1024), (sB, sA, 1536)):
                        ti = cnt % 4
                        cnt += 1
                        ACT(I("activation", out=tmpb[ti][:, :], in_=s1[:, k8, 1, :], func=AF.Identity, scale=fsel),
                            reads=[f"stg{r}0", f"stg{r}1", "cct"], writes=[f"tmpb{ti}"] + (["x1bf0", "x1bf1"] if cnt <= 4 else []))
                        DVE(I("scalar_tensor_tensor", out=xT_bf[:, kc, c0:c0 + 512], in0=s0[:, k8, 0, :], scalar=gsel, in1=tmpb[ti][:, :],
                              op0=ALU.mult, op1=ALU.add), reads=[f"stg{r}0", f"stg{r}1", f"tmpb{ti}", "cct"], writes=[xk(kc, 0), xk(kc, 1)])
        if not last:
            for t in range(3):
                self.pre_w[(li + 1, t)] = self.load_w(S, li + 1, t)
        S.run()


def build_single_layer(L, debug=False, units=range(16), do_post=True):
    p = Prog([L], debug=debug)
    p.phase_clear()
    p.phase_setup()
    p.phase_attn(0, L, True, units=units, next_xres=(p.xT[:, 0:1024] if do_post else None))
    if do_post:
        p.phase_post(0, L, p.xT[:, 0:1024], True)
    return p.nc


def build_fused():
    p = Prog([0, 1])
    p.phase_clear()
    p.phase_setup()
    p.phase_attn(0, 0, True, next_xres=p.xT[:, 0:1024])
    p.phase_post(0, 0, p.xT[:, 0:1024], False)
    p.phase_attn(1, 1, False, next_xres=p.xsave)
    p.phase_post(1, 1, p.xsave, True)
    return p.nc


_CACHE = {}


def _run_layer(L, x, inp, debug=False):
    if ("nc", L, debug) not in _CACHE:
        _CACHE[("nc", L, debug)] = build_single_layer(L, debug)
    nc = _CACHE[("nc", L, debug)]
    W, wfz = _layer_weights(inp["w_in"][L], inp["w_out"][L], inp["w_mlp_in"][L], inp["w_mlp_out"][L])
    pp = _layer_params(L, inp)
    masks = _mask_tables()
    cmat = _const_mats()
    in_maps = []
    for c in range(8):
        b, half = divmod(c, 2)
        tp = _tokperm(half)
        cc, rtab = _core_consts(half)
        xT = np.ascontiguousarray(x[b][tp, :].T)
        in_maps.append({"xT": xT, "W0": W, "wfz0": wfz, "pp0": pp, "cc": cc, "rtab": rtab, "masks": masks, "cmat": cmat})
    res = run_bass_kernel_spmd(nc, in_maps, core_ids=list(range(8)))
    out = np.empty_like(x)
    dbg = None
    if debug:
        dbg = np.empty((4, 2048, 2048), np.float32)
    for c in range(8):
        b, half = divmod(c, 2)
        tp = _tokperm(half)
        out[b][tp[0:1024], :] = res.results[c]["yT"].T
        if debug:
            dbg[b][tp[0:1024], :] = res.results[c]["dbg"].T
    return (out, dbg) if debug else out


def kernel_unfused(**inputs):
    inp = {k: np.asarray(v) for k, v in inputs.items()}
    x = np.ascontiguousarray(inp["x"], dtype=np.float32)
    for L in range(DEPTH):
        x = _run_layer(L, x, inp)
    return x


def kernel(**inputs):
    inp = {k: np.asarray(v) for k, v in inputs.items()}
    x = np.ascontiguousarray(inp["x"], dtype=np.float32)
    if "fused" not in _CACHE:
        _CACHE["fused"] = build_fused()
    nc = _CACHE["fused"]
    shared = {"masks": _mask_tables(), "cmat": _const_mats()}
    for L in range(DEPTH):
        W, wfz = _layer_weights(inp["w_in"][L], inp["w_out"][L], inp["w_mlp_in"][L], inp["w_mlp_out"][L])
        shared[f"W{L}"] = W
        shared[f"wfz{L}"] = wfz
        shared[f"pp{L}"] = _layer_params(L, inp)
    in_maps = []
    for c in range(8):
        b, half = divmod(c, 2)
        tp = _tokperm(half)
        cc, rtab = _core_consts(half)
        m = dict(shared)
        m.update({"xT": np.ascontiguousarray(x[b][tp, :].T), "cc": cc, "rtab": rtab})
        in_maps.append(m)
    res = run_bass_kernel_spmd(nc, in_maps, core_ids=list(range(8)))
    out = np.empty_like(x)
    for c in range(8):
        b, half = divmod(c, 2)
        tp = _tokperm(half)
        out[b][tp[0:1024], :] = res.results[c]["yT"].T
    return out
```
